# Optimizing a Trainium2 kernel written in Bass

```python
import jax, jax.numpy as jnp
from jax import lax
import numpy as np

D_MODEL = 2048
BATCH = 4
SEQ = 4096
DEPTH = 4
DEC_BATCH = 8
DEC_SEQ = 16
PAST_LEN = 4096

CHUNK = 64
MA_HEADS = 8
MA_HEAD_DIM = 256
MA_WIDTH = MA_HEADS * MA_HEAD_DIM
CONV_W = 4
HB_HEADS = 16
HB_KEY_DIM = 128
HB_VAL_DIM = 128
HB_WIDTH = HB_HEADS * HB_KEY_DIM
HB_VWIDTH = HB_HEADS * HB_VAL_DIM
D_FF = 4 * D_MODEL
N_IN = 4 * MA_WIDTH + 2 * HB_WIDTH + 2 * HB_VWIDTH + 2 * D_MODEL + 2 * MA_HEADS
EPS = 1e-6
NEG = -1e30

kernel_name = 'hybrid_mlstm_hgrn2_streaming_step'


def rmsnorm(x, g):
    xf = x.astype(jnp.float32)
    y = xf * lax.rsqrt(jnp.mean(xf * xf, axis=-1, keepdims=True) + EPS)
    return (y * g.astype(jnp.float32)).astype(x.dtype)


def head_rmsnorm(h, g):
    y = h * lax.rsqrt(jnp.mean(h * h, axis=-1, keepdims=True) + EPS)
    return y.reshape(h.shape[:2] + (-1,)) * g.astype(jnp.float32)


def to_chunks(a, L):
    B, T = a.shape[:2]
    return jnp.swapaxes(a.reshape((B, T // L, L) + a.shape[2:]), 0, 1)


def from_chunks(a):
    a = jnp.swapaxes(a, 0, 1)
    return a.reshape((a.shape[0], a.shape[1] * a.shape[2]) + a.shape[3:])


def causal_dwconv(buf, u, w, b):
    full = jnp.concatenate([buf.astype(jnp.float32), u], axis=1)
    T = u.shape[1]
    wf = w.astype(jnp.float32)
    y = b.astype(jnp.float32) + full[:, 0:T] * wf[0]
    for j in range(1, CONV_W):
        y = y + full[:, j:j + T] * wf[j]
    return y, full[:, -(CONV_W - 1):]


def mlstm_step(carry, inp):
    c_mat, n_vec, m_run = carry
    q, k, v, ig, lf = inp
    L = q.shape[1]
    mask = jnp.tril(jnp.ones((L, L), dtype=bool))
    b = jnp.swapaxes(jnp.cumsum(lf, axis=1), 1, 2)
    igt = jnp.swapaxes(ig, 1, 2)
    log_d = jnp.where(mask, b[..., :, None] - b[..., None, :] + igt[..., None, :], NEG)
    log_inter = b + m_run[..., None]
    m_t = jnp.maximum(log_inter, jnp.max(log_d, axis=-1))
    d = jnp.exp(log_d - m_t[..., None])
    w_inter = jnp.exp(log_inter - m_t)
    s = jnp.einsum('blhd,bshd->bhls', q, k) * d
    num = (jnp.einsum('bhls,bshv->blhv', s, v)
           + jnp.swapaxes(w_inter, 1, 2)[..., None] * jnp.einsum('blhd,bhdv->blhv', q, c_mat))
    qn = jnp.sum(s, axis=-1) + w_inter * jnp.einsum('blhd,bhd->bhl', q, n_vec)
    denom = jnp.maximum(jnp.abs(qn), jnp.exp(-m_t))
    h = num / jnp.swapaxes(denom, 1, 2)[..., None]
    w_last = jnp.swapaxes(d[..., -1, :], 1, 2)[..., None]
    decay = w_inter[..., -1]
    c_new = decay[..., None, None] * c_mat + jnp.einsum('bshd,bshv->bhdv', k * w_last, v)
    n_new = decay[..., None] * n_vec + jnp.sum(k * w_last, axis=1)
    return (c_new, n_new, m_t[..., -1]), h


def mlstm_scan(q, k, v, ig, lf, c0, n0, m0):
    L = min(CHUNK, q.shape[1])
    xs = (to_chunks(q, L), to_chunks(k, L), to_chunks(v, L), to_chunks(ig, L), to_chunks(lf, L))
    init = (c0.astype(jnp.float32), n0.astype(jnp.float32), m0.astype(jnp.float32))
    (c1, n1, m1), h = lax.scan(mlstm_step, init, xs)
    return from_chunks(h), c1, n1, m1


def hgrn_step(s_mat, inp):
    q, k, v, lf = inp
    L = q.shape[1]
    mask = jnp.tril(jnp.ones((L, L), dtype=bool))[:, :, None, None]
    b = jnp.cumsum(lf, axis=1)
    decay = jnp.exp(jnp.where(mask, b[:, :, None] - b[:, None, :], NEG))
    a = jnp.einsum('btshd,bshd->bhts', decay * q[:, :, None], k)
    o = jnp.einsum('bhts,bshv->bthv', a, v) + jnp.einsum('bthd,bhdv->bthv', q * jnp.exp(b), s_mat)
    b_last = b[:, -1]
    s_new = (jnp.exp(b_last)[..., None] * s_mat
             + jnp.einsum('bshd,bshv->bhdv', k * jnp.exp(b_last[:, None] - b), v))
    return s_new, o


def hgrn_scan(q, k, v, lf, s0):
    L = min(CHUNK, q.shape[1])
    xs = (to_chunks(q, L), to_chunks(k, L), to_chunks(v, L), to_chunks(lf, L))
    s1, o = lax.scan(hgrn_step, s0.astype(jnp.float32), xs)
    return from_chunks(o), s1


def mixer_block(h, conv_buf, c0, n0, m0, s0, lb, w_in, b_in, conv_w, conv_b,
                ma_norm, hb_norm, w_br_a, w_br_b, w_o):
    B, T, _ = h.shape
    dt = h.dtype
    z = (h @ w_in + b_in).astype(jnp.float32)
    cuts = [int(i) for i in np.cumsum([2 * MA_WIDTH, MA_WIDTH, MA_WIDTH, HB_WIDTH, HB_WIDTH,
                                      HB_VWIDTH, HB_VWIDTH, D_MODEL, D_MODEL, MA_HEADS])]
    qk_a, v_a, o_a, q_b, f_b, i_b, og_b, g_a, g_b, ig_a, fg_a = jnp.split(z, cuts, axis=-1)
    qk_a, new_buf = causal_dwconv(conv_buf, qk_a, conv_w, conv_b)
    q_a, k_a = jnp.split(jax.nn.silu(qk_a), 2, axis=-1)
    q_a = q_a.reshape(B, T, MA_HEADS, MA_HEAD_DIM)
    k_a = k_a.reshape(B, T, MA_HEADS, MA_HEAD_DIM) * (MA_HEAD_DIM ** -0.5)
    v_a = v_a.reshape(B, T, MA_HEADS, MA_HEAD_DIM)
    h_a, c1, n1, m1 = mlstm_scan(q_a, k_a, v_a, ig_a, jax.nn.log_sigmoid(fg_a), c0, n0, m0)
    y_a = head_rmsnorm(h_a, ma_norm) * jax.nn.sigmoid(o_a)
    lb = lb.astype(jnp.float32)
    f_gate = lb + (1.0 - lb) * jax.nn.sigmoid(f_b)
    log_f = jnp.log(f_gate)
    k_b = (1.0 - lb) * jax.nn.sigmoid(-f_b)
    q_b = jax.nn.silu(q_b)
    hshape = (B, T, HB_HEADS, HB_KEY_DIM)
    h_b, s1 = hgrn_scan(q_b.reshape(hshape), k_b.reshape(hshape),
                        i_b.reshape(B, T, HB_HEADS, HB_VAL_DIM), log_f.reshape(hshape), s0)
    y_b = head_rmsnorm(h_b, hb_norm) * jax.nn.sigmoid(og_b)
    merged = (jax.nn.sigmoid(g_a) * (y_a.astype(dt) @ w_br_a)
              + jax.nn.sigmoid(g_b) * (y_b.astype(dt) @ w_br_b))
    out = merged.astype(dt) @ w_o
    return out, new_buf, c1, n1, m1, s1


def trunk(x, c, conv_c, st_c, st_n, st_m, st_s, lb_all, ada_w, ada_b, norm1_g, norm2_g,
          w_in, b_in, conv_w, conv_b, ma_norm, hb_norm, w_br_a, w_br_b, w_o, w_up, w_down, final_g):
    cs = jax.nn.silu(c)
    bufs, cms, nvs, mrs, sms = [], [], [], [], []
    for l in range(DEPTH):
        mod = (cs @ ada_w[l] + ada_b[l])[:, None, :]
        sh1, sc1, g1, sh2, sc2, g2 = jnp.split(mod, 6, axis=-1)
        h = rmsnorm(x, norm1_g[l]) * (1 + sc1) + sh1
        out, buf, c1, n1, m1, s1 = mixer_block(h, conv_c[l], st_c[l], st_n[l], st_m[l], st_s[l], lb_all[l],
                                               w_in[l], b_in[l], conv_w[l], conv_b[l], ma_norm[l], hb_norm[l],
                                               w_br_a[l], w_br_b[l], w_o[l])
        x = x + (g1 * out).astype(x.dtype)
        h = rmsnorm(x, norm2_g[l]) * (1 + sc2) + sh2
        u = jnp.square(jax.nn.relu(h @ w_up[l]))
        x = x + (g2 * (u @ w_down[l])).astype(x.dtype)
        bufs.append(buf); cms.append(c1); nvs.append(n1); mrs.append(m1); sms.append(s1)
    y = rmsnorm(x, final_g)
    return y, jnp.stack(bufs), jnp.stack(cms), jnp.stack(nvs), jnp.stack(mrs), jnp.stack(sms)


def setup_inputs(seed: int = 0) -> dict:
    key = jax.random.key(seed)
    ks = jax.random.split(key, 32)
    f32 = jnp.float32

    def nrm(k, shape, s):
        return jax.random.normal(k, shape, f32) * s

    D = D_MODEL
    b_in = nrm(ks[13], (DEPTH, N_IN), 0.02)
    fg_off = jax.random.uniform(ks[14], (DEPTH, MA_HEADS), f32, minval=3.0, maxval=6.0)
    b_in = b_in.at[:, N_IN - MA_HEADS:].add(fg_off)
    return {
        'x_prompt': nrm(ks[0], (BATCH, SEQ, D), 1.0),
        'x_sample': nrm(ks[1], (DEC_BATCH, DEC_SEQ, D), 1.0),
        'cache_conv': nrm(ks[2], (DEPTH, DEC_BATCH, CONV_W - 1, 2 * MA_WIDTH), 1.0),
        'state_mlstm_C': nrm(ks[3], (DEPTH, DEC_BATCH, MA_HEADS, MA_HEAD_DIM, MA_HEAD_DIM), 0.1),
        'state_mlstm_n': nrm(ks[4], (DEPTH, DEC_BATCH, MA_HEADS, MA_HEAD_DIM), 0.1),
        'state_mlstm_m': nrm(ks[5], (DEPTH, DEC_BATCH, MA_HEADS), 1.0),
        'state_hgrn': nrm(ks[6], (DEPTH, DEC_BATCH, HB_HEADS, HB_KEY_DIM, HB_VAL_DIM), 0.3),
        'c_prompt': nrm(ks[7], (BATCH, D), 1.0),
        'c_sample': nrm(ks[8], (DEC_BATCH, D), 1.0),
        'ada_w': nrm(ks[9], (DEPTH, D, 6 * D), 0.5 * D ** -0.5),
        'ada_b': nrm(ks[10], (DEPTH, 6 * D), 0.02),
        'norm1_g': 1.0 + nrm(ks[11], (DEPTH, D), 0.02),
        'norm2_g': 1.0 + nrm(ks[12], (DEPTH, D), 0.02),
        'w_in': nrm(ks[15], (DEPTH, D, N_IN), D ** -0.5),
        'b_in': b_in,
        'conv_w': nrm(ks[16], (DEPTH, CONV_W, 2 * MA_WIDTH), 0.5),
        'conv_b': nrm(ks[17], (DEPTH, 2 * MA_WIDTH), 0.02),
        'ma_norm': 1.0 + nrm(ks[18], (DEPTH, MA_WIDTH), 0.02),
        'hgrn_lb_raw': nrm(ks[19], (DEPTH, HB_WIDTH), 1.0),
        'hb_norm': 1.0 + nrm(ks[20], (DEPTH, HB_VWIDTH), 0.02),
        'w_br_a': nrm(ks[21], (DEPTH, MA_WIDTH, D), MA_WIDTH ** -0.5),
        'w_br_b': nrm(ks[22], (DEPTH, HB_VWIDTH, D), HB_VWIDTH ** -0.5),
        'w_o': nrm(ks[23], (DEPTH, D, D), D ** -0.5),
        'w_up': nrm(ks[24], (DEPTH, D, D_FF), D ** -0.5),
        'w_down': nrm(ks[25], (DEPTH, D_FF, D), D_FF ** -0.5),
        'final_g': 1.0 + nrm(ks[26], (D,), 0.02),
    }


def reference(x_prompt, x_sample, cache_conv, state_mlstm_C, state_mlstm_n, state_mlstm_m, state_hgrn,
              c_prompt, c_sample, ada_w, ada_b, norm1_g, norm2_g, w_in, b_in, conv_w, conv_b, ma_norm,
              hgrn_lb_raw, hb_norm, w_br_a, w_br_b, w_o, w_up, w_down, final_g):
    lb_sm = jax.nn.softmax(hgrn_lb_raw.astype(jnp.float32), axis=0)
    lb_all = jnp.cumsum(lb_sm, axis=0) - lb_sm[0]
    weights = (ada_w, ada_b, norm1_g, norm2_g, w_in, b_in, conv_w, conv_b, ma_norm, hb_norm,
               w_br_a, w_br_b, w_o, w_up, w_down, final_g)
    bp = x_prompt.shape[0]
    f32 = jnp.float32
    z_conv = jnp.zeros((DEPTH, bp, CONV_W - 1, 2 * MA_WIDTH), f32)
    z_c = jnp.zeros((DEPTH, bp, MA_HEADS, MA_HEAD_DIM, MA_HEAD_DIM), f32)
    z_n = jnp.zeros((DEPTH, bp, MA_HEADS, MA_HEAD_DIM), f32)
    z_m = jnp.zeros((DEPTH, bp, MA_HEADS), f32)
    z_s = jnp.zeros((DEPTH, bp, HB_HEADS, HB_KEY_DIM, HB_VAL_DIM), f32)
    y_prompt, conv_p, c_p, n_p, m_p, s_p = trunk(x_prompt, c_prompt, z_conv, z_c, z_n, z_m, z_s,
                                                 lb_all, *weights)
    y_sample, conv_s, c_s, n_s, m_s, s_s = trunk(x_sample, c_sample, cache_conv, state_mlstm_C,
                                                 state_mlstm_n, state_mlstm_m, state_hgrn,
                                                 lb_all, *weights)
    return (y_prompt, y_sample, conv_p, c_p, n_p, m_p, s_p, conv_s, c_s, n_s, m_s, s_s)
```

```python
import math
from contextlib import ExitStack
import numpy as np
import concourse.bass as bass
import concourse.mybir as mybir
from concourse.bass_utils import run_bass_kernel_spmd

F32 = mybir.dt.float32
BF16 = mybir.dt.bfloat16
AF = mybir.ActivationFunctionType
ALU = mybir.AluOpType

D = 2048
KC = 16
N_IN = 20496
EPS = 1e-6
NSLOT = 2
LN16 = math.log(1.0 / 16.0)
V_N1G, V_N2G, V_ADAB, V_BIN, V_CW, V_CB, V_MAN, V_HBN, V_LBR = 0, 16, 32, 128, 288, 416, 448, 464, 480
V_PER = 496


class Buf:
    def __init__(self, t):
        self.t = t
        self.w = None
        self.r = {}

    def __getitem__(self, k):
        return self.t[k]


class K:
    def __init__(self, nc, es):
        self.nc = nc
        self.es = es
        self.engs = {'pe': nc.tensor, 'act': nc.scalar, 'dve': nc.vector, 'pool': nc.gpsimd, 'sp': nc.sync}
        self.sem = {}
        self.cnt = {}
        self.seen = {e: {} for e in self.engs}
        for e in ('pe', 'act', 'dve', 'pool'):
            self.newsem(e)

    def newsem(self, name):
        self.sem[name] = self.es.enter_context(self.nc.semaphore(name))
        self.cnt[name] = 0

    def sb(self, name, shape, dt):
        return Buf(self.es.enter_context(self.nc.sbuf_tensor("s_" + name, shape, dt)))

    def ps(self, name, shape, dt):
        return Buf(self.es.enter_context(self.nc.psum_tensor("p_" + name, shape, dt)))

    def wait(self, eng, tok):
        if tok is None:
            return
        k, v = tok
        if self.seen[eng].get(k, 0) >= v:
            return
        self.engs[eng].wait_ge(self.sem[k], v)
        self.seen[eng][k] = v

    def deps(self, eng, reads, writes):
        for b in reads:
            if b.w is not None and not (eng == 'pe' and b.w[0] == 'pe'):
                self.wait(eng, b.w)
        for b in writes:
            if b.w is not None and not (eng == 'pe' and b.w[0] == 'pe'):
                self.wait(eng, b.w)
            for k, t in b.r.items():
                if not (eng == 'pe' and t[0] == 'pe'):
                    self.wait(eng, t)

    def mark(self, tok, reads, writes):
        for b in reads:
            b.r[tok[0]] = tok
        for b in writes:
            b.w = tok
            b.r = {}

    def E(self, eng, fn, reads=(), writes=()):
        self.deps(eng, reads, writes)
        inst = fn()
        self.cnt[eng] += 1
        inst.then_inc(self.sem[eng], 1)
        tok = (eng, self.cnt[eng])
        self.mark(tok, reads, writes)
        return tok

    def DMA(self, eng, semname, fns, reads=(), writes=()):
        self.deps(eng, reads, writes)
        for fn in fns:
            inst = fn()
            self.cnt[semname] += 16
            inst.then_inc(self.sem[semname], 16)
        tok = (semname, self.cnt[semname])
        self.mark(tok, reads, writes)
        return tok


def build(NL, NTP, WITH_S, TP=512):
    SEQP = max(NTP, 1) * TP
    nc = bass.Bass("TRN2", target_bir_lowering=False)

    def din(name, shape):
        return nc.dram_tensor(name, shape, F32, kind="ExternalInput").ap()

    def dout(name, shape):
        return nc.dram_tensor(name, shape, F32, kind="ExternalOutput").ap()

    xp = din("xp", [D, SEQP]); xs = din("xs", [D, 16]); cvec = din("cvec", [128, KC, 2])
    vecs_d = din("vecs", [128, NL * V_PER + 16]); consts_d = din("consts", [128, 896])
    msamp = din("msamp", [128, NL * 8]); convs_in = din("convs_in", [128, NL, 32, 3])
    ns_in = din("ns_in", [NL, 8, 128, 2]); Cs_in = din("Cs_in", [NL, 8, 256, 256]); Ss_in = din("Ss_in", [NL, 16, 128, 128])
    Wd = {"w_in": din("w_in", [NL, D, N_IN]), "ada_w": din("ada_w", [NL, D, 6 * D]),
          "w_br_a": din("w_br_a", [NL, D, D]), "w_br_b": din("w_br_b", [NL, D, D]), "w_o": din("w_o", [NL, D, D]),
          "w_up": din("w_up", [NL, D, 4 * D]), "w_down": din("w_down", [NL, 4 * D, D])}
    b_in_d = din("b_in", [NL, N_IN])
    yp = dout("yp", [D, SEQP]); ys = dout("ys", [D, 16])
    O = {}
    for g in ("p", "s"):
        O[g] = dict(conv=dout("conv" + g, [128, NL, 32, 3]), C=dout("C" + g, [NL, 8, 256, 256]),
                    n=dout("n" + g, [NL, 8, 128, 2]), m=dout("m" + g, [1, NL * 8]), S=dout("S" + g, [NL, 16, 128, 128]))

    with ExitStack() as es:
        es.enter_context(nc.allow_non_contiguous_dma(reason="small strided state vectors"))
        k = K(nc, es)
        E, DMA = k.E, k.DMA
        xT = k.sb("xT", [128, KC, TP], F32)
        hT = k.sb("hT", [128, KC, TP], BF16)
        yaT = k.sb("yaT", [128, KC, TP], BF16)
        ybT = k.sb("ybT", [128, KC, TP], BF16)
        uT = k.sb("uT", [128, KC, TP], BF16)
        slots = [k.sb("ws%d" % i, [128, KC + 1, 512], BF16) for i in range(NSLOT)]
        for i in range(NSLOT):
            k.newsem("wsem%d" % i)
        vecs = k.sb("vecs", [128, NL * V_PER + 16], F32)
        cst = k.sb("cst", [128, 896], F32)
        tri = cst.t[:, 0:128]; ident = cst.t[:, 128:256]; maskb = cst.t[:, 256:384]; tri8 = cst.t[:, 384:896]
        identb = k.sb("identb", [128, 128], BF16); onesb = k.sb("onesb", [128, 128], BF16)
        onesf = k.sb("onesf", [128, 128], F32); negbig = k.sb("negbig", [128, 128], F32)
        rmask = k.sb("rmask", [128, 512], F32)
        mods = k.sb("mods", [128, NL, 96, 2], F32)
        A12 = k.sb("A12", [128, NL, 2, 16, 2], F32)
        lbt = k.sb("lbt", [128, NL, 16], F32); omlt = k.sb("omlt", [128, NL, 16], F32); nomlt = k.sb("nomlt", [128, NL, 16], F32)
        hist = k.sb("hist", [128, NL, 32, 3], F32)
        mst = k.sb("mst", [128, NL * 8], F32)
        rstd = k.sb("rstd", [128, TP], F32); tmpf = k.sb("tmpf", [128, TP], F32); sqb = k.sb("sqb", [128, TP], BF16)
        sg4 = k.sb("sg4", [128, 4, TP], BF16)
        uqk = k.sb("uqk", [128, 4, TP + 3], F32); cacc = k.sb("cacc", [128, TP], F32)
        qkT = k.sb("qkT", [128, 4, TP], BF16)
        vext = k.sb("vext", [128, 4, 257], BF16); sigo = k.sb("sigo", [128, 4, 256], BF16)
        igc = k.sb("igc", [128, 4, 8], F32); nlf = k.sb("nlf", [128, 4, 8], F32); nbc = k.sb("nbc", [128, 4, 8], F32)
        acadj = k.sb("acadj", [128, 4, 8], F32); gtmp = k.sb("gtmp", [128, 8], F32)
        igrep = k.sb("igrep", [128, 128], F32); nlfrep = k.sb("nlfrep", [128, 128], F32)
        grow = k.sb("grow", [128, 128], F32); wrow = k.sb("wrow", [128, 128], F32)
        dtmp = k.sb("dtmp", [128, 128], F32); DT = k.sb("DT", [128, 128], F32); STb = k.sb("STb", [128, 128], BF16)
        junk = k.sb("junk", [128, 256], F32)
        qsc = k.sb("qsc", [128, 2, 128], BF16); ytm = k.sb("ytm", [128, 256], BF16); ktm = k.sb("ktm", [128, 256], BF16)
        sm = k.sb("sm", [128, 16], F32)
        Cf = k.sb("Cf", [128, 2, 257], F32); Cb = k.sb("Cb", [128, 2, 257], BF16)
        vtm = k.sb("vtm", [32, 16, 256], BF16)
        bh = rstd
        ebh = k.sb("ebh", [128, TP], F32); enbh = k.sb("enbh", [128, TP], F32)
        khT = k.sb("khT", [128, TP], BF16)
        Abf = k.sb("Abf", [32, TP], BF16); khtm = k.sb("khtm", [32, 16 * 128], BF16)
        Sf = k.sb("Sf", [128, 128], F32); Sb = k.sb("Sb", [128, 128], BF16)
        P0 = k.ps("P0", [128, 512], F32); P1 = k.ps("P1", [128, 512], F32); PT = k.ps("PT", [128, 512], F32)
        PS = k.ps("PS", [128, 512], F32); PN = k.ps("PN", [128, 512], F32); PA = k.ps("PA", [128, 512], F32)
        PX = k.ps("PX", [128, 1024], BF16); PC = k.ps("PC", [128, 512], F32)
        PD = [P0, P1]
        pdi = [0]
        SPN = 8
        for i in range(SPN):
            k.newsem("sp%d" % i)
        spi = [0]
        sp_last = [None] * SPN

        def spdma(out_ap, in_ap, reads=(), writes=()):
            i = spi[0] % SPN
            spi[0] += 1
            k.wait('sp', sp_last[i])
            tok = DMA('sp', "sp%d" % i, [lambda: nc.sync.dma_start(out=out_ap, in_=in_ap)], reads, writes)
            sp_last[i] = tok
            return tok

        def plan():
            pl = []
            for l in range(NL):
                for jb in range(24):
                    pl.append(("ada_w", l, 0, ((jb * 512, 512),), False))
            tiles = [("p", i) for i in range(NTP)] + ([("s", 0)] if WITH_S else [])
            for _ in tiles:
                for l in range(NL):
                    pl.append(("w_in", l, 0, ((20480, 16),), True))
                    for hd in range(8):
                        pl.append(("w_in", l, 0, ((hd * 256, 256), (2048 + hd * 256, 256)), False))
                        pl.append(("w_in", l, 0, ((4096 + hd * 256, 256), (6144 + hd * 256, 256)), True))
                    for hp in range(8):
                        pl.append(("w_in", l, 0, ((8192 + hp * 256, 256), (10240 + hp * 256, 256)), False))
                        pl.append(("w_in", l, 0, ((12288 + hp * 256, 256), (14336 + hp * 256, 256)), True))
                    for jb in range(4):
                        pl.append(("w_in", l, 0, ((16384 + jb * 512, 512),), False))
                        pl.append(("w_br_a", l, 0, ((jb * 512, 512),), False))
                        pl.append(("w_in", l, 0, ((18432 + jb * 512, 512),), False))
                        pl.append(("w_br_b", l, 0, ((jb * 512, 512),), False))
                    for jb in range(4):
                        pl.append(("w_o", l, 0, ((jb * 512, 512),), False))
                    for qd in range(4):
                        for ub in range(4):
                            pl.append(("w_up", l, 0, ((qd * 2048 + ub * 512, 512),), False))
                        for ob in range(4):
                            pl.append(("w_down", l, qd * 2048, ((ob * 512, 512),), False))
            return pl

        PL = plan()
        wstate = dict(issued=0, used=0)

        def w_issue(j):
            name, l, r0, segs, bias = PL[j]
            s = slots[j % NSLOT]
            fns = []
            c = 0
            for (c0, n) in segs:
                src = Wd[name][l, r0:r0 + D, c0:c0 + n].rearrange("(kc p) n -> p kc n", p=128)
                dst = s.t[:, 0:KC, c:c + n]
                fns.append(lambda src=src, dst=dst: nc.gpsimd.dma_start(out=dst, in_=src))
                if bias:
                    bsrc = b_in_d[l:l + 1, c0:c0 + n]
                    bdst = s.t[0:1, KC, c:c + n]
                    fns.append(lambda bsrc=bsrc, bdst=bdst: nc.gpsimd.dma_start(out=bdst, in_=bsrc))
                c += n
            DMA('pool', "wsem%d" % (j % NSLOT), fns, reads=(), writes=(s,))

        def wnext(desc):
            j = wstate['used']
            assert PL[j] == desc, (j, PL[j], desc)
            while wstate['issued'] < len(PL) and wstate['issued'] <= j + NSLOT - 1:
                w_issue(wstate['issued'])
                wstate['issued'] += 1
            wstate['used'] += 1
            return slots[j % NSLOT]

        def nextpd():
            p = PD[pdi[0] % 2]
            pdi[0] += 1
            return p

        def fm_group(ps, slot, col, act, T, nk=KC):
            for kc in range(nk):
                E('pe', lambda kc=kc: nc.tensor.matmul(ps.t[:, 0:T], lhsT=slot.t[:, kc, col * 128:(col + 1) * 128],
                                                       rhs=act.t[:, kc, 0:T], start=(kc == 0), stop=(kc == nk - 1)),
                  reads=(slot, act), writes=(ps,))

        spdma(vecs.t[:], vecs_d, writes=(vecs,))
        spdma(cst.t[:], consts_d, writes=(cst,))
        E('dve', lambda: nc.vector.tensor_copy(out=identb.t[:], in_=ident), reads=(cst,), writes=(identb,))
        E('dve', lambda: nc.vector.memset(onesb.t[:], 1.0), writes=(onesb,))
        E('dve', lambda: nc.vector.memset(onesf.t[:], 1.0), writes=(onesf,))
        E('dve', lambda: nc.vector.memset(negbig.t[:], -1e30), writes=(negbig,))
        E('dve', lambda: nc.vector.memset(rmask.t[:], 1.0), writes=(rmask,))
        for c in range(16):
            E('dve', lambda c=c: nc.vector.memset(rmask.t[:, c * 32:c * 32 + 1], 0.0), writes=(rmask,))
        E('dve', lambda: nc.vector.memset(vext.t[:, :, 256:257], 1.0), writes=(vext,))

        def V(l, off, j=0, n=1):
            return vecs.t[:, l * V_PER + off + j: l * V_PER + off + j + n]

        lbe = k.sb("lbe", [128, NL, 16], F32); lbm = k.sb("lbm", [128, 16], F32); lbs = k.sb("lbs", [128, 16], F32)
        E('dve', lambda: nc.vector.tensor_copy(out=lbm.t[:], in_=V(0, V_LBR, 0, 16)), reads=(vecs,), writes=(lbm,))
        for l in range(1, NL):
            E('dve', lambda l=l: nc.vector.tensor_tensor(out=lbm.t[:], in0=lbm.t[:], in1=V(l, V_LBR, 0, 16), op=ALU.max), reads=(vecs, lbm), writes=(lbm,))
        for l in range(NL):
            E('dve', lambda l=l: nc.vector.tensor_tensor(out=lbe.t[:, l, :], in0=V(l, V_LBR, 0, 16), in1=lbm.t[:], op=ALU.subtract), reads=(vecs, lbm), writes=(lbe,))
            E('act', lambda l=l: nc.scalar.activation(out=lbe.t[:, l, :], in_=lbe.t[:, l, :], func=AF.Exp), reads=(lbe,), writes=(lbe,))
        E('dve', lambda: nc.vector.tensor_copy(out=lbs.t[:], in_=lbe.t[:, 0, :]), reads=(lbe,), writes=(lbs,))
        for l in range(1, NL):
            E('dve', lambda l=l: nc.vector.tensor_tensor(out=lbs.t[:], in0=lbs.t[:], in1=lbe.t[:, l, :], op=ALU.add), reads=(lbe, lbs), writes=(lbs,))
        E('dve', lambda: nc.vector.reciprocal(out=lbs.t[:], in_=lbs.t[:]), reads=(lbs,), writes=(lbs,))
        for l in range(NL):
            E('dve', lambda l=l: nc.vector.tensor_tensor(out=lbe.t[:, l, :], in0=lbe.t[:, l, :], in1=lbs.t[:], op=ALU.mult), reads=(lbe, lbs), writes=(lbe,))
        E('dve', lambda: nc.vector.memset(lbt.t[:, 0, :], 0.0), writes=(lbt,))
        for l in range(1, NL):
            E('dve', lambda l=l: nc.vector.tensor_tensor(out=lbt.t[:, l, :], in0=lbt.t[:, l - 1, :], in1=lbe.t[:, l, :], op=ALU.add), reads=(lbe, lbt), writes=(lbt,))
        E('dve', lambda: nc.vector.tensor_scalar(out=omlt.t[:], in0=lbt.t[:], scalar1=-1.0, scalar2=1.0, op0=ALU.mult, op1=ALU.add), reads=(lbt,), writes=(omlt,))
        E('dve', lambda: nc.vector.tensor_scalar(out=nomlt.t[:], in0=omlt.t[:], scalar1=-1.0, scalar2=None, op0=ALU.mult), reads=(omlt,), writes=(nomlt,))

        csf = k.sb("csf", [128, KC, 2], F32); csb = k.sb("csb", [128, KC, 2], BF16)
        spdma(csf.t[:], cvec, writes=(csf,))
        E('act', lambda: nc.scalar.activation(out=csb.t[:], in_=csf.t[:], func=AF.Silu), reads=(csf,), writes=(csb,))
        for l in range(NL):
            for jb in range(24):
                s = wnext(("ada_w", l, 0, ((jb * 512, 512),), False))
                for jj in range(4):
                    j = jb * 4 + jj
                    for kc in range(KC):
                        E('pe', lambda kc=kc, jj=jj, j=j, s=s: nc.tensor.matmul(PA.t[:, 2 * j:2 * j + 2], lhsT=s.t[:, kc, jj * 128:(jj + 1) * 128],
                                                                                rhs=csb.t[:, kc, :], start=(kc == 0), stop=(kc == KC - 1)),
                          reads=(s, csb), writes=(PA,))
            for sq in range(2):
                pav = PA.t[:, 0:192].rearrange("p (j s) -> p j s", s=2)[:, :, sq]
                E('dve', lambda l=l, sq=sq, pav=pav: nc.vector.tensor_tensor(out=mods.t[:, l, :, sq], in0=pav, in1=V(l, V_ADAB, 0, 96), op=ALU.add),
                  reads=(PA, vecs), writes=(mods,))
                for which, (goff, scoff) in enumerate(((V_N1G, 16), (V_N2G, 64))):
                    E('dve', lambda l=l, sq=sq, which=which, goff=goff, scoff=scoff: nc.vector.scalar_tensor_tensor(
                        out=A12.t[:, l, which, :, sq], in0=mods.t[:, l, scoff:scoff + 16, sq], scalar=1.0, in1=V(l, goff, 0, 16),
                        op0=ALU.add, op1=ALU.mult), reads=(mods, vecs), writes=(A12,))

        def modnorm(l, which, sq, T, shoff):
            for kc in range(KC):
                E('act', lambda kc=kc: nc.scalar.activation(out=sqb.t[:, 0:T], in_=xT.t[:, kc, 0:T], func=AF.Square), reads=(xT,), writes=(sqb,))
                E('pe', lambda kc=kc: nc.tensor.matmul(PA.t[:, 0:T], lhsT=onesb.t[:, :], rhs=sqb.t[:, 0:T], start=(kc == 0), stop=(kc == KC - 1)),
                  reads=(onesb, sqb), writes=(PA,))
            E('act', lambda: nc.scalar.activation(out=rstd.t[:, 0:T], in_=PA.t[:, 0:T], func=AF.Sqrt, scale=1.0 / D, bias=epsc), reads=(PA, smc), writes=(rstd,))
            E('dve', lambda: nc.vector.reciprocal(out=rstd.t[:, 0:T], in_=rstd.t[:, 0:T]), reads=(rstd,), writes=(rstd,))
            for kc in range(KC):
                E('dve', lambda kc=kc: nc.vector.tensor_tensor(out=tmpf.t[:, 0:T], in0=xT.t[:, kc, 0:T], in1=rstd.t[:, 0:T], op=ALU.mult), reads=(xT, rstd), writes=(tmpf,))
                E('act', lambda kc=kc: nc.scalar.activation(out=hT.t[:, kc, 0:T], in_=tmpf.t[:, 0:T], func=AF.Identity,
                                                            scale=A12.t[:, l, which, kc, sq:sq + 1], bias=mods.t[:, l, shoff + kc, sq:sq + 1]),
                  reads=(tmpf, A12, mods), writes=(hT,))

        smc = k.sb("smc", [128, 4], F32)
        E('dve', lambda: nc.vector.memset(smc.t[:, 0:1], EPS), writes=(smc,))
        E('dve', lambda: nc.vector.memset(smc.t[:, 1:2], 1.0), writes=(smc,))
        E('dve', lambda: nc.vector.memset(smc.t[:, 2:3], 0.0), writes=(smc,))
        epsc = smc.t[:, 0:1]; onec = smc.t[:, 1:2]

        def mlstm_gates(l, T, L, nch, gs):
            for c in range(nch):
                for kc in range(KC):
                    E('pe', lambda kc=kc, c=c: nc.tensor.matmul(PT.t[0:L, 0:16], lhsT=hT.t[:, kc, c * L:(c + 1) * L], rhs=gs.t[:, kc, 0:16], start=(kc == 0), stop=False),
                      reads=(hT, gs), writes=(PT,))
                E('pe', lambda: nc.tensor.matmul(PT.t[0:L, 0:16], lhsT=onesb.t[0:1, 0:L], rhs=gs.t[0:1, KC, 0:16], start=False, stop=True), reads=(onesb, gs), writes=(PT,))
                E('dve', lambda c=c: nc.vector.tensor_copy(out=igc.t[0:L, c, :], in_=PT.t[0:L, 0:8]), reads=(PT,), writes=(igc,))
                E('act', lambda c=c: nc.scalar.activation(out=gtmp.t[0:L, :], in_=PT.t[0:L, 8:16], func=AF.Exp, scale=-1.0), reads=(PT,), writes=(gtmp,))
                E('act', lambda c=c: nc.scalar.activation(out=nlf.t[0:L, c, :], in_=gtmp.t[0:L, :], func=AF.Ln, bias=onec[0:L, :]), reads=(gtmp, smc), writes=(nlf,))
                E('pe', lambda c=c: nc.tensor.matmul(PA.t[0:L, 0:8], lhsT=tri[0:L, 0:L], rhs=nlf.t[0:L, c, :], start=True, stop=True), reads=(cst, nlf), writes=(PA,))
                E('dve', lambda c=c: nc.vector.tensor_copy(out=nbc.t[0:L, c, :], in_=PA.t[0:L, 0:8]), reads=(PA,), writes=(nbc,))
                E('dve', lambda c=c: nc.vector.tensor_tensor(out=acadj.t[0:L, c, :], in0=igc.t[0:L, c, :], in1=nbc.t[0:L, c, :], op=ALU.add), reads=(igc, nbc), writes=(acadj,))
                E('dve', lambda c=c: nc.vector.tensor_scalar(out=acadj.t[0:L, c, :], in0=acadj.t[0:L, c, :], scalar1=LN16, scalar2=None, op0=ALU.add), reads=(acadj,), writes=(acadj,))

        def mlstm_head(l, hd, T, L, nch, g, first, sq):
            if first and g == "p":
                E('dve', lambda: nc.vector.memset(Cf.t[:], 0.0), writes=(Cf,))
            else:
                srcC = (Cs_in if first else O[g]["C"])[l, hd].rearrange("(dc p) v -> p dc v", p=128)
                srcn = (ns_in if first else O[g]["n"])[l, hd]
                spdma(Cf.t[:, :, 0:256], srcC, writes=(Cf,))
                spdma(Cf.t[:, :, 256], srcn, writes=(Cf,))
            E('act', lambda: nc.scalar.activation(out=Cb.t[:], in_=Cf.t[:], func=AF.Copy), reads=(Cf,), writes=(Cb,))
            s = wnext(("w_in", l, 0, ((hd * 256, 256), (2048 + hd * 256, 256)), False))
            for b4 in range(4):
                fb = (hd * 2 + b4) if b4 < 2 else (16 + hd * 2 + (b4 - 2))
                ps = nextpd()
                fm_group(ps, s, b4, hT, T)
                E('act', lambda b4=b4, fb=fb, ps=ps: nc.scalar.activation(out=uqk.t[:, b4, 3:3 + T], in_=ps.t[:, 0:T], func=AF.Identity, bias=V(l, V_BIN, fb)),
                  reads=(ps, vecs), writes=(uqk,))
                E('dve', lambda b4=b4, fb=fb: nc.vector.tensor_copy(out=uqk.t[:, b4, 0:3], in_=hist.t[:, l, fb, :]), reads=(hist,), writes=(uqk,))
                E('dve', lambda b4=b4, fb=fb: nc.vector.tensor_scalar(out=cacc.t[:, 0:T], in0=uqk.t[:, b4, 0:T], scalar1=V(l, V_CW, fb), scalar2=V(l, V_CB, fb),
                                                                      op0=ALU.mult, op1=ALU.add), reads=(uqk, vecs), writes=(cacc,))
                for j in range(1, 4):
                    E('dve', lambda b4=b4, fb=fb, j=j: nc.vector.scalar_tensor_tensor(out=cacc.t[:, 0:T], in0=uqk.t[:, b4, j:j + T], scalar=V(l, V_CW, j * 32 + fb),
                                                                                      in1=cacc.t[:, 0:T], op0=ALU.mult, op1=ALU.add), reads=(uqk, vecs, cacc), writes=(cacc,))
                E('dve', lambda b4=b4, fb=fb: nc.vector.tensor_copy(out=hist.t[:, l, fb, :], in_=uqk.t[:, b4, T:T + 3]), reads=(uqk,), writes=(hist,))
                E('act', lambda b4=b4: nc.scalar.activation(out=qkT.t[:, b4, 0:T], in_=cacc.t[:, 0:T], func=AF.Silu), reads=(cacc,), writes=(qkT,))
            s = wnext(("w_in", l, 0, ((4096 + hd * 256, 256), (6144 + hd * 256, 256)), True))
            for c in range(nch):
                for kc in range(KC):
                    E('pe', lambda kc=kc, c=c: nc.tensor.matmul(PT.t[0:L, 0:512], lhsT=hT.t[:, kc, c * L:(c + 1) * L], rhs=s.t[:, kc, 0:512], start=(kc == 0), stop=False),
                      reads=(hT, s), writes=(PT,))
                E('pe', lambda: nc.tensor.matmul(PT.t[0:L, 0:512], lhsT=onesb.t[0:1, 0:L], rhs=s.t[0:1, KC, 0:512], start=False, stop=True), reads=(onesb, s), writes=(PT,))
                E('act', lambda c=c: nc.scalar.activation(out=vext.t[0:L, c, 0:256], in_=PT.t[0:L, 0:256], func=AF.Copy), reads=(PT,), writes=(vext,))
                E('act', lambda c=c: nc.scalar.activation(out=sigo.t[0:L, c, :], in_=PT.t[0:L, 256:512], func=AF.Sigmoid), reads=(PT,), writes=(sigo,))
            m0 = mst.t[:, l * 8 + hd:l * 8 + hd + 1]
            for c in range(nch):
                cs_ = slice(c * L, (c + 1) * L)
                E('dve', lambda c=c: nc.vector.tensor_scalar(out=igrep.t[0:L, :], in0=onesf.t[0:L, :], scalar1=igc.t[0:L, c, hd:hd + 1], scalar2=None, op0=ALU.mult),
                  reads=(onesf, igc), writes=(igrep,))
                E('dve', lambda c=c: nc.vector.tensor_scalar(out=nlfrep.t[0:L, :], in0=onesf.t[0:L, :], scalar1=nlf.t[0:L, c, hd:hd + 1], scalar2=None, op0=ALU.mult),
                  reads=(onesf, nlf), writes=(nlfrep,))
                E('pe', lambda: nc.tensor.matmul(PA.t[:, 0:L], lhsT=igrep.t[0:L, :], rhs=ident[0:L, 0:L], start=True, stop=False), reads=(igrep, cst), writes=(PA,))
                E('pe', lambda: nc.tensor.matmul(PA.t[:, 0:L], lhsT=nlfrep.t[0:L, :], rhs=tri[0:L, 0:L], start=False, stop=True), reads=(nlfrep, cst), writes=(PA,))
                E('pe', lambda: nc.tensor.matmul(PA.t[:, 256:257], lhsT=nlfrep.t[0:L, :], rhs=onesf.t[0:L, 0:1], start=True, stop=True), reads=(nlfrep, onesf), writes=(PA,))
                E('dve', lambda: nc.vector.tensor_tensor_scan(out=grow.t[:, 0:L], data0=PA.t[:, 0:L], data1=negbig.t[:, 0:L], initial=m0, op0=ALU.max, op1=ALU.max),
                  reads=(PA, negbig, mst), writes=(grow,))
                E('dve', lambda: nc.vector.scalar_tensor_tensor(out=junk.t[0:L, 0:L], in0=grow.t[0:L, 0:L], scalar=1.0, in1=ident[0:L, 0:L], op0=ALU.mult, op1=ALU.mult,
                                                                accum_out=sm.t[0:L, 0:1]), reads=(grow, cst), writes=(junk, sm))
                E('dve', lambda: nc.vector.tensor_tensor(out=dtmp.t[0:L, 0:L], in0=maskb[0:L, 0:L], in1=grow.t[0:L, 0:L], op=ALU.subtract), reads=(cst, grow), writes=(dtmp,))
                E('act', lambda c=c: nc.scalar.activation(out=DT.t[0:L, 0:L], in_=dtmp.t[0:L, 0:L], func=AF.Exp, bias=acadj.t[0:L, c, hd:hd + 1]), reads=(dtmp, acadj), writes=(DT,))
                for dc in range(2):
                    E('pe', lambda dc=dc: nc.tensor.matmul(PS.t[0:L, 0:L], lhsT=qkT.t[:, 2 + dc, cs_], rhs=qkT.t[:, dc, cs_], start=(dc == 0), stop=(dc == 1)), reads=(qkT,), writes=(PS,))
                E('dve', lambda: nc.vector.tensor_tensor(out=STb.t[0:L, 0:L], in0=PS.t[0:L, 0:L], in1=DT.t[0:L, 0:L], op=ALU.mult), reads=(PS, DT), writes=(STb,))
                E('act', lambda: nc.scalar.activation(out=wrow.t[:, 0:L], in_=grow.t[:, 0:L], func=AF.Exp, scale=-1.0, bias=m0), reads=(grow, mst), writes=(wrow,))
                for dc in range(2):
                    E('dve', lambda dc=dc: nc.vector.tensor_tensor(out=qsc.t[:, dc, 0:L], in0=qkT.t[:, dc, cs_], in1=wrow.t[:, 0:L], op=ALU.mult), reads=(qkT, wrow), writes=(qsc,))
                E('pe', lambda c=c: nc.tensor.matmul(PN.t[0:L, 0:257], lhsT=STb.t[0:L, 0:L], rhs=vext.t[0:L, c, :], start=True, stop=False), reads=(STb, vext), writes=(PN,))
                for dc in range(2):
                    E('pe', lambda dc=dc: nc.tensor.matmul(PN.t[0:L, 0:257], lhsT=qsc.t[:, dc, 0:L], rhs=Cb.t[:, dc, :], start=False, stop=(dc == 1)), reads=(qsc, Cb), writes=(PN,))
                E('dve', lambda c=c: nc.vector.tensor_tensor(out=sm.t[0:L, 1:2], in0=nbc.t[0:L, c, hd:hd + 1], in1=sm.t[0:L, 0:1], op=ALU.subtract), reads=(nbc, sm), writes=(sm,))
                E('act', lambda: nc.scalar.activation(out=sm.t[0:L, 2:3], in_=sm.t[0:L, 1:2], func=AF.Exp), reads=(sm,), writes=(sm,))
                E('act', lambda: nc.scalar.activation(out=sm.t[0:L, 3:4], in_=PN.t[0:L, 256:257], func=AF.Abs), reads=(PN,), writes=(sm,))
                E('dve', lambda: nc.vector.tensor_tensor(out=sm.t[0:L, 3:4], in0=sm.t[0:L, 3:4], in1=sm.t[0:L, 2:3], op=ALU.max), reads=(sm,), writes=(sm,))
                E('act', lambda: nc.scalar.activation(out=junk.t[0:L, 0:256], in_=PN.t[0:L, 0:256], func=AF.Square, accum_out=sm.t[0:L, 4:5]), reads=(PN,), writes=(junk, sm))
                E('dve', lambda: nc.vector.scalar_tensor_tensor(out=sm.t[0:L, 5:6], in0=sm.t[0:L, 3:4], scalar=EPS, in1=sm.t[0:L, 3:4], op0=ALU.mult, op1=ALU.mult), reads=(sm,), writes=(sm,))
                E('dve', lambda: nc.vector.scalar_tensor_tensor(out=sm.t[0:L, 6:7], in0=sm.t[0:L, 4:5], scalar=1.0 / 256, in1=sm.t[0:L, 5:6], op0=ALU.mult, op1=ALU.add), reads=(sm,), writes=(sm,))
                E('act', lambda: nc.scalar.activation(out=sm.t[0:L, 6:7], in_=sm.t[0:L, 6:7], func=AF.Sqrt), reads=(sm,), writes=(sm,))
                E('dve', lambda: nc.vector.reciprocal(out=sm.t[0:L, 7:8], in_=sm.t[0:L, 6:7]), reads=(sm,), writes=(sm,))
                E('dve', lambda c=c: nc.vector.scalar_tensor_tensor(out=ytm.t[0:L, :], in0=PN.t[0:L, 0:256], scalar=sm.t[0:L, 7:8], in1=sigo.t[0:L, c, :], op0=ALU.mult, op1=ALU.mult),
                  reads=(PN, sm, sigo), writes=(ytm,))
                for vc in range(2):
                    E('pe', lambda vc=vc: nc.tensor.transpose(out=PX.t[:, vc * 128:vc * 128 + L], in_=ytm.t[0:L, vc * 128:(vc + 1) * 128], identity=identb.t[0:L, 0:L]), reads=(ytm, identb), writes=(PX,))
                    E('act', lambda vc=vc: nc.scalar.activation(out=yaT.t[:, hd * 2 + vc, cs_], in_=PX.t[:, vc * 128:vc * 128 + L], func=AF.Identity, scale=V(l, V_MAN, hd * 2 + vc)),
                      reads=(PX, vecs), writes=(yaT,))
                E('dve', lambda: nc.vector.tensor_scalar(out=sm.t[:, 8:9], in0=grow.t[:, L - 1:L], scalar1=-1.0, scalar2=None, op0=ALU.mult), reads=(grow,), writes=(sm,))
                E('act', lambda c=c: nc.scalar.activation(out=sm.t[0:L, 9:10], in_=acadj.t[0:L, c, hd:hd + 1], func=AF.Exp, bias=sm.t[0:L, 8:9]), reads=(acadj, sm), writes=(sm,))
                for dc in range(2):
                    E('pe', lambda dc=dc: nc.tensor.transpose(out=PX.t[0:L, 256 + dc * 128:256 + (dc + 1) * 128], in_=qkT.t[:, 2 + dc, cs_], identity=identb.t[:, :]), reads=(qkT, identb), writes=(PX,))
                E('act', lambda: nc.scalar.activation(out=ktm.t[0:L, :], in_=PX.t[0:L, 256:512], func=AF.Identity, scale=sm.t[0:L, 9:10]), reads=(PX, sm), writes=(ktm,))
                pcs = [PC, PT]
                for dc in range(2):
                    E('pe', lambda dc=dc, c=c: nc.tensor.matmul(pcs[dc].t[:, 0:257], lhsT=ktm.t[0:L, dc * 128:(dc + 1) * 128], rhs=vext.t[0:L, c, :], start=True, stop=True), reads=(ktm, vext), writes=(pcs[dc],))
                for dc in range(2):
                    E('dve', lambda dc=dc: nc.vector.scalar_tensor_tensor(out=Cf.t[:, dc, :], in0=Cf.t[:, dc, :], scalar=wrow.t[:, L - 1:L], in1=pcs[dc].t[:, 0:257], op0=ALU.mult, op1=ALU.add),
                      reads=(Cf, wrow, pcs[dc]), writes=(Cf,))
                E('act', lambda: nc.scalar.activation(out=Cb.t[:], in_=Cf.t[:], func=AF.Copy), reads=(Cf,), writes=(Cb,))
                E('dve', lambda: nc.vector.tensor_tensor(out=mst.t[:, l * 8 + hd:l * 8 + hd + 1], in0=grow.t[:, L - 1:L], in1=PA.t[:, 256:257], op=ALU.subtract), reads=(grow, PA), writes=(mst,))
            spdma(O[g]["C"][l, hd].rearrange("(dc p) v -> p dc v", p=128), Cf.t[:, :, 0:256], reads=(Cf,))
            spdma(O[g]["n"][l, hd], Cf.t[:, :, 256], reads=(Cf,))

        def hgrn_group(l, hp, T, Lh, nch, g, first, sq):
            s = wnext(("w_in", l, 0, ((8192 + hp * 256, 256), (10240 + hp * 256, 256)), False))
            for hh in range(2):
                ps = nextpd(); fm_group(ps, s, hh, hT, T)
                E('act', lambda hh=hh, ps=ps: nc.scalar.activation(out=uqk.t[:, hh, 0:T], in_=ps.t[:, 0:T], func=AF.Silu, bias=V(l, V_BIN, 64 + hp * 2 + hh)), reads=(ps, vecs), writes=(uqk,))
            for hh in range(2):
                ps = nextpd(); fm_group(ps, s, 2 + hh, hT, T)
                E('act', lambda hh=hh, ps=ps: nc.scalar.activation(out=uqk.t[:, 2 + hh, 0:T], in_=ps.t[:, 0:T], func=AF.Sigmoid, bias=V(l, V_BIN, 80 + hp * 2 + hh)), reads=(ps, vecs), writes=(uqk,))
            s = wnext(("w_in", l, 0, ((12288 + hp * 256, 256), (14336 + hp * 256, 256)), True))
            for c in range(nch):
                for kc in range(KC):
                    E('pe', lambda kc=kc, c=c: nc.tensor.matmul(PT.t[0:Lh, 0:256], lhsT=hT.t[:, kc, c * Lh:(c + 1) * Lh], rhs=s.t[:, kc, 0:256], start=(kc == 0), stop=False), reads=(hT, s), writes=(PT,))
                E('pe', lambda: nc.tensor.matmul(PT.t[0:Lh, 0:256], lhsT=onesb.t[0:1, 0:Lh], rhs=s.t[0:1, KC, 0:256], start=False, stop=True), reads=(onesb, s), writes=(PT,))
                E('act', lambda c=c: nc.scalar.activation(out=vtm.t[0:Lh, c, :], in_=PT.t[0:Lh, 0:256], func=AF.Copy), reads=(PT,), writes=(vtm,))
            for hh in range(2):
                ps = nextpd(); fm_group(ps, s, 2 + hh, hT, T)
                E('act', lambda hh=hh, ps=ps: nc.scalar.activation(out=qkT.t[:, hh, 0:T], in_=ps.t[:, 0:T], func=AF.Sigmoid, bias=V(l, V_BIN, 112 + hp * 2 + hh)), reads=(ps, vecs), writes=(qkT,))
            qth = qkT.t[:, 2, :]; kth = qkT.t[:, 3, :]
            lfh = cacc; kkh = tmpf
            for hh in range(2):
                hd = hp * 2 + hh
                if first and g == "p":
                    E('dve', lambda: nc.vector.memset(Sf.t[:], 0.0), writes=(Sf,))
                else:
                    spdma(Sf.t[:], (Ss_in if first else O[g]["S"])[l, hd], writes=(Sf,))
                E('act', lambda: nc.scalar.activation(out=Sb.t[:], in_=Sf.t[:], func=AF.Copy), reads=(Sf,), writes=(Sb,))
                E('act', lambda hh=hh, hd=hd: nc.scalar.activation(out=lfh.t[:, 0:T], in_=uqk.t[:, 2 + hh, 0:T], func=AF.Ln, scale=omlt.t[:, l, hd:hd + 1], bias=lbt.t[:, l, hd:hd + 1]),
                  reads=(uqk, omlt, lbt), writes=(lfh,))
                E('dve', lambda hh=hh, hd=hd: nc.vector.tensor_scalar(out=kkh.t[:, 0:T], in0=uqk.t[:, 2 + hh, 0:T], scalar1=nomlt.t[:, l, hd:hd + 1], scalar2=omlt.t[:, l, hd:hd + 1], op0=ALU.mult, op1=ALU.add),
                  reads=(uqk, nomlt, omlt), writes=(kkh,))
                E('dve', lambda: nc.vector.tensor_tensor_scan(out=bh.t[:, 0:T], data0=rmask.t[:, 0:T], data1=lfh.t[:, 0:T], initial=0.0, op0=ALU.mult, op1=ALU.add), reads=(rmask, lfh), writes=(bh,))
                E('act', lambda: nc.scalar.activation(out=ebh.t[:, 0:T], in_=bh.t[:, 0:T], func=AF.Exp), reads=(bh,), writes=(ebh,))
                E('act', lambda: nc.scalar.activation(out=enbh.t[:, 0:T], in_=bh.t[:, 0:T], func=AF.Exp, scale=-1.0), reads=(bh,), writes=(enbh,))
                E('dve', lambda hh=hh: nc.vector.tensor_tensor(out=qth[:, 0:T], in0=uqk.t[:, hh, 0:T], in1=ebh.t[:, 0:T], op=ALU.mult), reads=(uqk, ebh), writes=(qkT,))
                E('dve', lambda: nc.vector.tensor_tensor(out=kth[:, 0:T], in0=kkh.t[:, 0:T], in1=enbh.t[:, 0:T], op=ALU.mult), reads=(kkh, enbh), writes=(qkT,))
                for c in range(nch):
                    cs_ = slice(c * Lh, (c + 1) * Lh)
                    E('pe', lambda cs_=cs_: nc.tensor.matmul(PS.t[0:Lh, cs_], lhsT=kth[:, cs_], rhs=qth[:, cs_], start=True, stop=True), reads=(qkT,), writes=(PS,))
                E('dve', lambda: nc.vector.tensor_tensor(out=Abf.t[0:Lh, 0:T], in0=PS.t[0:Lh, 0:T], in1=tri8[0:Lh, 0:T], op=ALU.mult), reads=(PS, cst), writes=(Abf,))
                for c in range(nch):
                    cs_ = slice(c * Lh, (c + 1) * Lh)
                    ce = (c + 1) * Lh - 1
                    E('dve', lambda cs_=cs_, ce=ce: nc.vector.tensor_scalar(out=khT.t[:, cs_], in0=kth[:, cs_], scalar1=ebh.t[:, ce:ce + 1], scalar2=None, op0=ALU.mult), reads=(qkT, ebh), writes=(khT,))
                    E('pe', lambda cs_=cs_, c=c: nc.tensor.transpose(out=PX.t[0:Lh, (c % 8) * 128:(c % 8 + 1) * 128], in_=khT.t[:, cs_], identity=identb.t[:, :]), reads=(khT, identb), writes=(PX,))
                    if c % 8 == 7 or c == nch - 1:
                        c0 = (c // 8) * 8
                        nn = c - c0 + 1
                        E('act', lambda c0=c0, nn=nn: nc.scalar.activation(out=khtm.t[0:Lh, c0 * 128:(c0 + nn) * 128], in_=PX.t[0:Lh, 0:nn * 128], func=AF.Copy), reads=(PX,), writes=(khtm,))
                for c in range(nch):
                    cs_ = slice(c * Lh, (c + 1) * Lh)
                    ce = (c + 1) * Lh - 1
                    vsl = vtm.t[0:Lh, c, hh * 128:(hh + 1) * 128]
                    E('pe', lambda cs_=cs_, vsl=vsl: nc.tensor.matmul(PN.t[:, cs_], lhsT=vsl, rhs=Abf.t[0:Lh, cs_], start=True, stop=False), reads=(vtm, Abf), writes=(PN,))
                    E('pe', lambda cs_=cs_: nc.tensor.matmul(PN.t[:, cs_], lhsT=Sb.t[:, :], rhs=qth[:, cs_], start=False, stop=True), reads=(Sb, qkT), writes=(PN,))
                    E('pe', lambda c=c, vsl=vsl: nc.tensor.matmul(PC.t[:, 0:128], lhsT=khtm.t[0:Lh, c * 128:(c + 1) * 128], rhs=vsl, start=True, stop=True), reads=(khtm, vtm), writes=(PC,))
                    E('dve', lambda ce=ce: nc.vector.scalar_tensor_tensor(out=Sf.t[:], in0=Sf.t[:], scalar=ebh.t[:, ce:ce + 1], in1=PC.t[:, 0:128], op0=ALU.mult, op1=ALU.add), reads=(Sf, ebh, PC), writes=(Sf,))
                    E('act', lambda: nc.scalar.activation(out=Sb.t[:], in_=Sf.t[:], func=AF.Copy), reads=(Sf,), writes=(Sb,))
                spdma(O[g]["S"][l, hd], Sf.t[:], reads=(Sf,))
                E('act', lambda: nc.scalar.activation(out=sqb.t[:, 0:T], in_=PN.t[:, 0:T], func=AF.Square), reads=(PN,), writes=(sqb,))
                E('pe', lambda: nc.tensor.matmul(PA.t[:, 0:T], lhsT=onesb.t[:, :], rhs=sqb.t[:, 0:T], start=True, stop=True), reads=(onesb, sqb), writes=(PA,))
                E('act', lambda: nc.scalar.activation(out=rstd.t[:, 0:T], in_=PA.t[:, 0:T], func=AF.Sqrt, scale=1.0 / 128, bias=epsc), reads=(PA, smc), writes=(rstd,))
                E('dve', lambda: nc.vector.reciprocal(out=rstd.t[:, 0:T], in_=rstd.t[:, 0:T]), reads=(rstd,), writes=(rstd,))
                E('dve', lambda hd=hd: nc.vector.scalar_tensor_tensor(out=tmpf.t[:, 0:T], in0=PN.t[:, 0:T], scalar=V(l, V_HBN, hd), in1=rstd.t[:, 0:T], op0=ALU.mult, op1=ALU.mult), reads=(PN, vecs, rstd), writes=(tmpf,))
                E('dve', lambda hd=hd, hh=hh: nc.vector.tensor_tensor(out=ybT.t[:, hd, 0:T], in0=tmpf.t[:, 0:T], in1=qkT.t[:, hh, 0:T], op=ALU.mult), reads=(tmpf, qkT), writes=(ybT,))

        def layer(l, T, g, first, sq):
            Lm = min(128, T); Lh = min(32, T)
            modnorm(l, 0, sq, T, 0)
            gs = wnext(("w_in", l, 0, ((20480, 16),), True))
            mlstm_gates(l, T, Lm, T // Lm, gs)
            for hd in range(8):
                mlstm_head(l, hd, T, Lm, T // Lm, g, first, sq)
            for hp in range(8):
                hgrn_group(l, hp, T, Lh, T // Lh, g, first, sq)
            for jb in range(4):
                s = wnext(("w_in", l, 0, ((16384 + jb * 512, 512),), False))
                for jj in range(4):
                    ps = nextpd(); fm_group(ps, s, jj, hT, T)
                    E('act', lambda jj=jj, ps=ps: nc.scalar.activation(out=sg4.t[:, jj, 0:T], in_=ps.t[:, 0:T], func=AF.Sigmoid, bias=V(l, V_BIN, 128 + jb * 4 + jj)), reads=(ps, vecs), writes=(sg4,))
                s = wnext(("w_br_a", l, 0, ((jb * 512, 512),), False))
                for jj in range(4):
                    ps = nextpd(); fm_group(ps, s, jj, yaT, T)
                    E('dve', lambda jj=jj, ps=ps: nc.vector.tensor_tensor(out=uqk.t[:, jj, 0:T], in0=ps.t[:, 0:T], in1=sg4.t[:, jj, 0:T], op=ALU.mult), reads=(ps, sg4), writes=(uqk,))
                s = wnext(("w_in", l, 0, ((18432 + jb * 512, 512),), False))
                for jj in range(4):
                    ps = nextpd(); fm_group(ps, s, jj, hT, T)
                    E('act', lambda jj=jj, ps=ps: nc.scalar.activation(out=sg4.t[:, jj, 0:T], in_=ps.t[:, 0:T], func=AF.Sigmoid, bias=V(l, V_BIN, 144 + jb * 4 + jj)), reads=(ps, vecs), writes=(sg4,))
                s = wnext(("w_br_b", l, 0, ((jb * 512, 512),), False))
                for jj in range(4):
                    ps = nextpd(); fm_group(ps, s, jj, ybT, T)
                    E('dve', lambda jj=jj, ps=ps: nc.vector.tensor_tensor(out=tmpf.t[:, 0:T], in0=ps.t[:, 0:T], in1=sg4.t[:, jj, 0:T], op=ALU.mult), reads=(ps, sg4), writes=(tmpf,))
                    E('dve', lambda jj=jj: nc.vector.tensor_tensor(out=uT.t[:, jb * 4 + jj, 0:T], in0=tmpf.t[:, 0:T], in1=uqk.t[:, jj, 0:T], op=ALU.add), reads=(tmpf, uqk), writes=(uT,))
            for jb in range(4):
                s = wnext(("w_o", l, 0, ((jb * 512, 512),), False))
                for jj in range(4):
                    j = jb * 4 + jj
                    ps = nextpd(); fm_group(ps, s, jj, uT, T)
                    E('dve', lambda j=j, ps=ps: nc.vector.scalar_tensor_tensor(out=xT.t[:, j, 0:T], in0=ps.t[:, 0:T], scalar=mods.t[:, l, 32 + j, sq:sq + 1], in1=xT.t[:, j, 0:T], op0=ALU.mult, op1=ALU.add),
                      reads=(ps, mods, xT), writes=(xT,))
            modnorm(l, 1, sq, T, 48)
            for qd in range(4):
                for ub in range(4):
                    s = wnext(("w_up", l, 0, ((qd * 2048 + ub * 512, 512),), False))
                    for jj in range(4):
                        ps = nextpd(); fm_group(ps, s, jj, hT, T)
                        E('act', lambda ps=ps: nc.scalar.activation(out=sqb.t[:, 0:T], in_=ps.t[:, 0:T], func=AF.Relu), reads=(ps,), writes=(sqb,))
                        E('dve', lambda ub=ub, jj=jj: nc.vector.tensor_tensor(out=uT.t[:, ub * 4 + jj, 0:T], in0=sqb.t[:, 0:T], in1=sqb.t[:, 0:T], op=ALU.mult), reads=(sqb,), writes=(uT,))
                for ob in range(4):
                    s = wnext(("w_down", l, qd * 2048, ((ob * 512, 512),), False))
                    for jj in range(4):
                        j = ob * 4 + jj
                        ps = nextpd(); fm_group(ps, s, jj, uT, T)
                        E('dve', lambda j=j, ps=ps: nc.vector.scalar_tensor_tensor(out=xT.t[:, j, 0:T], in0=ps.t[:, 0:T], scalar=mods.t[:, l, 80 + j, sq:sq + 1], in1=xT.t[:, j, 0:T], op0=ALU.mult, op1=ALU.add),
                          reads=(ps, mods, xT), writes=(xT,))

        def final_norm(T):
            for kc in range(KC):
                E('act', lambda kc=kc: nc.scalar.activation(out=sqb.t[:, 0:T], in_=xT.t[:, kc, 0:T], func=AF.Square), reads=(xT,), writes=(sqb,))
                E('pe', lambda kc=kc: nc.tensor.matmul(PA.t[:, 0:T], lhsT=onesb.t[:, :], rhs=sqb.t[:, 0:T], start=(kc == 0), stop=(kc == KC - 1)), reads=(onesb, sqb), writes=(PA,))
            E('act', lambda: nc.scalar.activation(out=rstd.t[:, 0:T], in_=PA.t[:, 0:T], func=AF.Sqrt, scale=1.0 / D, bias=epsc), reads=(PA, smc), writes=(rstd,))
            E('dve', lambda: nc.vector.reciprocal(out=rstd.t[:, 0:T], in_=rstd.t[:, 0:T]), reads=(rstd,), writes=(rstd,))
            fg0 = NL * V_PER
            for kc in range(KC):
                E('dve', lambda kc=kc: nc.vector.scalar_tensor_tensor(out=xT.t[:, kc, 0:T], in0=xT.t[:, kc, 0:T], scalar=vecs.t[:, fg0 + kc:fg0 + kc + 1], in1=rstd.t[:, 0:T], op0=ALU.mult, op1=ALU.mult),
                  reads=(xT, vecs, rstd), writes=(xT,))

        groups = []
        if NTP > 0:
            groups.append(("p", NTP, TP, 0))
        if WITH_S:
            groups.append(("s", 1, 16, 1))
        for (g, ntiles, T, sq) in groups:
            if g == "p":
                E('dve', lambda: nc.vector.memset(hist.t[:], 0.0), writes=(hist,))
                E('dve', lambda: nc.vector.memset(mst.t[:], 0.0), writes=(mst,))
            else:
                spdma(hist.t[:], convs_in, writes=(hist,))
                spdma(mst.t[:], msamp, writes=(mst,))
            for ti in range(ntiles):
                src = (xp[:, ti * T:(ti + 1) * T] if g == "p" else xs).rearrange("(kc p) t -> p kc t", p=128)
                spdma(xT.t[:, :, 0:T], src, writes=(xT,))
                for l in range(NL):
                    layer(l, T, g, ti == 0, sq)
                final_norm(T)
                dst = (yp[:, ti * T:(ti + 1) * T] if g == "p" else ys).rearrange("(kc p) t -> p kc t", p=128)
                spdma(dst, xT.t[:, :, 0:T], reads=(xT,))
            spdma(O[g]["conv"], hist.t[:], reads=(hist,))
            spdma(O[g]["m"], mst.t[0:1, :], reads=(mst,))
        for i in range(SPN):
            k.wait('sp', sp_last[i])
        assert wstate['used'] == len(PL), (wstate, len(PL))
    return nc


def _fm(v):
    return np.ascontiguousarray(v.reshape(-1, 128).T)


def _consts():
    s = np.arange(128)
    tri = (s[:, None] <= s[None, :]).astype(np.float32)
    ident = np.eye(128, dtype=np.float32)
    maskb = np.where(s[:, None] <= s[None, :], 0.0, -1e4).astype(np.float32)
    t32 = np.zeros((128, 32), np.float32)
    t32[:32] = tri[:32, :32]
    return np.ascontiguousarray(np.concatenate([tri, ident, maskb, np.tile(t32, (1, 16))], axis=1))


def make_inputs(NL, NTP, TP, inp, core):
    bp = core % 4
    f = lambda a: np.ascontiguousarray(np.asarray(a, dtype=np.float32))
    SEQP = max(NTP, 1) * TP
    m = {}
    m["xp"] = f(inp["x_prompt"][bp, :SEQP].T)
    m["xs"] = f(inp["x_sample"][core].T)
    cv = np.stack([_fm(inp["c_prompt"][bp]), _fm(inp["c_sample"][core])], axis=-1)
    m["cvec"] = f(cv)
    cols = []
    for l in range(NL):
        cols += [_fm(inp["norm1_g"][l]), _fm(inp["norm2_g"][l]), _fm(inp["ada_b"][l]), _fm(inp["b_in"][l, :20480])]
        cols += [_fm(inp["conv_w"][l, j]) for j in range(4)]
        cols += [_fm(inp["conv_b"][l]), _fm(inp["ma_norm"][l]), _fm(inp["hb_norm"][l]), _fm(inp["hgrn_lb_raw"][l])]
    cols.append(_fm(inp["final_g"]))
    m["vecs"] = f(np.concatenate(cols, axis=1))
    m["consts"] = _consts()
    m["msamp"] = f(np.broadcast_to(inp["state_mlstm_m"][:NL, core].reshape(1, NL * 8), (128, NL * 8)))
    cc = inp["cache_conv"][:NL, core]
    m["convs_in"] = f(cc.reshape(NL, 3, 32, 128).transpose(3, 0, 2, 1))
    m["ns_in"] = f(inp["state_mlstm_n"][:NL, core].reshape(NL, 8, 2, 128).transpose(0, 1, 3, 2))
    m["Cs_in"] = f(inp["state_mlstm_C"][:NL, core])
    m["Ss_in"] = f(inp["state_hgrn"][:NL, core])
    for n in ("w_in", "ada_w", "w_br_a", "w_br_b", "w_o", "w_up", "w_down", "b_in"):
        m[n] = f(inp[n][:NL])
    return m


def assemble(NL, NTP, TP, results, WITH_S=True):
    SEQP = NTP * TP
    outs = {}
    yp = np.stack([results[b]["yp"].T for b in range(4)], axis=0)
    ys = np.stack([results[c]["ys"].T for c in range(8)], axis=0)

    def grp(g, cores):
        conv = np.stack([results[c]["conv" + g].transpose(1, 3, 2, 0).reshape(NL, 3, 4096) for c in cores], axis=1)
        C = np.stack([results[c]["C" + g] for c in cores], axis=1)
        n = np.stack([results[c]["n" + g].transpose(0, 1, 3, 2).reshape(NL, 8, 256) for c in cores], axis=1)
        mm = np.stack([results[c]["m" + g].reshape(NL, 8) for c in cores], axis=1)
        S = np.stack([results[c]["S" + g] for c in cores], axis=1)
        return [np.ascontiguousarray(a, dtype=np.float32) for a in (conv, C, n, mm, S)]

    return tuple([np.ascontiguousarray(yp, dtype=np.float32), np.ascontiguousarray(ys, dtype=np.float32)] + grp("p", range(4)) + grp("s", range(8)))


def kernel(**inputs):
    NL, NTP, TP = 4, 8, 512
    inp = {k_: np.asarray(v) for k_, v in inputs.items()}
    nc = build(NL, NTP, True, TP)
    in_maps = [make_inputs(NL, NTP, TP, inp, c) for c in range(8)]
    res = run_bass_kernel_spmd(nc, in_maps, core_ids=list(range(8)))
    return assemble(NL, NTP, TP, res.results)
```

```python
import math
from contextlib import ExitStack
import numpy as np
import concourse.bass as bass
import concourse.mybir as mybir
from concourse.bass_utils import run_bass_kernel_spmd

F32 = mybir.dt.float32
BF16 = mybir.dt.bfloat16
AF = mybir.ActivationFunctionType
ALU = mybir.AluOpType

D = 2048
KC = 16
N_IN = 20496
EPS = 1e-6
NSLOT = 2
LN16 = math.log(1.0 / 16.0)
V_N1G, V_N2G, V_ADAB, V_BIN, V_CW, V_CB, V_MAN, V_HBN, V_LBR = 0, 16, 32, 128, 288, 416, 448, 464, 480
V_PER = 496


class Buf:
    def __init__(self, t):
        self.t = t
        self.w = None
        self.r = {}

    def __getitem__(self, k):
        return self.t[k]


class K:
    def __init__(self, nc, es):
        self.nc = nc
        self.es = es
        self.engs = {'pe': nc.tensor, 'act': nc.scalar, 'dve': nc.vector, 'pool': nc.gpsimd, 'sp': nc.sync}
        self.sem = {}
        self.cnt = {}
        self.seen = {e: {} for e in self.engs}
        for e in ('pe', 'act', 'dve', 'pool'):
            self.newsem(e)

    def newsem(self, name):
        self.sem[name] = self.es.enter_context(self.nc.semaphore(name))
        self.cnt[name] = 0

    def sb(self, name, shape, dt):
        return Buf(self.es.enter_context(self.nc.sbuf_tensor("s_" + name, shape, dt)))

    def ps(self, name, shape, dt):
        return Buf(self.es.enter_context(self.nc.psum_tensor("p_" + name, shape, dt)))

    def wait(self, eng, tok):
        if tok is None:
            return
        k, v = tok
        if self.seen[eng].get(k, 0) >= v:
            return
        self.engs[eng].wait_ge(self.sem[k], v)
        self.seen[eng][k] = v

    def deps(self, eng, reads, writes):
        for b in reads:
            if b.w is not None and not (eng == 'pe' and b.w[0] == 'pe'):
                self.wait(eng, b.w)
        for b in writes:
            if b.w is not None and not (eng == 'pe' and b.w[0] == 'pe'):
                self.wait(eng, b.w)
            for k, t in b.r.items():
                if not (eng == 'pe' and t[0] == 'pe'):
                    self.wait(eng, t)

    def mark(self, tok, reads, writes):
        for b in reads:
            b.r[tok[0]] = tok
        for b in writes:
            b.w = tok
            b.r = {}

    def E(self, eng, fn, reads=(), writes=()):
        self.deps(eng, reads, writes)
        inst = fn()
        self.cnt[eng] += 1
        inst.then_inc(self.sem[eng], 1)
        tok = (eng, self.cnt[eng])
        self.mark(tok, reads, writes)
        return tok

    def DMA(self, eng, semname, fns, reads=(), writes=()):
        self.deps(eng, reads, writes)
        for fn in fns:
            inst = fn()
            self.cnt[semname] += 16
            inst.then_inc(self.sem[semname], 16)
        tok = (semname, self.cnt[semname])
        self.mark(tok, reads, writes)
        return tok


def build(NL, NTP, WITH_S, TP=512):
    SEQP = max(NTP, 1) * TP
    nc = bass.Bass("TRN2", target_bir_lowering=False)

    def din(name, shape):
        return nc.dram_tensor(name, shape, F32, kind="ExternalInput").ap()

    def dout(name, shape):
        return nc.dram_tensor(name, shape, F32, kind="ExternalOutput").ap()

    xp = din("xp", [D, SEQP]); xs = din("xs", [D, 16]); cvec = din("cvec", [128, KC, 2])
    vecs_d = din("vecs", [128, NL * V_PER + 16]); consts_d = din("consts", [128, 896])
    msamp = din("msamp", [128, NL * 8]); convs_in = din("convs_in", [128, NL, 32, 3])
    ns_in = din("ns_in", [NL, 8, 128, 2]); Cs_in = din("Cs_in", [NL, 8, 256, 256]); Ss_in = din("Ss_in", [NL, 16, 128, 128])
    Wd = {"w_in": din("w_in", [NL, D, N_IN]), "ada_w": din("ada_w", [NL, D, 6 * D]),
          "w_br_a": din("w_br_a", [NL, D, D]), "w_br_b": din("w_br_b", [NL, D, D]), "w_o": din("w_o", [NL, D, D]),
          "w_up": din("w_up", [NL, D, 4 * D]), "w_down": din("w_down", [NL, 4 * D, D])}
    b_in_d = din("b_in", [NL, N_IN])
    yp = dout("yp", [D, SEQP]); ys = dout("ys", [D, 16])
    O = {}
    for g in ("p", "s"):
        O[g] = dict(conv=dout("conv" + g, [128, NL, 32, 3]), C=dout("C" + g, [NL, 8, 256, 256]),
                    n=dout("n" + g, [NL, 8, 128, 2]), m=dout("m" + g, [1, NL * 8]), S=dout("S" + g, [NL, 16, 128, 128]))

    with ExitStack() as es:
        es.enter_context(nc.allow_non_contiguous_dma(reason="small strided state vectors"))
        k = K(nc, es)
        E, DMA = k.E, k.DMA
        xT = k.sb("xT", [128, KC, TP], F32)
        hT = k.sb("hT", [128, KC, TP], BF16)
        yaT = k.sb("yaT", [128, KC, TP], BF16)
        ybT = k.sb("ybT", [128, KC, TP], BF16)
        uT = k.sb("uT", [128, KC, TP], BF16)
        slots = [k.sb("ws%d" % i, [128, KC + 1, 512], BF16) for i in range(NSLOT)]
        for i in range(NSLOT):
            k.newsem("wsem%d" % i)
        vecs = k.sb("vecs", [128, NL * V_PER + 16], F32)
        cst = k.sb("cst", [128, 896], F32)
        tri = cst.t[:, 0:128]; ident = cst.t[:, 128:256]; maskb = cst.t[:, 256:384]; tri8 = cst.t[:, 384:896]
        identb = k.sb("identb", [128, 128], BF16); onesb = k.sb("onesb", [128, 128], BF16)
        onesf = k.sb("onesf", [128, 128], F32); negbig = k.sb("negbig", [128, 128], F32)
        rmask = k.sb("rmask", [128, 512], F32)
        mods = k.sb("mods", [128, NL, 96, 2], F32)
        A12 = k.sb("A12", [128, NL, 2, 16, 2], F32)
        lbt = k.sb("lbt", [128, NL, 16], F32); omlt = k.sb("omlt", [128, NL, 16], F32); nomlt = k.sb("nomlt", [128, NL, 16], F32)
        hist = k.sb("hist", [128, NL, 32, 3], F32)
        mst = k.sb("mst", [128, NL * 8], F32)
        rstd = k.sb("rstd", [128, TP], F32); tmpf = k.sb("tmpf", [128, TP], F32); sqb = k.sb("sqb", [128, TP], BF16)
        sg4 = k.sb("sg4", [128, 4, TP], BF16)
        uqk = k.sb("uqk", [128, 4, TP + 3], F32); cacc = k.sb("cacc", [128, TP], F32)
        qkT = k.sb("qkT", [128, 4, TP], BF16)
        vext = k.sb("vext", [128, 4, 257], BF16); sigo = k.sb("sigo", [128, 4, 256], BF16)
        igc = k.sb("igc", [128, 4, 8], F32); nlf = k.sb("nlf", [128, 4, 8], F32); nbc = k.sb("nbc", [128, 4, 8], F32)
        acadj = k.sb("acadj", [128, 4, 8], F32); gtmp = k.sb("gtmp", [128, 8], F32)
        igrep = k.sb("igrep", [128, 128], F32); nlfrep = k.sb("nlfrep", [128, 128], F32)
        grow = k.sb("grow", [128, 128], F32); wrow = k.sb("wrow", [128, 128], F32)
        dtmp = k.sb("dtmp", [128, 128], F32); DT = k.sb("DT", [128, 128], F32); STb = k.sb("STb", [128, 128], BF16)
        junk = k.sb("junk", [128, 256], F32)
        qsc = k.sb("qsc", [128, 2, 128], BF16); ytm = k.sb("ytm", [128, 256], BF16); ktm = k.sb("ktm", [128, 256], BF16)
        sm = k.sb("sm", [128, 16], F32)
        Cf = k.sb("Cf", [128, 2, 257], F32); Cb = k.sb("Cb", [128, 2, 257], BF16)
        vtm = k.sb("vtm", [32, 16, 256], BF16)
        bh = rstd
        ebh = k.sb("ebh", [128, TP], F32); enbh = k.sb("enbh", [128, TP], F32)
        khT = k.sb("khT", [128, TP], BF16)
        Abf = k.sb("Abf", [32, TP], BF16); khtm = k.sb("khtm", [32, 16 * 128], BF16)
        Sf = k.sb("Sf", [128, 128], F32); Sb = k.sb("Sb", [128, 128], BF16)
        Sbc = [None] + [Buf(sg4.t[:, c // 4, (c % 4) * 128:(c % 4 + 1) * 128]) for c in range(1, 16)]
        P0 = k.ps("P0", [128, 512], F32); P1 = k.ps("P1", [128, 512], F32); PT = k.ps("PT", [128, 512], F32)
        PS = k.ps("PS", [128, 512], F32); PN = k.ps("PN", [128, 512], F32); PA = k.ps("PA", [128, 512], F32)
        PX = k.ps("PX", [128, 1024], BF16); PC = k.ps("PC", [128, 512], F32)
        PD = [P0, P1]
        pdi = [0]
        SPN = 8
        for i in range(SPN):
            k.newsem("sp%d" % i)
        spi = [0]
        sp_last = [None] * SPN

        def spdma(out_ap, in_ap, reads=(), writes=()):
            i = spi[0] % SPN
            spi[0] += 1
            k.wait('sp', sp_last[i])
            tok = DMA('sp', "sp%d" % i, [lambda: nc.sync.dma_start(out=out_ap, in_=in_ap)], reads, writes)
            sp_last[i] = tok
            return tok

        def plan():
            pl = []
            for l in range(NL):
                for jb in range(24):
                    pl.append(("ada_w", l, 0, ((jb * 512, 512),), False))
            tiles = [("p", i) for i in range(NTP)] + ([("s", 0)] if WITH_S else [])
            for _ in tiles:
                for l in range(NL):
                    pl.append(("w_in", l, 0, ((20480, 16),), True))
                    for hd in range(8):
                        pl.append(("w_in", l, 0, ((hd * 256, 256), (2048 + hd * 256, 256)), False))
                        pl.append(("w_in", l, 0, ((4096 + hd * 256, 256), (6144 + hd * 256, 256)), True))
                    for hp in range(8):
                        pl.append(("w_in", l, 0, ((8192 + hp * 256, 256), (10240 + hp * 256, 256)), False))
                        pl.append(("w_in", l, 0, ((12288 + hp * 256, 256), (14336 + hp * 256, 256)), True))
                    for jb in range(4):
                        pl.append(("w_in", l, 0, ((16384 + jb * 512, 512),), False))
                        pl.append(("w_br_a", l, 0, ((jb * 512, 512),), False))
                        pl.append(("w_in", l, 0, ((18432 + jb * 512, 512),), False))
                        pl.append(("w_br_b", l, 0, ((jb * 512, 512),), False))
                    for jb in range(4):
                        pl.append(("w_o", l, 0, ((jb * 512, 512),), False))
                    for qd in range(4):
                        for ub in range(4):
                            pl.append(("w_up", l, 0, ((qd * 2048 + ub * 512, 512),), False))
                        for ob in range(4):
                            pl.append(("w_down", l, qd * 2048, ((ob * 512, 512),), False))
            return pl

        PL = plan()
        wstate = dict(issued=0, used=0)
        N_ADA = NL * 24
        NTILES = NTP + (1 if WITH_S else 0)
        PER_PASS = (len(PL) - N_ADA) // max(NTILES, 1)
        USE_SCR = NTILES > 1
        if USE_SCR:
            SCRN = 100
            wscr_t = [nc.dram_tensor("wscr%d" % i, [min(SCRN, PER_PASS - i * SCRN), 128, (KC + 1) * 512], BF16, kind="Internal").ap()
                      for i in range((PER_PASS + SCRN - 1) // SCRN)]
            wscr = lambda pidx: wscr_t[pidx // SCRN][pidx % SCRN]
            scrb = [Buf(None) for _ in range(PER_PASS)]

        def w_issue(j):
            name, l, r0, segs, bias = PL[j]
            s = slots[j % NSLOT]
            fns = []
            if USE_SCR and j >= N_ADA:
                pno, pidx = divmod(j - N_ADA, PER_PASS)
                if pno >= 1:
                    src = wscr(pidx).rearrange("p (k n) -> p k n", n=512)
                    DMA('pool', "wsem%d" % (j % NSLOT), [lambda: nc.gpsimd.dma_start(out=s.t[:, :, :], in_=src)], reads=(scrb[pidx],), writes=(s,))
                    return
            c = 0
            for (c0, n) in segs:
                src = Wd[name][l, r0:r0 + D, c0:c0 + n].rearrange("(kc p) n -> p kc n", p=128)
                dst = s.t[:, 0:KC, c:c + n]
                fns.append(lambda src=src, dst=dst: nc.gpsimd.dma_start(out=dst, in_=src))
                if bias:
                    bsrc = b_in_d[l:l + 1, c0:c0 + n]
                    bdst = s.t[0:1, KC, c:c + n]
                    fns.append(lambda bsrc=bsrc, bdst=bdst: nc.gpsimd.dma_start(out=bdst, in_=bsrc))
                c += n
            DMA('pool', "wsem%d" % (j % NSLOT), fns, reads=(), writes=(s,))
            if USE_SCR and j >= N_ADA:
                pidx = (j - N_ADA) % PER_PASS
                spdma(wscr(pidx).rearrange("p (k n) -> p k n", n=512), s.t[:, :, :], reads=(s,), writes=(scrb[pidx],))

        def wnext(desc):
            j = wstate['used']
            assert PL[j] == desc, (j, PL[j], desc)
            while wstate['issued'] < len(PL) and wstate['issued'] <= j + NSLOT - 1:
                w_issue(wstate['issued'])
                wstate['issued'] += 1
            wstate['used'] += 1
            return slots[j % NSLOT]

        def nextpd():
            p = PD[pdi[0] % 2]
            pdi[0] += 1
            return p

        def fm_group(ps, slot, col, act, T, nk=KC):
            for kc in range(nk):
                E('pe', lambda kc=kc: nc.tensor.matmul(ps.t[:, 0:T], lhsT=slot.t[:, kc, col * 128:(col + 1) * 128],
                                                       rhs=act.t[:, kc, 0:T], start=(kc == 0), stop=(kc == nk - 1)),
                  reads=(slot, act), writes=(ps,))

        spdma(vecs.t[:], vecs_d, writes=(vecs,))
        spdma(cst.t[:], consts_d, writes=(cst,))
        E('dve', lambda: nc.vector.tensor_copy(out=identb.t[:], in_=ident), reads=(cst,), writes=(identb,))
        E('dve', lambda: nc.vector.memset(onesb.t[:], 1.0), writes=(onesb,))
        E('dve', lambda: nc.vector.memset(onesf.t[:], 1.0), writes=(onesf,))
        E('dve', lambda: nc.vector.memset(negbig.t[:], -1e30), writes=(negbig,))
        E('dve', lambda: nc.vector.memset(rmask.t[:], 1.0), writes=(rmask,))
        for c in range(16):
            E('dve', lambda c=c: nc.vector.memset(rmask.t[:, c * 32:c * 32 + 1], 0.0), writes=(rmask,))
        E('dve', lambda: nc.vector.memset(vext.t[:, :, 256:257], 1.0), writes=(vext,))

        def V(l, off, j=0, n=1):
            return vecs.t[:, l * V_PER + off + j: l * V_PER + off + j + n]

        lbe = k.sb("lbe", [128, NL, 16], F32); lbm = k.sb("lbm", [128, 16], F32); lbs = k.sb("lbs", [128, 16], F32)
        E('dve', lambda: nc.vector.tensor_copy(out=lbm.t[:], in_=V(0, V_LBR, 0, 16)), reads=(vecs,), writes=(lbm,))
        for l in range(1, NL):
            E('dve', lambda l=l: nc.vector.tensor_tensor(out=lbm.t[:], in0=lbm.t[:], in1=V(l, V_LBR, 0, 16), op=ALU.max), reads=(vecs, lbm), writes=(lbm,))
        for l in range(NL):
            E('dve', lambda l=l: nc.vector.tensor_tensor(out=lbe.t[:, l, :], in0=V(l, V_LBR, 0, 16), in1=lbm.t[:], op=ALU.subtract), reads=(vecs, lbm), writes=(lbe,))
            E('act', lambda l=l: nc.scalar.activation(out=lbe.t[:, l, :], in_=lbe.t[:, l, :], func=AF.Exp), reads=(lbe,), writes=(lbe,))
        E('dve', lambda: nc.vector.tensor_copy(out=lbs.t[:], in_=lbe.t[:, 0, :]), reads=(lbe,), writes=(lbs,))
        for l in range(1, NL):
            E('dve', lambda l=l: nc.vector.tensor_tensor(out=lbs.t[:], in0=lbs.t[:], in1=lbe.t[:, l, :], op=ALU.add), reads=(lbe, lbs), writes=(lbs,))
        E('dve', lambda: nc.vector.reciprocal(out=lbs.t[:], in_=lbs.t[:]), reads=(lbs,), writes=(lbs,))
        for l in range(NL):
            E('dve', lambda l=l: nc.vector.tensor_tensor(out=lbe.t[:, l, :], in0=lbe.t[:, l, :], in1=lbs.t[:], op=ALU.mult), reads=(lbe, lbs), writes=(lbe,))
        E('dve', lambda: nc.vector.memset(lbt.t[:, 0, :], 0.0), writes=(lbt,))
        for l in range(1, NL):
            E('dve', lambda l=l: nc.vector.tensor_tensor(out=lbt.t[:, l, :], in0=lbt.t[:, l - 1, :], in1=lbe.t[:, l, :], op=ALU.add), reads=(lbe, lbt), writes=(lbt,))
        E('dve', lambda: nc.vector.tensor_scalar(out=omlt.t[:], in0=lbt.t[:], scalar1=-1.0, scalar2=1.0, op0=ALU.mult, op1=ALU.add), reads=(lbt,), writes=(omlt,))
        E('dve', lambda: nc.vector.tensor_scalar(out=nomlt.t[:], in0=omlt.t[:], scalar1=-1.0, scalar2=None, op0=ALU.mult), reads=(omlt,), writes=(nomlt,))

        csf = k.sb("csf", [128, KC, 2], F32); csb = k.sb("csb", [128, KC, 2], BF16)
        spdma(csf.t[:], cvec, writes=(csf,))
        E('act', lambda: nc.scalar.activation(out=csb.t[:], in_=csf.t[:], func=AF.Silu), reads=(csf,), writes=(csb,))
        for l in range(NL):
            for jb in range(24):
                s = wnext(("ada_w", l, 0, ((jb * 512, 512),), False))
                for jj in range(4):
                    j = jb * 4 + jj
                    for kc in range(KC):
                        E('pe', lambda kc=kc, jj=jj, j=j, s=s: nc.tensor.matmul(PA.t[:, 2 * j:2 * j + 2], lhsT=s.t[:, kc, jj * 128:(jj + 1) * 128],
                                                                                rhs=csb.t[:, kc, :], start=(kc == 0), stop=(kc == KC - 1)),
                          reads=(s, csb), writes=(PA,))
            for sq in range(2):
                pav = PA.t[:, 0:192].rearrange("p (j s) -> p j s", s=2)[:, :, sq]
                E('dve', lambda l=l, sq=sq, pav=pav: nc.vector.tensor_tensor(out=mods.t[:, l, :, sq], in0=pav, in1=V(l, V_ADAB, 0, 96), op=ALU.add),
                  reads=(PA, vecs), writes=(mods,))
                for which, (goff, scoff) in enumerate(((V_N1G, 16), (V_N2G, 64))):
                    E('dve', lambda l=l, sq=sq, which=which, goff=goff, scoff=scoff: nc.vector.scalar_tensor_tensor(
                        out=A12.t[:, l, which, :, sq], in0=mods.t[:, l, scoff:scoff + 16, sq], scalar=1.0, in1=V(l, goff, 0, 16),
                        op0=ALU.add, op1=ALU.mult), reads=(mods, vecs), writes=(A12,))

        def modnorm(l, which, sq, T, shoff):
            for kc in range(KC):
                E('act', lambda kc=kc: nc.scalar.activation(out=sqb.t[:, 0:T], in_=xT.t[:, kc, 0:T], func=AF.Square), reads=(xT,), writes=(sqb,))
                E('pe', lambda kc=kc: nc.tensor.matmul(PA.t[:, 0:T], lhsT=onesb.t[:, :], rhs=sqb.t[:, 0:T], start=(kc == 0), stop=(kc == KC - 1)),
                  reads=(onesb, sqb), writes=(PA,))
            E('act', lambda: nc.scalar.activation(out=rstd.t[:, 0:T], in_=PA.t[:, 0:T], func=AF.Sqrt, scale=1.0 / D, bias=epsc), reads=(PA, smc), writes=(rstd,))
            E('dve', lambda: nc.vector.reciprocal(out=rstd.t[:, 0:T], in_=rstd.t[:, 0:T]), reads=(rstd,), writes=(rstd,))
            for kc in range(KC):
                E('dve', lambda kc=kc: nc.vector.tensor_tensor(out=tmpf.t[:, 0:T], in0=xT.t[:, kc, 0:T], in1=rstd.t[:, 0:T], op=ALU.mult), reads=(xT, rstd), writes=(tmpf,))
                E('act', lambda kc=kc: nc.scalar.activation(out=hT.t[:, kc, 0:T], in_=tmpf.t[:, 0:T], func=AF.Identity,
                                                            scale=A12.t[:, l, which, kc, sq:sq + 1], bias=mods.t[:, l, shoff + kc, sq:sq + 1]),
                  reads=(tmpf, A12, mods), writes=(hT,))

        smc = k.sb("smc", [128, 4], F32)
        E('dve', lambda: nc.vector.memset(smc.t[:, 0:1], EPS), writes=(smc,))
        E('dve', lambda: nc.vector.memset(smc.t[:, 1:2], 1.0), writes=(smc,))
        E('dve', lambda: nc.vector.memset(smc.t[:, 2:3], 0.0), writes=(smc,))
        epsc = smc.t[:, 0:1]; onec = smc.t[:, 1:2]

        def mlstm_gates(l, T, L, nch, gs):
            for c in range(nch):
                for kc in range(KC):
                    E('pe', lambda kc=kc, c=c: nc.tensor.matmul(PT.t[0:L, 0:16], lhsT=hT.t[:, kc, c * L:(c + 1) * L], rhs=gs.t[:, kc, 0:16], start=(kc == 0), stop=False),
                      reads=(hT, gs), writes=(PT,))
                E('pe', lambda: nc.tensor.matmul(PT.t[0:L, 0:16], lhsT=onesb.t[0:1, 0:L], rhs=gs.t[0:1, KC, 0:16], start=False, stop=True), reads=(onesb, gs), writes=(PT,))
                E('dve', lambda c=c: nc.vector.tensor_copy(out=igc.t[0:L, c, :], in_=PT.t[0:L, 0:8]), reads=(PT,), writes=(igc,))
                E('act', lambda c=c: nc.scalar.activation(out=gtmp.t[0:L, :], in_=PT.t[0:L, 8:16], func=AF.Exp, scale=-1.0), reads=(PT,), writes=(gtmp,))
                E('act', lambda c=c: nc.scalar.activation(out=nlf.t[0:L, c, :], in_=gtmp.t[0:L, :], func=AF.Ln, bias=onec[0:L, :]), reads=(gtmp, smc), writes=(nlf,))
                E('pe', lambda c=c: nc.tensor.matmul(PA.t[0:L, 0:8], lhsT=tri[0:L, 0:L], rhs=nlf.t[0:L, c, :], start=True, stop=True), reads=(cst, nlf), writes=(PA,))
                E('dve', lambda c=c: nc.vector.tensor_copy(out=nbc.t[0:L, c, :], in_=PA.t[0:L, 0:8]), reads=(PA,), writes=(nbc,))
                E('dve', lambda c=c: nc.vector.tensor_tensor(out=acadj.t[0:L, c, :], in0=igc.t[0:L, c, :], in1=nbc.t[0:L, c, :], op=ALU.add), reads=(igc, nbc), writes=(acadj,))
                E('dve', lambda c=c: nc.vector.tensor_scalar(out=acadj.t[0:L, c, :], in0=acadj.t[0:L, c, :], scalar1=LN16, scalar2=None, op0=ALU.add), reads=(acadj,), writes=(acadj,))

        def mlstm_head(l, hd, T, L, nch, g, first, sq):
            if first and g == "p":
                E('dve', lambda: nc.vector.memset(Cf.t[:], 0.0), writes=(Cf,))
            else:
                srcC = (Cs_in if first else O[g]["C"])[l, hd].rearrange("(dc p) v -> p dc v", p=128)
                srcn = (ns_in if first else O[g]["n"])[l, hd]
                spdma(Cf.t[:, :, 0:256], srcC, writes=(Cf,))
                spdma(Cf.t[:, :, 256], srcn, writes=(Cf,))
            E('act', lambda: nc.scalar.activation(out=Cb.t[:], in_=Cf.t[:], func=AF.Copy), reads=(Cf,), writes=(Cb,))
            s = wnext(("w_in", l, 0, ((hd * 256, 256), (2048 + hd * 256, 256)), False))
            for b4 in range(4):
                fb = (hd * 2 + b4) if b4 < 2 else (16 + hd * 2 + (b4 - 2))
                ps = nextpd()
                fm_group(ps, s, b4, hT, T)
                E('act', lambda b4=b4, fb=fb, ps=ps: nc.scalar.activation(out=uqk.t[:, b4, 3:3 + T], in_=ps.t[:, 0:T], func=AF.Identity, bias=V(l, V_BIN, fb)),
                  reads=(ps, vecs), writes=(uqk,))
                E('dve', lambda b4=b4, fb=fb: nc.vector.tensor_copy(out=uqk.t[:, b4, 0:3], in_=hist.t[:, l, fb, :]), reads=(hist,), writes=(uqk,))
                E('dve', lambda b4=b4, fb=fb: nc.vector.tensor_scalar(out=cacc.t[:, 0:T], in0=uqk.t[:, b4, 0:T], scalar1=V(l, V_CW, fb), scalar2=V(l, V_CB, fb),
                                                                      op0=ALU.mult, op1=ALU.add), reads=(uqk, vecs), writes=(cacc,))
                for j in range(1, 4):
                    E('dve', lambda b4=b4, fb=fb, j=j: nc.vector.scalar_tensor_tensor(out=cacc.t[:, 0:T], in0=uqk.t[:, b4, j:j + T], scalar=V(l, V_CW, j * 32 + fb),
                                                                                      in1=cacc.t[:, 0:T], op0=ALU.mult, op1=ALU.add), reads=(uqk, vecs, cacc), writes=(cacc,))
                E('dve', lambda b4=b4, fb=fb: nc.vector.tensor_copy(out=hist.t[:, l, fb, :], in_=uqk.t[:, b4, T:T + 3]), reads=(uqk,), writes=(hist,))
                E('act', lambda b4=b4: nc.scalar.activation(out=qkT.t[:, b4, 0:T], in_=cacc.t[:, 0:T], func=AF.Silu), reads=(cacc,), writes=(qkT,))
            s = wnext(("w_in", l, 0, ((4096 + hd * 256, 256), (6144 + hd * 256, 256)), True))
            for c in range(nch):
                for kc in range(KC):
                    E('pe', lambda kc=kc, c=c: nc.tensor.matmul(PT.t[0:L, 0:512], lhsT=hT.t[:, kc, c * L:(c + 1) * L], rhs=s.t[:, kc, 0:512], start=(kc == 0), stop=False),
                      reads=(hT, s), writes=(PT,))
                E('pe', lambda: nc.tensor.matmul(PT.t[0:L, 0:512], lhsT=onesb.t[0:1, 0:L], rhs=s.t[0:1, KC, 0:512], start=False, stop=True), reads=(onesb, s), writes=(PT,))
                E('act', lambda c=c: nc.scalar.activation(out=vext.t[0:L, c, 0:256], in_=PT.t[0:L, 0:256], func=AF.Copy), reads=(PT,), writes=(vext,))
                E('act', lambda c=c: nc.scalar.activation(out=sigo.t[0:L, c, :], in_=PT.t[0:L, 256:512], func=AF.Sigmoid), reads=(PT,), writes=(sigo,))
            m0 = mst.t[:, l * 8 + hd:l * 8 + hd + 1]
            for c in range(nch):
                cs_ = slice(c * L, (c + 1) * L)
                E('dve', lambda c=c: nc.vector.tensor_scalar(out=igrep.t[0:L, :], in0=onesf.t[0:L, :], scalar1=igc.t[0:L, c, hd:hd + 1], scalar2=None, op0=ALU.mult),
                  reads=(onesf, igc), writes=(igrep,))
                E('dve', lambda c=c: nc.vector.tensor_scalar(out=nlfrep.t[0:L, :], in0=onesf.t[0:L, :], scalar1=nlf.t[0:L, c, hd:hd + 1], scalar2=None, op0=ALU.mult),
                  reads=(onesf, nlf), writes=(nlfrep,))
                E('pe', lambda: nc.tensor.matmul(PA.t[:, 0:L], lhsT=igrep.t[0:L, :], rhs=ident[0:L, 0:L], start=True, stop=False), reads=(igrep, cst), writes=(PA,))
                E('pe', lambda: nc.tensor.matmul(PA.t[:, 0:L], lhsT=nlfrep.t[0:L, :], rhs=tri[0:L, 0:L], start=False, stop=True), reads=(nlfrep, cst), writes=(PA,))
                E('pe', lambda: nc.tensor.matmul(PA.t[:, 256:257], lhsT=nlfrep.t[0:L, :], rhs=onesf.t[0:L, 0:1], start=True, stop=True), reads=(nlfrep, onesf), writes=(PA,))
                E('dve', lambda: nc.vector.tensor_tensor_scan(out=grow.t[:, 0:L], data0=PA.t[:, 0:L], data1=negbig.t[:, 0:L], initial=m0, op0=ALU.max, op1=ALU.max),
                  reads=(PA, negbig, mst), writes=(grow,))
                E('dve', lambda: nc.vector.scalar_tensor_tensor(out=junk.t[0:L, 0:L], in0=grow.t[0:L, 0:L], scalar=1.0, in1=ident[0:L, 0:L], op0=ALU.mult, op1=ALU.mult,
                                                                accum_out=sm.t[0:L, 0:1]), reads=(grow, cst), writes=(junk, sm))
                E('dve', lambda: nc.vector.tensor_tensor(out=dtmp.t[0:L, 0:L], in0=maskb[0:L, 0:L], in1=grow.t[0:L, 0:L], op=ALU.subtract), reads=(cst, grow), writes=(dtmp,))
                E('act', lambda c=c: nc.scalar.activation(out=DT.t[0:L, 0:L], in_=dtmp.t[0:L, 0:L], func=AF.Exp, bias=acadj.t[0:L, c, hd:hd + 1]), reads=(dtmp, acadj), writes=(DT,))
                for dc in range(2):
                    E('pe', lambda dc=dc: nc.tensor.matmul(PS.t[0:L, 0:L], lhsT=qkT.t[:, 2 + dc, cs_], rhs=qkT.t[:, dc, cs_], start=(dc == 0), stop=(dc == 1)), reads=(qkT,), writes=(PS,))
                E('dve', lambda: nc.vector.tensor_tensor(out=STb.t[0:L, 0:L], in0=PS.t[0:L, 0:L], in1=DT.t[0:L, 0:L], op=ALU.mult), reads=(PS, DT), writes=(STb,))
                E('act', lambda: nc.scalar.activation(out=wrow.t[:, 0:L], in_=grow.t[:, 0:L], func=AF.Exp, scale=-1.0, bias=m0), reads=(grow, mst), writes=(wrow,))
                for dc in range(2):
                    E('dve', lambda dc=dc: nc.vector.tensor_tensor(out=qsc.t[:, dc, 0:L], in0=qkT.t[:, dc, cs_], in1=wrow.t[:, 0:L], op=ALU.mult), reads=(qkT, wrow), writes=(qsc,))
                E('pe', lambda c=c: nc.tensor.matmul(PN.t[0:L, 0:257], lhsT=STb.t[0:L, 0:L], rhs=vext.t[0:L, c, :], start=True, stop=False), reads=(STb, vext), writes=(PN,))
                for dc in range(2):
                    E('pe', lambda dc=dc: nc.tensor.matmul(PN.t[0:L, 0:257], lhsT=qsc.t[:, dc, 0:L], rhs=Cb.t[:, dc, :], start=False, stop=(dc == 1)), reads=(qsc, Cb), writes=(PN,))
                E('dve', lambda c=c: nc.vector.tensor_tensor(out=sm.t[0:L, 1:2], in0=nbc.t[0:L, c, hd:hd + 1], in1=sm.t[0:L, 0:1], op=ALU.subtract), reads=(nbc, sm), writes=(sm,))
                E('act', lambda: nc.scalar.activation(out=sm.t[0:L, 2:3], in_=sm.t[0:L, 1:2], func=AF.Exp), reads=(sm,), writes=(sm,))
                E('act', lambda: nc.scalar.activation(out=sm.t[0:L, 3:4], in_=PN.t[0:L, 256:257], func=AF.Abs), reads=(PN,), writes=(sm,))
                E('dve', lambda: nc.vector.tensor_tensor(out=sm.t[0:L, 3:4], in0=sm.t[0:L, 3:4], in1=sm.t[0:L, 2:3], op=ALU.max), reads=(sm,), writes=(sm,))
                E('act', lambda: nc.scalar.activation(out=junk.t[0:L, 0:256], in_=PN.t[0:L, 0:256], func=AF.Square, accum_out=sm.t[0:L, 4:5]), reads=(PN,), writes=(junk, sm))
                E('dve', lambda: nc.vector.scalar_tensor_tensor(out=sm.t[0:L, 5:6], in0=sm.t[0:L, 3:4], scalar=EPS, in1=sm.t[0:L, 3:4], op0=ALU.mult, op1=ALU.mult), reads=(sm,), writes=(sm,))
                E('dve', lambda: nc.vector.scalar_tensor_tensor(out=sm.t[0:L, 6:7], in0=sm.t[0:L, 4:5], scalar=1.0 / 256, in1=sm.t[0:L, 5:6], op0=ALU.mult, op1=ALU.add), reads=(sm,), writes=(sm,))
                E('act', lambda: nc.scalar.activation(out=sm.t[0:L, 6:7], in_=sm.t[0:L, 6:7], func=AF.Sqrt), reads=(sm,), writes=(sm,))
                E('dve', lambda: nc.vector.reciprocal(out=sm.t[0:L, 7:8], in_=sm.t[0:L, 6:7]), reads=(sm,), writes=(sm,))
                E('dve', lambda c=c: nc.vector.scalar_tensor_tensor(out=ytm.t[0:L, :], in0=PN.t[0:L, 0:256], scalar=sm.t[0:L, 7:8], in1=sigo.t[0:L, c, :], op0=ALU.mult, op1=ALU.mult),
                  reads=(PN, sm, sigo), writes=(ytm,))
                for vc in range(2):
                    E('pe', lambda vc=vc: nc.tensor.transpose(out=PX.t[:, vc * 128:vc * 128 + L], in_=ytm.t[0:L, vc * 128:(vc + 1) * 128], identity=identb.t[0:L, 0:L]), reads=(ytm, identb), writes=(PX,))
                    E('act', lambda vc=vc: nc.scalar.activation(out=yaT.t[:, hd * 2 + vc, cs_], in_=PX.t[:, vc * 128:vc * 128 + L], func=AF.Identity, scale=V(l, V_MAN, hd * 2 + vc)),
                      reads=(PX, vecs), writes=(yaT,))
                E('dve', lambda: nc.vector.tensor_scalar(out=sm.t[:, 8:9], in0=grow.t[:, L - 1:L], scalar1=-1.0, scalar2=None, op0=ALU.mult), reads=(grow,), writes=(sm,))
                E('act', lambda c=c: nc.scalar.activation(out=sm.t[0:L, 9:10], in_=acadj.t[0:L, c, hd:hd + 1], func=AF.Exp, bias=sm.t[0:L, 8:9]), reads=(acadj, sm), writes=(sm,))
                for dc in range(2):
                    E('pe', lambda dc=dc: nc.tensor.transpose(out=PX.t[0:L, 256 + dc * 128:256 + (dc + 1) * 128], in_=qkT.t[:, 2 + dc, cs_], identity=identb.t[:, :]), reads=(qkT, identb), writes=(PX,))
                E('act', lambda: nc.scalar.activation(out=ktm.t[0:L, :], in_=PX.t[0:L, 256:512], func=AF.Identity, scale=sm.t[0:L, 9:10]), reads=(PX, sm), writes=(ktm,))
                pcs = [PC, PT]
                for dc in range(2):
                    E('pe', lambda dc=dc, c=c: nc.tensor.matmul(pcs[dc].t[:, 0:257], lhsT=ktm.t[0:L, dc * 128:(dc + 1) * 128], rhs=vext.t[0:L, c, :], start=True, stop=True), reads=(ktm, vext), writes=(pcs[dc],))
                for dc in range(2):
                    E('dve', lambda dc=dc: nc.vector.scalar_tensor_tensor(out=Cf.t[:, dc, :], in0=Cf.t[:, dc, :], scalar=wrow.t[:, L - 1:L], in1=pcs[dc].t[:, 0:257], op0=ALU.mult, op1=ALU.add),
                      reads=(Cf, wrow, pcs[dc]), writes=(Cf,))
                E('act', lambda: nc.scalar.activation(out=Cb.t[:], in_=Cf.t[:], func=AF.Copy), reads=(Cf,), writes=(Cb,))
                E('dve', lambda: nc.vector.tensor_tensor(out=mst.t[:, l * 8 + hd:l * 8 + hd + 1], in0=grow.t[:, L - 1:L], in1=PA.t[:, 256:257], op=ALU.subtract), reads=(grow, PA), writes=(mst,))
            spdma(O[g]["C"][l, hd].rearrange("(dc p) v -> p dc v", p=128), Cf.t[:, :, 0:256], reads=(Cf,))
            spdma(O[g]["n"][l, hd], Cf.t[:, :, 256], reads=(Cf,))

        def hgrn_group(l, hp, T, Lh, nch, g, first, sq):
            s = wnext(("w_in", l, 0, ((8192 + hp * 256, 256), (10240 + hp * 256, 256)), False))
            for hh in range(2):
                ps = nextpd(); fm_group(ps, s, hh, hT, T)
                E('act', lambda hh=hh, ps=ps: nc.scalar.activation(out=uqk.t[:, hh, 0:T], in_=ps.t[:, 0:T], func=AF.Silu, bias=V(l, V_BIN, 64 + hp * 2 + hh)), reads=(ps, vecs), writes=(uqk,))
            for hh in range(2):
                ps = nextpd(); fm_group(ps, s, 2 + hh, hT, T)
                E('act', lambda hh=hh, ps=ps: nc.scalar.activation(out=uqk.t[:, 2 + hh, 0:T], in_=ps.t[:, 0:T], func=AF.Sigmoid, bias=V(l, V_BIN, 80 + hp * 2 + hh)), reads=(ps, vecs), writes=(uqk,))
            s = wnext(("w_in", l, 0, ((12288 + hp * 256, 256), (14336 + hp * 256, 256)), True))
            for c in range(nch):
                for kc in range(KC):
                    E('pe', lambda kc=kc, c=c: nc.tensor.matmul(PT.t[0:Lh, 0:256], lhsT=hT.t[:, kc, c * Lh:(c + 1) * Lh], rhs=s.t[:, kc, 0:256], start=(kc == 0), stop=False), reads=(hT, s), writes=(PT,))
                E('pe', lambda: nc.tensor.matmul(PT.t[0:Lh, 0:256], lhsT=onesb.t[0:1, 0:Lh], rhs=s.t[0:1, KC, 0:256], start=False, stop=True), reads=(onesb, s), writes=(PT,))
                E('act', lambda c=c: nc.scalar.activation(out=vtm.t[0:Lh, c, :], in_=PT.t[0:Lh, 0:256], func=AF.Copy), reads=(PT,), writes=(vtm,))
            for hh in range(2):
                ps = nextpd(); fm_group(ps, s, 2 + hh, hT, T)
                E('act', lambda hh=hh, ps=ps: nc.scalar.activation(out=qkT.t[:, hh, 0:T], in_=ps.t[:, 0:T], func=AF.Sigmoid, bias=V(l, V_BIN, 112 + hp * 2 + hh)), reads=(ps, vecs), writes=(qkT,))
            qth = qkT.t[:, 2, :]; kth = qkT.t[:, 3, :]
            lfh = cacc; kkh = tmpf
            for hh in range(2):
                hd = hp * 2 + hh
                if first and g == "p":
                    E('dve', lambda: nc.vector.memset(Sf.t[:], 0.0), writes=(Sf,))
                else:
                    spdma(Sf.t[:], (Ss_in if first else O[g]["S"])[l, hd], writes=(Sf,))
                E('act', lambda: nc.scalar.activation(out=Sb.t[:], in_=Sf.t[:], func=AF.Copy), reads=(Sf,), writes=(Sb,))
                E('act', lambda hh=hh, hd=hd: nc.scalar.activation(out=lfh.t[:, 0:T], in_=uqk.t[:, 2 + hh, 0:T], func=AF.Ln, scale=omlt.t[:, l, hd:hd + 1], bias=lbt.t[:, l, hd:hd + 1]),
                  reads=(uqk, omlt, lbt), writes=(lfh,))
                E('dve', lambda hh=hh, hd=hd: nc.vector.tensor_scalar(out=kkh.t[:, 0:T], in0=uqk.t[:, 2 + hh, 0:T], scalar1=nomlt.t[:, l, hd:hd + 1], scalar2=omlt.t[:, l, hd:hd + 1], op0=ALU.mult, op1=ALU.add),
                  reads=(uqk, nomlt, omlt), writes=(kkh,))
                E('dve', lambda: nc.vector.tensor_tensor_scan(out=bh.t[:, 0:T], data0=rmask.t[:, 0:T], data1=lfh.t[:, 0:T], initial=0.0, op0=ALU.mult, op1=ALU.add), reads=(rmask, lfh), writes=(bh,))
                E('act', lambda: nc.scalar.activation(out=ebh.t[:, 0:T], in_=bh.t[:, 0:T], func=AF.Exp), reads=(bh,), writes=(ebh,))
                E('act', lambda: nc.scalar.activation(out=enbh.t[:, 0:T], in_=bh.t[:, 0:T], func=AF.Exp, scale=-1.0), reads=(bh,), writes=(enbh,))
                E('dve', lambda hh=hh: nc.vector.tensor_tensor(out=qth[:, 0:T], in0=uqk.t[:, hh, 0:T], in1=ebh.t[:, 0:T], op=ALU.mult), reads=(uqk, ebh), writes=(qkT,))
                E('dve', lambda: nc.vector.tensor_tensor(out=kth[:, 0:T], in0=kkh.t[:, 0:T], in1=enbh.t[:, 0:T], op=ALU.mult), reads=(kkh, enbh), writes=(qkT,))
                for c in range(nch):
                    cs_ = slice(c * Lh, (c + 1) * Lh)
                    E('pe', lambda cs_=cs_: nc.tensor.matmul(PS.t[0:Lh, cs_], lhsT=kth[:, cs_], rhs=qth[:, cs_], start=True, stop=True), reads=(qkT,), writes=(PS,))
                E('dve', lambda: nc.vector.tensor_tensor(out=Abf.t[0:Lh, 0:T], in0=PS.t[0:Lh, 0:T], in1=tri8[0:Lh, 0:T], op=ALU.mult), reads=(PS, cst), writes=(Abf,))
                for c in range(nch):
                    cs_ = slice(c * Lh, (c + 1) * Lh)
                    ce = (c + 1) * Lh - 1
                    E('dve', lambda cs_=cs_, ce=ce: nc.vector.tensor_scalar(out=khT.t[:, cs_], in0=kth[:, cs_], scalar1=ebh.t[:, ce:ce + 1], scalar2=None, op0=ALU.mult), reads=(qkT, ebh), writes=(khT,))
                    E('pe', lambda cs_=cs_, c=c: nc.tensor.transpose(out=PX.t[0:Lh, (c % 8) * 128:(c % 8 + 1) * 128], in_=khT.t[:, cs_], identity=identb.t[:, :]), reads=(khT, identb), writes=(PX,))
                    if c % 8 == 7 or c == nch - 1:
                        c0 = (c // 8) * 8
                        nn = c - c0 + 1
                        E('act', lambda c0=c0, nn=nn: nc.scalar.activation(out=khtm.t[0:Lh, c0 * 128:(c0 + nn) * 128], in_=PX.t[0:Lh, 0:nn * 128], func=AF.Copy), reads=(PX,), writes=(khtm,))
                ubanks = [P0, P1, PC, PT]
                for c in range(nch):
                    vsl = vtm.t[0:Lh, c, hh * 128:(hh + 1) * 128]
                    ub = ubanks[c // 4]
                    E('pe', lambda c=c, vsl=vsl, ub=ub: nc.tensor.matmul(ub.t[:, (c % 4) * 128:(c % 4 + 1) * 128], lhsT=khtm.t[0:Lh, c * 128:(c + 1) * 128], rhs=vsl, start=True, stop=True), reads=(khtm, vtm), writes=(ub,))
                for c in range(nch):
                    ce = (c + 1) * Lh - 1
                    ub = ubanks[c // 4]
                    E('dve', lambda c=c, ce=ce, ub=ub: nc.vector.scalar_tensor_tensor(out=Sf.t[:], in0=Sf.t[:], scalar=ebh.t[:, ce:ce + 1], in1=ub.t[:, (c % 4) * 128:(c % 4 + 1) * 128], op0=ALU.mult, op1=ALU.add), reads=(Sf, ebh, ub), writes=(Sf,))
                    if c < nch - 1:
                        E('act', lambda c=c: nc.scalar.activation(out=Sbc[c + 1].t, in_=Sf.t[:], func=AF.Copy), reads=(Sf,), writes=(Sbc[c + 1],))
                for c in range(nch):
                    cs_ = slice(c * Lh, (c + 1) * Lh)
                    vsl = vtm.t[0:Lh, c, hh * 128:(hh + 1) * 128]
                    sbc = Sb if c == 0 else Sbc[c]
                    sbap = Sb.t[:, :] if c == 0 else Sbc[c].t
                    E('pe', lambda cs_=cs_, vsl=vsl: nc.tensor.matmul(PN.t[:, cs_], lhsT=vsl, rhs=Abf.t[0:Lh, cs_], start=True, stop=False), reads=(vtm, Abf), writes=(PN,))
                    E('pe', lambda cs_=cs_, sbap=sbap: nc.tensor.matmul(PN.t[:, cs_], lhsT=sbap, rhs=qth[:, cs_], start=False, stop=True), reads=(sbc, qkT), writes=(PN,))
                spdma(O[g]["S"][l, hd], Sf.t[:], reads=(Sf,))
                E('act', lambda: nc.scalar.activation(out=sqb.t[:, 0:T], in_=PN.t[:, 0:T], func=AF.Square), reads=(PN,), writes=(sqb,))
                E('pe', lambda: nc.tensor.matmul(PA.t[:, 0:T], lhsT=onesb.t[:, :], rhs=sqb.t[:, 0:T], start=True, stop=True), reads=(onesb, sqb), writes=(PA,))
                E('act', lambda: nc.scalar.activation(out=rstd.t[:, 0:T], in_=PA.t[:, 0:T], func=AF.Sqrt, scale=1.0 / 128, bias=epsc), reads=(PA, smc), writes=(rstd,))
                E('dve', lambda: nc.vector.reciprocal(out=rstd.t[:, 0:T], in_=rstd.t[:, 0:T]), reads=(rstd,), writes=(rstd,))
                E('dve', lambda hd=hd: nc.vector.scalar_tensor_tensor(out=tmpf.t[:, 0:T], in0=PN.t[:, 0:T], scalar=V(l, V_HBN, hd), in1=rstd.t[:, 0:T], op0=ALU.mult, op1=ALU.mult), reads=(PN, vecs, rstd), writes=(tmpf,))
                E('dve', lambda hd=hd, hh=hh: nc.vector.tensor_tensor(out=ybT.t[:, hd, 0:T], in0=tmpf.t[:, 0:T], in1=qkT.t[:, hh, 0:T], op=ALU.mult), reads=(tmpf, qkT), writes=(ybT,))

        def layer(l, T, g, first, sq):
            Lm = min(128, T); Lh = min(32, T)
            modnorm(l, 0, sq, T, 0)
            gs = wnext(("w_in", l, 0, ((20480, 16),), True))
            mlstm_gates(l, T, Lm, T // Lm, gs)
            for hd in range(8):
                mlstm_head(l, hd, T, Lm, T // Lm, g, first, sq)
            for hp in range(8):
                hgrn_group(l, hp, T, Lh, T // Lh, g, first, sq)
            for jb in range(4):
                s = wnext(("w_in", l, 0, ((16384 + jb * 512, 512),), False))
                for jj in range(4):
                    ps = nextpd(); fm_group(ps, s, jj, hT, T)
                    E('act', lambda jj=jj, ps=ps: nc.scalar.activation(out=sg4.t[:, jj, 0:T], in_=ps.t[:, 0:T], func=AF.Sigmoid, bias=V(l, V_BIN, 128 + jb * 4 + jj)), reads=(ps, vecs), writes=(sg4,))
                s = wnext(("w_br_a", l, 0, ((jb * 512, 512),), False))
                for jj in range(4):
                    ps = nextpd(); fm_group(ps, s, jj, yaT, T)
                    E('dve', lambda jj=jj, ps=ps: nc.vector.tensor_tensor(out=uqk.t[:, jj, 0:T], in0=ps.t[:, 0:T], in1=sg4.t[:, jj, 0:T], op=ALU.mult), reads=(ps, sg4), writes=(uqk,))
                s = wnext(("w_in", l, 0, ((18432 + jb * 512, 512),), False))
                for jj in range(4):
                    ps = nextpd(); fm_group(ps, s, jj, hT, T)
                    E('act', lambda jj=jj, ps=ps: nc.scalar.activation(out=sg4.t[:, jj, 0:T], in_=ps.t[:, 0:T], func=AF.Sigmoid, bias=V(l, V_BIN, 144 + jb * 4 + jj)), reads=(ps, vecs), writes=(sg4,))
                s = wnext(("w_br_b", l, 0, ((jb * 512, 512),), False))
                for jj in range(4):
                    ps = nextpd(); fm_group(ps, s, jj, ybT, T)
                    E('dve', lambda jj=jj, ps=ps: nc.vector.tensor_tensor(out=tmpf.t[:, 0:T], in0=ps.t[:, 0:T], in1=sg4.t[:, jj, 0:T], op=ALU.mult), reads=(ps, sg4), writes=(tmpf,))
                    E('dve', lambda jj=jj: nc.vector.tensor_tensor(out=uT.t[:, jb * 4 + jj, 0:T], in0=tmpf.t[:, 0:T], in1=uqk.t[:, jj, 0:T], op=ALU.add), reads=(tmpf, uqk), writes=(uT,))
            for jb in range(4):
                s = wnext(("w_o", l, 0, ((jb * 512, 512),), False))
                for jj in range(4):
                    j = jb * 4 + jj
                    ps = nextpd(); fm_group(ps, s, jj, uT, T)
                    E('dve', lambda j=j, ps=ps: nc.vector.scalar_tensor_tensor(out=xT.t[:, j, 0:T], in0=ps.t[:, 0:T], scalar=mods.t[:, l, 32 + j, sq:sq + 1], in1=xT.t[:, j, 0:T], op0=ALU.mult, op1=ALU.add),
                      reads=(ps, mods, xT), writes=(xT,))
            modnorm(l, 1, sq, T, 48)
            for qd in range(4):
                for ub in range(4):
                    s = wnext(("w_up", l, 0, ((qd * 2048 + ub * 512, 512),), False))
                    for jj in range(4):
                        ps = nextpd(); fm_group(ps, s, jj, hT, T)
                        E('act', lambda ps=ps: nc.scalar.activation(out=sqb.t[:, 0:T], in_=ps.t[:, 0:T], func=AF.Relu), reads=(ps,), writes=(sqb,))
                        E('dve', lambda ub=ub, jj=jj: nc.vector.tensor_tensor(out=uT.t[:, ub * 4 + jj, 0:T], in0=sqb.t[:, 0:T], in1=sqb.t[:, 0:T], op=ALU.mult), reads=(sqb,), writes=(uT,))
                for ob in range(4):
                    s = wnext(("w_down", l, qd * 2048, ((ob * 512, 512),), False))
                    for jj in range(4):
                        j = ob * 4 + jj
                        ps = nextpd(); fm_group(ps, s, jj, uT, T)
                        E('dve', lambda j=j, ps=ps: nc.vector.scalar_tensor_tensor(out=xT.t[:, j, 0:T], in0=ps.t[:, 0:T], scalar=mods.t[:, l, 80 + j, sq:sq + 1], in1=xT.t[:, j, 0:T], op0=ALU.mult, op1=ALU.add),
                          reads=(ps, mods, xT), writes=(xT,))

        def final_norm(T):
            for kc in range(KC):
                E('act', lambda kc=kc: nc.scalar.activation(out=sqb.t[:, 0:T], in_=xT.t[:, kc, 0:T], func=AF.Square), reads=(xT,), writes=(sqb,))
                E('pe', lambda kc=kc: nc.tensor.matmul(PA.t[:, 0:T], lhsT=onesb.t[:, :], rhs=sqb.t[:, 0:T], start=(kc == 0), stop=(kc == KC - 1)), reads=(onesb, sqb), writes=(PA,))
            E('act', lambda: nc.scalar.activation(out=rstd.t[:, 0:T], in_=PA.t[:, 0:T], func=AF.Sqrt, scale=1.0 / D, bias=epsc), reads=(PA, smc), writes=(rstd,))
            E('dve', lambda: nc.vector.reciprocal(out=rstd.t[:, 0:T], in_=rstd.t[:, 0:T]), reads=(rstd,), writes=(rstd,))
            fg0 = NL * V_PER
            for kc in range(KC):
                E('dve', lambda kc=kc: nc.vector.scalar_tensor_tensor(out=xT.t[:, kc, 0:T], in0=xT.t[:, kc, 0:T], scalar=vecs.t[:, fg0 + kc:fg0 + kc + 1], in1=rstd.t[:, 0:T], op0=ALU.mult, op1=ALU.mult),
                  reads=(xT, vecs, rstd), writes=(xT,))

        groups = []
        if NTP > 0:
            groups.append(("p", NTP, TP, 0))
        if WITH_S:
            groups.append(("s", 1, 16, 1))
        for (g, ntiles, T, sq) in groups:
            if g == "p":
                E('dve', lambda: nc.vector.memset(hist.t[:], 0.0), writes=(hist,))
                E('dve', lambda: nc.vector.memset(mst.t[:], 0.0), writes=(mst,))
            else:
                spdma(hist.t[:], convs_in, writes=(hist,))
                spdma(mst.t[:], msamp, writes=(mst,))
            for ti in range(ntiles):
                src = (xp[:, ti * T:(ti + 1) * T] if g == "p" else xs).rearrange("(kc p) t -> p kc t", p=128)
                spdma(xT.t[:, :, 0:T], src, writes=(xT,))
                for l in range(NL):
                    layer(l, T, g, ti == 0, sq)
                final_norm(T)
                dst = (yp[:, ti * T:(ti + 1) * T] if g == "p" else ys).rearrange("(kc p) t -> p kc t", p=128)
                spdma(dst, xT.t[:, :, 0:T], reads=(xT,))
            spdma(O[g]["conv"], hist.t[:], reads=(hist,))
            spdma(O[g]["m"], mst.t[0:1, :], reads=(mst,))
        for i in range(SPN):
            k.wait('sp', sp_last[i])
        assert wstate['used'] == len(PL), (wstate, len(PL))
    return nc


def _fm(v):
    return np.ascontiguousarray(v.reshape(-1, 128).T)


def _consts():
    s = np.arange(128)
    tri = (s[:, None] <= s[None, :]).astype(np.float32)
    ident = np.eye(128, dtype=np.float32)
    maskb = np.where(s[:, None] <= s[None, :], 0.0, -1e4).astype(np.float32)
    t32 = np.zeros((128, 32), np.float32)
    t32[:32] = tri[:32, :32]
    return np.ascontiguousarray(np.concatenate([tri, ident, maskb, np.tile(t32, (1, 16))], axis=1))


def make_inputs(NL, NTP, TP, inp, core):
    bp = core % 4
    f = lambda a: np.ascontiguousarray(np.asarray(a, dtype=np.float32))
    SEQP = max(NTP, 1) * TP
    m = {}
    m["xp"] = f(inp["x_prompt"][bp, :SEQP].T)
    m["xs"] = f(inp["x_sample"][core].T)
    cv = np.stack([_fm(inp["c_prompt"][bp]), _fm(inp["c_sample"][core])], axis=-1)
    m["cvec"] = f(cv)
    cols = []
    for l in range(NL):
        cols += [_fm(inp["norm1_g"][l]), _fm(inp["norm2_g"][l]), _fm(inp["ada_b"][l]), _fm(inp["b_in"][l, :20480])]
        cols += [_fm(inp["conv_w"][l, j]) for j in range(4)]
        cols += [_fm(inp["conv_b"][l]), _fm(inp["ma_norm"][l]), _fm(inp["hb_norm"][l]), _fm(inp["hgrn_lb_raw"][l])]
    cols.append(_fm(inp["final_g"]))
    m["vecs"] = f(np.concatenate(cols, axis=1))
    m["consts"] = _consts()
    m["msamp"] = f(np.broadcast_to(inp["state_mlstm_m"][:NL, core].reshape(1, NL * 8), (128, NL * 8)))
    cc = inp["cache_conv"][:NL, core]
    m["convs_in"] = f(cc.reshape(NL, 3, 32, 128).transpose(3, 0, 2, 1))
    m["ns_in"] = f(inp["state_mlstm_n"][:NL, core].reshape(NL, 8, 2, 128).transpose(0, 1, 3, 2))
    m["Cs_in"] = f(inp["state_mlstm_C"][:NL, core])
    m["Ss_in"] = f(inp["state_hgrn"][:NL, core])
    for n in ("w_in", "ada_w", "w_br_a", "w_br_b", "w_o", "w_up", "w_down", "b_in"):
        m[n] = f(inp[n][:NL])
    return m


def assemble(NL, NTP, TP, results, WITH_S=True):
    SEQP = NTP * TP
    outs = {}
    yp = np.stack([results[b]["yp"].T for b in range(4)], axis=0)
    ys = np.stack([results[c]["ys"].T for c in range(8)], axis=0)

    def grp(g, cores):
        conv = np.stack([results[c]["conv" + g].transpose(1, 3, 2, 0).reshape(NL, 3, 4096) for c in cores], axis=1)
        C = np.stack([results[c]["C" + g] for c in cores], axis=1)
        n = np.stack([results[c]["n" + g].transpose(0, 1, 3, 2).reshape(NL, 8, 256) for c in cores], axis=1)
        mm = np.stack([results[c]["m" + g].reshape(NL, 8) for c in cores], axis=1)
        S = np.stack([results[c]["S" + g] for c in cores], axis=1)
        return [np.ascontiguousarray(a, dtype=np.float32) for a in (conv, C, n, mm, S)]

    return tuple([np.ascontiguousarray(yp, dtype=np.float32), np.ascontiguousarray(ys, dtype=np.float32)] + grp("p", range(4)) + grp("s", range(8)))


def kernel(**inputs):
    NL, NTP, TP = 4, 8, 512
    inp = {k_: np.asarray(v) for k_, v in inputs.items()}
    nc = build(NL, NTP, True, TP)
    in_maps = [make_inputs(NL, NTP, TP, inp, c) for c in range(8)]
    res = run_bass_kernel_spmd(nc, in_maps, core_ids=list(range(8)))
    return assemble(NL, NTP, TP, res.results)
```

```python
import math
from contextlib import ExitStack
import numpy as np
import concourse.bass as bass
import concourse.mybir as mybir
from concourse.bass_utils import run_bass_kernel_spmd

F32 = mybir.dt.float32
BF16 = mybir.dt.bfloat16
AF = mybir.ActivationFunctionType
ALU = mybir.AluOpType

D = 2048
KC = 16
N_IN = 20496
EPS = 1e-6
NSLOT = 2
LN16 = math.log(1.0 / 16.0)
V_N1G, V_N2G, V_ADAB, V_BIN, V_CW, V_CB, V_MAN, V_HBN, V_LBR = 0, 16, 32, 128, 288, 416, 448, 464, 480
V_PER = 496


class Buf:
    def __init__(self, t):
        self.t = t
        self.w = None
        self.r = {}

    def __getitem__(self, k):
        return self.t[k]


class K:
    def __init__(self, nc, es):
        self.nc = nc
        self.es = es
        self.engs = {'pe': nc.tensor, 'act': nc.scalar, 'dve': nc.vector, 'pool': nc.gpsimd, 'sp': nc.sync}
        self.sem = {}
        self.cnt = {}
        self.seen = {e: {} for e in self.engs}
        for e in ('pe', 'act', 'dve', 'pool'):
            self.newsem(e)

    def newsem(self, name):
        self.sem[name] = self.es.enter_context(self.nc.semaphore(name))
        self.cnt[name] = 0

    def sb(self, name, shape, dt):
        return Buf(self.es.enter_context(self.nc.sbuf_tensor("s_" + name, shape, dt)))

    def ps(self, name, shape, dt):
        return Buf(self.es.enter_context(self.nc.psum_tensor("p_" + name, shape, dt)))

    def wait(self, eng, tok):
        if tok is None:
            return
        k, v = tok
        if self.seen[eng].get(k, 0) >= v:
            return
        self.engs[eng].wait_ge(self.sem[k], v)
        self.seen[eng][k] = v

    def deps(self, eng, reads, writes):
        for b in reads:
            if b.w is not None and not (eng == 'pe' and b.w[0] == 'pe'):
                self.wait(eng, b.w)
        for b in writes:
            if b.w is not None and not (eng == 'pe' and b.w[0] == 'pe'):
                self.wait(eng, b.w)
            for k, t in b.r.items():
                if not (eng == 'pe' and t[0] == 'pe'):
                    self.wait(eng, t)

    def mark(self, tok, reads, writes):
        for b in reads:
            b.r[tok[0]] = tok
        for b in writes:
            b.w = tok
            b.r = {}

    def E(self, eng, fn, reads=(), writes=()):
        self.deps(eng, reads, writes)
        inst = fn()
        self.cnt[eng] += 1
        inst.then_inc(self.sem[eng], 1)
        tok = (eng, self.cnt[eng])
        self.mark(tok, reads, writes)
        return tok

    def DMA(self, eng, semname, fns, reads=(), writes=()):
        self.deps(eng, reads, writes)
        for fn in fns:
            inst = fn()
            self.cnt[semname] += 16
            inst.then_inc(self.sem[semname], 16)
        tok = (semname, self.cnt[semname])
        self.mark(tok, reads, writes)
        return tok


def build(NL, NTP, WITH_S, TP=512):
    SEQP = max(NTP, 1) * TP
    nc = bass.Bass("TRN2", target_bir_lowering=False)

    def din(name, shape):
        return nc.dram_tensor(name, shape, F32, kind="ExternalInput").ap()

    def dout(name, shape):
        return nc.dram_tensor(name, shape, F32, kind="ExternalOutput").ap()

    xp = din("xp", [D, SEQP]); xs = din("xs", [D, 16]); cvec = din("cvec", [128, KC, 2])
    vecs_d = din("vecs", [128, NL * V_PER + 16]); consts_d = din("consts", [128, 896])
    msamp = din("msamp", [128, NL * 8]); convs_in = din("convs_in", [128, NL, 32, 3])
    ns_in = din("ns_in", [NL, 8, 128, 2]); Cs_in = din("Cs_in", [NL, 8, 256, 256]); Ss_in = din("Ss_in", [NL, 16, 128, 128])
    Wd = {"w_in": din("w_in", [NL, D, N_IN]), "ada_w": din("ada_w", [NL, D, 6 * D]),
          "w_br_a": din("w_br_a", [NL, D, D]), "w_br_b": din("w_br_b", [NL, D, D]), "w_o": din("w_o", [NL, D, D]),
          "w_up": din("w_up", [NL, D, 4 * D]), "w_down": din("w_down", [NL, 4 * D, D])}
    b_in_d = din("b_in", [NL, N_IN])
    yp = dout("yp", [D, SEQP]); ys = dout("ys", [D, 16])
    O = {}
    for g in ("p", "s"):
        O[g] = dict(conv=dout("conv" + g, [128, NL, 32, 3]), C=dout("C" + g, [NL, 8, 256, 256]),
                    n=dout("n" + g, [NL, 8, 128, 2]), m=dout("m" + g, [1, NL * 8]), S=dout("S" + g, [NL, 16, 128, 128]))

    with ExitStack() as es:
        es.enter_context(nc.allow_non_contiguous_dma(reason="small strided state vectors"))
        k = K(nc, es)
        E, DMA = k.E, k.DMA
        xT = k.sb("xT", [128, KC, TP], F32)
        hT = k.sb("hT", [128, KC, TP], BF16)
        yaT = k.sb("yaT", [128, KC, TP], BF16)
        ybT = k.sb("ybT", [128, KC, TP], BF16)
        uT = k.sb("uT", [128, KC, TP], BF16)
        slots = [k.sb("ws%d" % i, [128, KC + 1, 512], BF16) for i in range(NSLOT)]
        for i in range(NSLOT):
            k.newsem("wsem%d" % i)
        vecs = k.sb("vecs", [128, NL * V_PER + 16], F32)
        cst = k.sb("cst", [128, 896], F32)
        tri = cst.t[:, 0:128]; ident = cst.t[:, 128:256]; maskb = cst.t[:, 256:384]; tri8 = cst.t[:, 384:896]
        identb = k.sb("identb", [128, 128], BF16); onesb = k.sb("onesb", [128, 128], BF16)
        onesf = k.sb("onesf", [128, 128], F32); negbig = k.sb("negbig", [128, 128], F32)
        rmask = k.sb("rmask", [128, 512], F32)
        mods = k.sb("mods", [128, NL, 96, 2], F32)
        A12 = k.sb("A12", [128, NL, 2, 16, 2], F32)
        lbt = k.sb("lbt", [128, NL, 16], F32); omlt = k.sb("omlt", [128, NL, 16], F32); nomlt = k.sb("nomlt", [128, NL, 16], F32)
        hist = k.sb("hist", [128, NL, 32, 3], F32)
        mst = k.sb("mst", [128, NL * 8], F32)
        rstd = k.sb("rstd", [128, TP], F32); tmpf = k.sb("tmpf", [128, TP], F32); sqb = k.sb("sqb", [128, TP], BF16)
        sg4 = k.sb("sg4", [128, 4, TP], BF16)
        uqk = k.sb("uqk", [128, 4, TP + 3], F32); cacc = k.sb("cacc", [128, TP], F32)
        qkT = k.sb("qkT", [128, 4, TP], BF16)
        vext = k.sb("vext", [128, 4, 257], BF16); sigo = k.sb("sigo", [128, 4, 256], BF16)
        igc = k.sb("igc", [128, 4, 8], F32); nlf = k.sb("nlf", [128, 4, 8], F32); nbc = k.sb("nbc", [128, 4, 8], F32)
        acadj = k.sb("acadj", [128, 4, 8], F32); gtmp = k.sb("gtmp", [128, 8], F32)
        igrep = k.sb("igrep", [128, 128], F32); nlfrep = k.sb("nlfrep", [128, 128], F32)
        grow = k.sb("grow", [128, 128], F32); wrow2 = [k.sb("wrow%d" % i, [128, 128], F32) for i in range(2)]
        dtmp = k.sb("dtmp", [128, 128], F32); DT = k.sb("DT", [128, 128], F32); STb2 = [k.sb("STb%d" % i, [128, 128], BF16) for i in range(2)]
        junk = dtmp; junk2 = k.sb("junk2", [128, 256], BF16)
        qsc2 = [k.sb("qsc%d" % i, [128, 2, 128], BF16) for i in range(2)]; ytm = k.sb("ytm", [128, 256], BF16); ktm = k.sb("ktm", [128, 256], BF16)
        sm2 = [k.sb("sm%d" % i, [128, 16], F32) for i in range(2)]
        Cf = k.sb("Cf", [128, 2, 257], F32); Cb = k.sb("Cb", [128, 2, 257], BF16)
        vtm = k.sb("vtm", [32, 16, 256], BF16)
        bh = rstd
        ebh = k.sb("ebh", [128, TP], F32); enbh = k.sb("enbh", [128, TP], F32)
        khT = k.sb("khT", [128, TP], BF16)
        Abf = k.sb("Abf", [32, TP], BF16); khtm = k.sb("khtm", [32, 16 * 128], BF16)
        Sf = k.sb("Sf", [128, 128], F32); Sb = k.sb("Sb", [128, 128], BF16)
        Sbc = [None] + [Buf(sg4.t[:, c // 4, (c % 4) * 128:(c % 4 + 1) * 128]) for c in range(1, 16)]
        P0 = k.ps("P0", [128, 512], F32); P1 = k.ps("P1", [128, 512], F32); PT = k.ps("PT", [128, 512], F32)
        PS = k.ps("PS", [128, 512], F32); PN = k.ps("PN", [128, 512], F32); PA = k.ps("PA", [128, 512], F32)
        PX = k.ps("PX", [128, 1024], BF16); PC = k.ps("PC", [128, 512], F32)
        PD = [P0, P1]
        pdi = [0]
        SPN = 8
        for i in range(SPN):
            k.newsem("sp%d" % i)
        spi = [0]
        sp_last = [None] * SPN

        def spdma(out_ap, in_ap, reads=(), writes=()):
            i = spi[0] % SPN
            spi[0] += 1
            k.wait('sp', sp_last[i])
            tok = DMA('sp', "sp%d" % i, [lambda: nc.sync.dma_start(out=out_ap, in_=in_ap)], reads, writes)
            sp_last[i] = tok
            return tok

        def plan():
            pl = []
            for l in range(NL):
                for jb in range(24):
                    pl.append(("ada_w", l, 0, ((jb * 512, 512),), False))
            tiles = [("p", i) for i in range(NTP)] + ([("s", 0)] if WITH_S else [])
            for _ in tiles:
                for l in range(NL):
                    pl.append(("w_in", l, 0, ((20480, 16),), True))
                    for hd in range(8):
                        pl.append(("w_in", l, 0, ((hd * 256, 256), (2048 + hd * 256, 256)), False))
                        pl.append(("w_in", l, 0, ((4096 + hd * 256, 256), (6144 + hd * 256, 256)), True))
                    for hp in range(8):
                        pl.append(("w_in", l, 0, ((8192 + hp * 256, 256), (10240 + hp * 256, 256)), False))
                        pl.append(("w_in", l, 0, ((12288 + hp * 256, 256), (14336 + hp * 256, 256)), True))
                    for jb in range(4):
                        pl.append(("w_in", l, 0, ((16384 + jb * 512, 512),), False))
                        pl.append(("w_br_a", l, 0, ((jb * 512, 512),), False))
                        pl.append(("w_in", l, 0, ((18432 + jb * 512, 512),), False))
                        pl.append(("w_br_b", l, 0, ((jb * 512, 512),), False))
                    for jb in range(4):
                        pl.append(("w_o", l, 0, ((jb * 512, 512),), False))
                    for qd in range(4):
                        for ub in range(4):
                            pl.append(("w_up", l, 0, ((qd * 2048 + ub * 512, 512),), False))
                        for ob in range(4):
                            pl.append(("w_down", l, qd * 2048, ((ob * 512, 512),), False))
            return pl

        PL = plan()
        wstate = dict(issued=0, used=0)
        N_ADA = NL * 24
        NTILES = NTP + (1 if WITH_S else 0)
        PER_PASS = (len(PL) - N_ADA) // max(NTILES, 1)
        USE_SCR = NTILES > 1
        if USE_SCR:
            SCRN = 100
            wscr_t = [nc.dram_tensor("wscr%d" % i, [min(SCRN, PER_PASS - i * SCRN), 128, (KC + 1) * 512], BF16, kind="Internal").ap()
                      for i in range((PER_PASS + SCRN - 1) // SCRN)]
            wscr = lambda pidx: wscr_t[pidx // SCRN][pidx % SCRN]
            scrb = [Buf(None) for _ in range(PER_PASS)]

        def w_issue(j):
            name, l, r0, segs, bias = PL[j]
            s = slots[j % NSLOT]
            fns = []
            if USE_SCR and j >= N_ADA:
                pno, pidx = divmod(j - N_ADA, PER_PASS)
                if pno >= 1:
                    src = wscr(pidx).rearrange("p (k n) -> p k n", n=512)
                    DMA('pool', "wsem%d" % (j % NSLOT), [lambda: nc.gpsimd.dma_start(out=s.t[:, :, :], in_=src)], reads=(scrb[pidx],), writes=(s,))
                    return
            c = 0
            for (c0, n) in segs:
                src = Wd[name][l, r0:r0 + D, c0:c0 + n].rearrange("(kc p) n -> p kc n", p=128)
                dst = s.t[:, 0:KC, c:c + n]
                fns.append(lambda src=src, dst=dst: nc.gpsimd.dma_start(out=dst, in_=src))
                if bias:
                    bsrc = b_in_d[l:l + 1, c0:c0 + n]
                    bdst = s.t[0:1, KC, c:c + n]
                    fns.append(lambda bsrc=bsrc, bdst=bdst: nc.gpsimd.dma_start(out=bdst, in_=bsrc))
                c += n
            DMA('pool', "wsem%d" % (j % NSLOT), fns, reads=(), writes=(s,))
            if USE_SCR and j >= N_ADA:
                pidx = (j - N_ADA) % PER_PASS
                spdma(wscr(pidx).rearrange("p (k n) -> p k n", n=512), s.t[:, :, :], reads=(s,), writes=(scrb[pidx],))

        def wnext(desc):
            j = wstate['used']
            assert PL[j] == desc, (j, PL[j], desc)
            while wstate['issued'] < len(PL) and wstate['issued'] <= j + NSLOT - 1:
                w_issue(wstate['issued'])
                wstate['issued'] += 1
            wstate['used'] += 1
            return slots[j % NSLOT]

        def nextpd():
            p = PD[pdi[0] % 2]
            pdi[0] += 1
            return p

        def fm_group(ps, slot, col, act, T, nk=KC):
            for kc in range(nk):
                E('pe', lambda kc=kc: nc.tensor.matmul(ps.t[:, 0:T], lhsT=slot.t[:, kc, col * 128:(col + 1) * 128],
                                                       rhs=act.t[:, kc, 0:T], start=(kc == 0), stop=(kc == nk - 1)),
                  reads=(slot, act), writes=(ps,))

        spdma(vecs.t[:], vecs_d, writes=(vecs,))
        spdma(cst.t[:], consts_d, writes=(cst,))
        E('dve', lambda: nc.vector.tensor_copy(out=identb.t[:], in_=ident), reads=(cst,), writes=(identb,))
        E('dve', lambda: nc.vector.memset(onesb.t[:], 1.0), writes=(onesb,))
        E('dve', lambda: nc.vector.memset(onesf.t[:], 1.0), writes=(onesf,))
        E('dve', lambda: nc.vector.memset(negbig.t[:], -1e30), writes=(negbig,))
        E('dve', lambda: nc.vector.memset(rmask.t[:], 1.0), writes=(rmask,))
        for c in range(16):
            E('dve', lambda c=c: nc.vector.memset(rmask.t[:, c * 32:c * 32 + 1], 0.0), writes=(rmask,))
        E('dve', lambda: nc.vector.memset(vext.t[:, :, 256:257], 1.0), writes=(vext,))

        def V(l, off, j=0, n=1):
            return vecs.t[:, l * V_PER + off + j: l * V_PER + off + j + n]

        lbe = Buf(tmpf.t[:, 0:NL * 16].rearrange("p (l f) -> p l f", f=16)); lbm = Buf(tmpf.t[:, 64:80]); lbs = Buf(tmpf.t[:, 80:96])
        E('dve', lambda: nc.vector.tensor_copy(out=lbm.t[:], in_=V(0, V_LBR, 0, 16)), reads=(vecs,), writes=(lbm,))
        for l in range(1, NL):
            E('dve', lambda l=l: nc.vector.tensor_tensor(out=lbm.t[:], in0=lbm.t[:], in1=V(l, V_LBR, 0, 16), op=ALU.max), reads=(vecs, lbm), writes=(lbm,))
        for l in range(NL):
            E('dve', lambda l=l: nc.vector.tensor_tensor(out=lbe.t[:, l, :], in0=V(l, V_LBR, 0, 16), in1=lbm.t[:], op=ALU.subtract), reads=(vecs, lbm), writes=(lbe,))
            E('act', lambda l=l: nc.scalar.activation(out=lbe.t[:, l, :], in_=lbe.t[:, l, :], func=AF.Exp), reads=(lbe,), writes=(lbe,))
        E('dve', lambda: nc.vector.tensor_copy(out=lbs.t[:], in_=lbe.t[:, 0, :]), reads=(lbe,), writes=(lbs,))
        for l in range(1, NL):
            E('dve', lambda l=l: nc.vector.tensor_tensor(out=lbs.t[:], in0=lbs.t[:], in1=lbe.t[:, l, :], op=ALU.add), reads=(lbe, lbs), writes=(lbs,))
        E('dve', lambda: nc.vector.reciprocal(out=lbs.t[:], in_=lbs.t[:]), reads=(lbs,), writes=(lbs,))
        for l in range(NL):
            E('dve', lambda l=l: nc.vector.tensor_tensor(out=lbe.t[:, l, :], in0=lbe.t[:, l, :], in1=lbs.t[:], op=ALU.mult), reads=(lbe, lbs), writes=(lbe,))
        E('dve', lambda: nc.vector.memset(lbt.t[:, 0, :], 0.0), writes=(lbt,))
        for l in range(1, NL):
            E('dve', lambda l=l: nc.vector.tensor_tensor(out=lbt.t[:, l, :], in0=lbt.t[:, l - 1, :], in1=lbe.t[:, l, :], op=ALU.add), reads=(lbe, lbt), writes=(lbt,))
        E('dve', lambda: nc.vector.tensor_scalar(out=omlt.t[:], in0=lbt.t[:], scalar1=-1.0, scalar2=1.0, op0=ALU.mult, op1=ALU.add), reads=(lbt,), writes=(omlt,))
        E('dve', lambda: nc.vector.tensor_scalar(out=nomlt.t[:], in0=omlt.t[:], scalar1=-1.0, scalar2=None, op0=ALU.mult), reads=(omlt,), writes=(nomlt,))

        csf = Buf(tmpf.t[:, 96:128].rearrange("p (k s) -> p k s", s=2)); csb = k.sb("csb", [128, KC, 2], BF16)
        spdma(csf.t[:], cvec, writes=(csf,))
        E('act', lambda: nc.scalar.activation(out=csb.t[:], in_=csf.t[:], func=AF.Silu), reads=(csf,), writes=(csb,))
        for l in range(NL):
            for jb in range(24):
                s = wnext(("ada_w", l, 0, ((jb * 512, 512),), False))
                for jj in range(4):
                    j = jb * 4 + jj
                    for kc in range(KC):
                        E('pe', lambda kc=kc, jj=jj, j=j, s=s: nc.tensor.matmul(PA.t[:, 2 * j:2 * j + 2], lhsT=s.t[:, kc, jj * 128:(jj + 1) * 128],
                                                                                rhs=csb.t[:, kc, :], start=(kc == 0), stop=(kc == KC - 1)),
                          reads=(s, csb), writes=(PA,))
            for sq in range(2):
                pav = PA.t[:, 0:192].rearrange("p (j s) -> p j s", s=2)[:, :, sq]
                E('dve', lambda l=l, sq=sq, pav=pav: nc.vector.tensor_tensor(out=mods.t[:, l, :, sq], in0=pav, in1=V(l, V_ADAB, 0, 96), op=ALU.add),
                  reads=(PA, vecs), writes=(mods,))
                for which, (goff, scoff) in enumerate(((V_N1G, 16), (V_N2G, 64))):
                    E('dve', lambda l=l, sq=sq, which=which, goff=goff, scoff=scoff: nc.vector.scalar_tensor_tensor(
                        out=A12.t[:, l, which, :, sq], in0=mods.t[:, l, scoff:scoff + 16, sq], scalar=1.0, in1=V(l, goff, 0, 16),
                        op0=ALU.add, op1=ALU.mult), reads=(mods, vecs), writes=(A12,))

        def modnorm(l, which, sq, T, shoff):
            for kc in range(KC):
                E('act', lambda kc=kc: nc.scalar.activation(out=sqb.t[:, 0:T], in_=xT.t[:, kc, 0:T], func=AF.Square), reads=(xT,), writes=(sqb,))
                E('pe', lambda kc=kc: nc.tensor.matmul(PA.t[:, 0:T], lhsT=onesb.t[:, :], rhs=sqb.t[:, 0:T], start=(kc == 0), stop=(kc == KC - 1)),
                  reads=(onesb, sqb), writes=(PA,))
            E('act', lambda: nc.scalar.activation(out=rstd.t[:, 0:T], in_=PA.t[:, 0:T], func=AF.Sqrt, scale=1.0 / D, bias=epsc), reads=(PA, smc), writes=(rstd,))
            E('dve', lambda: nc.vector.reciprocal(out=rstd.t[:, 0:T], in_=rstd.t[:, 0:T]), reads=(rstd,), writes=(rstd,))
            for kc in range(KC):
                E('dve', lambda kc=kc: nc.vector.tensor_tensor(out=tmpf.t[:, 0:T], in0=xT.t[:, kc, 0:T], in1=rstd.t[:, 0:T], op=ALU.mult), reads=(xT, rstd), writes=(tmpf,))
                E('act', lambda kc=kc: nc.scalar.activation(out=hT.t[:, kc, 0:T], in_=tmpf.t[:, 0:T], func=AF.Identity,
                                                            scale=A12.t[:, l, which, kc, sq:sq + 1], bias=mods.t[:, l, shoff + kc, sq:sq + 1]),
                  reads=(tmpf, A12, mods), writes=(hT,))

        smc = k.sb("smc", [128, 4], F32)
        E('dve', lambda: nc.vector.memset(smc.t[:, 0:1], EPS), writes=(smc,))
        E('dve', lambda: nc.vector.memset(smc.t[:, 1:2], 1.0), writes=(smc,))
        E('dve', lambda: nc.vector.memset(smc.t[:, 2:3], 0.0), writes=(smc,))
        epsc = smc.t[:, 0:1]; onec = smc.t[:, 1:2]

        def mlstm_gates(l, T, L, nch, gs):
            for c in range(nch):
                for kc in range(KC):
                    E('pe', lambda kc=kc, c=c: nc.tensor.matmul(PT.t[0:L, 0:16], lhsT=hT.t[:, kc, c * L:(c + 1) * L], rhs=gs.t[:, kc, 0:16], start=(kc == 0), stop=False),
                      reads=(hT, gs), writes=(PT,))
                E('pe', lambda: nc.tensor.matmul(PT.t[0:L, 0:16], lhsT=onesb.t[0:1, 0:L], rhs=gs.t[0:1, KC, 0:16], start=False, stop=True), reads=(onesb, gs), writes=(PT,))
                E('dve', lambda c=c: nc.vector.tensor_copy(out=igc.t[0:L, c, :], in_=PT.t[0:L, 0:8]), reads=(PT,), writes=(igc,))
                E('act', lambda c=c: nc.scalar.activation(out=gtmp.t[0:L, :], in_=PT.t[0:L, 8:16], func=AF.Exp, scale=-1.0), reads=(PT,), writes=(gtmp,))
                E('act', lambda c=c: nc.scalar.activation(out=nlf.t[0:L, c, :], in_=gtmp.t[0:L, :], func=AF.Ln, bias=onec[0:L, :]), reads=(gtmp, smc), writes=(nlf,))
                E('pe', lambda c=c: nc.tensor.matmul(PA.t[0:L, 0:8], lhsT=tri[0:L, 0:L], rhs=nlf.t[0:L, c, :], start=True, stop=True), reads=(cst, nlf), writes=(PA,))
                E('dve', lambda c=c: nc.vector.tensor_copy(out=nbc.t[0:L, c, :], in_=PA.t[0:L, 0:8]), reads=(PA,), writes=(nbc,))
                E('dve', lambda c=c: nc.vector.tensor_tensor(out=acadj.t[0:L, c, :], in0=igc.t[0:L, c, :], in1=nbc.t[0:L, c, :], op=ALU.add), reads=(igc, nbc), writes=(acadj,))
                E('dve', lambda c=c: nc.vector.tensor_scalar(out=acadj.t[0:L, c, :], in0=acadj.t[0:L, c, :], scalar1=LN16, scalar2=None, op0=ALU.add), reads=(acadj,), writes=(acadj,))

        def mlstm_head(l, hd, T, L, nch, g, first, sq):
            if first and g == "p":
                E('dve', lambda: nc.vector.memset(Cf.t[:], 0.0), writes=(Cf,))
            else:
                srcC = (Cs_in if first else O[g]["C"])[l, hd].rearrange("(dc p) v -> p dc v", p=128)
                srcn = (ns_in if first else O[g]["n"])[l, hd]
                spdma(Cf.t[:, :, 0:256], srcC, writes=(Cf,))
                spdma(Cf.t[:, :, 256], srcn, writes=(Cf,))
            E('act', lambda: nc.scalar.activation(out=Cb.t[:], in_=Cf.t[:], func=AF.Copy), reads=(Cf,), writes=(Cb,))
            s = wnext(("w_in", l, 0, ((hd * 256, 256), (2048 + hd * 256, 256)), False))
            for b4 in range(4):
                fb = (hd * 2 + b4) if b4 < 2 else (16 + hd * 2 + (b4 - 2))
                ps = nextpd()
                fm_group(ps, s, b4, hT, T)
                E('act', lambda b4=b4, fb=fb, ps=ps: nc.scalar.activation(out=uqk.t[:, b4, 3:3 + T], in_=ps.t[:, 0:T], func=AF.Identity, bias=V(l, V_BIN, fb)),
                  reads=(ps, vecs), writes=(uqk,))
                E('dve', lambda b4=b4, fb=fb: nc.vector.tensor_copy(out=uqk.t[:, b4, 0:3], in_=hist.t[:, l, fb, :]), reads=(hist,), writes=(uqk,))
                E('dve', lambda b4=b4, fb=fb: nc.vector.tensor_scalar(out=cacc.t[:, 0:T], in0=uqk.t[:, b4, 0:T], scalar1=V(l, V_CW, fb), scalar2=V(l, V_CB, fb),
                                                                      op0=ALU.mult, op1=ALU.add), reads=(uqk, vecs), writes=(cacc,))
                for j in range(1, 4):
                    E('dve', lambda b4=b4, fb=fb, j=j: nc.vector.scalar_tensor_tensor(out=cacc.t[:, 0:T], in0=uqk.t[:, b4, j:j + T], scalar=V(l, V_CW, j * 32 + fb),
                                                                                      in1=cacc.t[:, 0:T], op0=ALU.mult, op1=ALU.add), reads=(uqk, vecs, cacc), writes=(cacc,))
                E('dve', lambda b4=b4, fb=fb: nc.vector.tensor_copy(out=hist.t[:, l, fb, :], in_=uqk.t[:, b4, T:T + 3]), reads=(uqk,), writes=(hist,))
                E('act', lambda b4=b4: nc.scalar.activation(out=qkT.t[:, b4, 0:T], in_=cacc.t[:, 0:T], func=AF.Silu), reads=(cacc,), writes=(qkT,))
            s = wnext(("w_in", l, 0, ((4096 + hd * 256, 256), (6144 + hd * 256, 256)), True))
            for c in range(nch):
                for kc in range(KC):
                    E('pe', lambda kc=kc, c=c: nc.tensor.matmul(PT.t[0:L, 0:512], lhsT=hT.t[:, kc, c * L:(c + 1) * L], rhs=s.t[:, kc, 0:512], start=(kc == 0), stop=False),
                      reads=(hT, s), writes=(PT,))
                E('pe', lambda: nc.tensor.matmul(PT.t[0:L, 0:512], lhsT=onesb.t[0:1, 0:L], rhs=s.t[0:1, KC, 0:512], start=False, stop=True), reads=(onesb, s), writes=(PT,))
                E('act', lambda c=c: nc.scalar.activation(out=vext.t[0:L, c, 0:256], in_=PT.t[0:L, 0:256], func=AF.Copy), reads=(PT,), writes=(vext,))
                E('act', lambda c=c: nc.scalar.activation(out=sigo.t[0:L, c, :], in_=PT.t[0:L, 256:512], func=AF.Sigmoid), reads=(PT,), writes=(sigo,))
            m0 = mst.t[:, l * 8 + hd:l * 8 + hd + 1]
            pcs = [PC, PT]

            def P1(c):
                par = c % 2
                STb, qsc, wrow, sm = STb2[par], qsc2[par], wrow2[par], sm2[par]
                cs_ = slice(c * L, (c + 1) * L)
                E('dve', lambda: nc.vector.tensor_scalar(out=igrep.t[0:L, :], in0=onesf.t[0:L, :], scalar1=igc.t[0:L, c, hd:hd + 1], scalar2=None, op0=ALU.mult),
                  reads=(onesf, igc), writes=(igrep,))
                E('dve', lambda: nc.vector.tensor_scalar(out=nlfrep.t[0:L, :], in0=onesf.t[0:L, :], scalar1=nlf.t[0:L, c, hd:hd + 1], scalar2=None, op0=ALU.mult),
                  reads=(onesf, nlf), writes=(nlfrep,))
                E('pe', lambda: nc.tensor.matmul(PA.t[:, 0:L], lhsT=igrep.t[0:L, :], rhs=ident[0:L, 0:L], start=True, stop=False), reads=(igrep, cst), writes=(PA,))
                E('pe', lambda: nc.tensor.matmul(PA.t[:, 0:L], lhsT=nlfrep.t[0:L, :], rhs=tri[0:L, 0:L], start=False, stop=True), reads=(nlfrep, cst), writes=(PA,))
                E('pe', lambda: nc.tensor.matmul(PA.t[:, 256:257], lhsT=nlfrep.t[0:L, :], rhs=onesf.t[0:L, 0:1], start=True, stop=True), reads=(nlfrep, onesf), writes=(PA,))
                E('dve', lambda: nc.vector.tensor_tensor_scan(out=grow.t[:, 0:L], data0=PA.t[:, 0:L], data1=negbig.t[:, 0:L], initial=m0, op0=ALU.max, op1=ALU.max),
                  reads=(PA, negbig, mst), writes=(grow,))
                E('dve', lambda: nc.vector.scalar_tensor_tensor(out=junk.t[0:L, 0:L], in0=grow.t[0:L, 0:L], scalar=1.0, in1=ident[0:L, 0:L], op0=ALU.mult, op1=ALU.mult,
                                                                accum_out=sm.t[0:L, 0:1]), reads=(grow, cst), writes=(junk, sm))
                E('dve', lambda: nc.vector.tensor_tensor(out=dtmp.t[0:L, 0:L], in0=maskb[0:L, 0:L], in1=grow.t[0:L, 0:L], op=ALU.subtract), reads=(cst, grow), writes=(dtmp,))
                E('act', lambda: nc.scalar.activation(out=DT.t[0:L, 0:L], in_=dtmp.t[0:L, 0:L], func=AF.Exp, bias=acadj.t[0:L, c, hd:hd + 1]), reads=(dtmp, acadj), writes=(DT,))
                for dc in range(2):
                    E('pe', lambda dc=dc: nc.tensor.matmul(PS.t[0:L, 0:L], lhsT=qkT.t[:, 2 + dc, cs_], rhs=qkT.t[:, dc, cs_], start=(dc == 0), stop=(dc == 1)), reads=(qkT,), writes=(PS,))
                E('dve', lambda: nc.vector.tensor_tensor(out=STb.t[0:L, 0:L], in0=PS.t[0:L, 0:L], in1=DT.t[0:L, 0:L], op=ALU.mult), reads=(PS, DT), writes=(STb,))
                E('act', lambda: nc.scalar.activation(out=wrow.t[:, 0:L], in_=grow.t[:, 0:L], func=AF.Exp, scale=-1.0, bias=m0), reads=(grow, mst), writes=(wrow,))
                for dc in range(2):
                    E('dve', lambda dc=dc: nc.vector.tensor_tensor(out=qsc.t[:, dc, 0:L], in0=qkT.t[:, dc, cs_], in1=wrow.t[:, 0:L], op=ALU.mult), reads=(qkT, wrow), writes=(qsc,))
                E('dve', lambda: nc.vector.tensor_tensor(out=sm.t[0:L, 1:2], in0=nbc.t[0:L, c, hd:hd + 1], in1=sm.t[0:L, 0:1], op=ALU.subtract), reads=(nbc, sm), writes=(sm,))
                E('act', lambda: nc.scalar.activation(out=sm.t[0:L, 2:3], in_=sm.t[0:L, 1:2], func=AF.Exp), reads=(sm,), writes=(sm,))
                E('dve', lambda: nc.vector.tensor_scalar(out=sm.t[:, 8:9], in0=grow.t[:, L - 1:L], scalar1=-1.0, scalar2=None, op0=ALU.mult), reads=(grow,), writes=(sm,))
                E('act', lambda: nc.scalar.activation(out=sm.t[0:L, 9:10], in_=acadj.t[0:L, c, hd:hd + 1], func=AF.Exp, bias=sm.t[0:L, 8:9]), reads=(acadj, sm), writes=(sm,))
                E('dve', lambda: nc.vector.tensor_tensor(out=mst.t[:, l * 8 + hd:l * 8 + hd + 1], in0=grow.t[:, L - 1:L], in1=PA.t[:, 256:257], op=ALU.subtract), reads=(grow, PA), writes=(mst,))

            def P2(c):
                par = c % 2
                sm = sm2[par]
                cs_ = slice(c * L, (c + 1) * L)
                for dc in range(2):
                    E('pe', lambda dc=dc: nc.tensor.transpose(out=PX.t[0:L, 256 + dc * 128:256 + (dc + 1) * 128], in_=qkT.t[:, 2 + dc, cs_], identity=identb.t[:, :]), reads=(qkT, identb), writes=(PX,))
                E('act', lambda: nc.scalar.activation(out=ktm.t[0:L, :], in_=PX.t[0:L, 256:512], func=AF.Identity, scale=sm.t[0:L, 9:10]), reads=(PX, sm), writes=(ktm,))
                for dc in range(2):
                    E('pe', lambda dc=dc: nc.tensor.matmul(pcs[dc].t[:, 0:257], lhsT=ktm.t[0:L, dc * 128:(dc + 1) * 128], rhs=vext.t[0:L, c, :], start=True, stop=True), reads=(ktm, vext), writes=(pcs[dc],))

            def TA(c):
                par = c % 2
                STb, qsc, wrow, sm = STb2[par], qsc2[par], wrow2[par], sm2[par]
                cs_ = slice(c * L, (c + 1) * L)
                E('pe', lambda: nc.tensor.matmul(PN.t[0:L, 0:257], lhsT=STb.t[0:L, 0:L], rhs=vext.t[0:L, c, :], start=True, stop=False), reads=(STb, vext), writes=(PN,))
                for dc in range(2):
                    E('pe', lambda dc=dc: nc.tensor.matmul(PN.t[0:L, 0:257], lhsT=qsc.t[:, dc, 0:L], rhs=Cb.t[:, dc, :], start=False, stop=(dc == 1)), reads=(qsc, Cb), writes=(PN,))
                E('act', lambda: nc.scalar.activation(out=sm.t[0:L, 3:4], in_=PN.t[0:L, 256:257], func=AF.Abs), reads=(PN,), writes=(sm,))
                E('dve', lambda: nc.vector.tensor_tensor(out=sm.t[0:L, 3:4], in0=sm.t[0:L, 3:4], in1=sm.t[0:L, 2:3], op=ALU.max), reads=(sm,), writes=(sm,))
                E('act', lambda: nc.scalar.activation(out=junk2.t[0:L, 0:256], in_=PN.t[0:L, 0:256], func=AF.Square, accum_out=sm.t[0:L, 4:5]), reads=(PN,), writes=(junk2, sm))
                E('dve', lambda: nc.vector.scalar_tensor_tensor(out=sm.t[0:L, 5:6], in0=sm.t[0:L, 3:4], scalar=EPS, in1=sm.t[0:L, 3:4], op0=ALU.mult, op1=ALU.mult), reads=(sm,), writes=(sm,))
                E('dve', lambda: nc.vector.scalar_tensor_tensor(out=sm.t[0:L, 6:7], in0=sm.t[0:L, 4:5], scalar=1.0 / 256, in1=sm.t[0:L, 5:6], op0=ALU.mult, op1=ALU.add), reads=(sm,), writes=(sm,))
                E('act', lambda: nc.scalar.activation(out=sm.t[0:L, 6:7], in_=sm.t[0:L, 6:7], func=AF.Sqrt), reads=(sm,), writes=(sm,))
                E('dve', lambda: nc.vector.reciprocal(out=sm.t[0:L, 7:8], in_=sm.t[0:L, 6:7]), reads=(sm,), writes=(sm,))
                E('dve', lambda: nc.vector.scalar_tensor_tensor(out=ytm.t[0:L, :], in0=PN.t[0:L, 0:256], scalar=sm.t[0:L, 7:8], in1=sigo.t[0:L, c, :], op0=ALU.mult, op1=ALU.mult),
                  reads=(PN, sm, sigo), writes=(ytm,))
                for vc in range(2):
                    E('pe', lambda vc=vc: nc.tensor.transpose(out=PX.t[:, vc * 128:vc * 128 + L], in_=ytm.t[0:L, vc * 128:(vc + 1) * 128], identity=identb.t[0:L, 0:L]), reads=(ytm, identb), writes=(PX,))
                    E('act', lambda vc=vc: nc.scalar.activation(out=yaT.t[:, hd * 2 + vc, cs_], in_=PX.t[:, vc * 128:vc * 128 + L], func=AF.Identity, scale=V(l, V_MAN, hd * 2 + vc)),
                      reads=(PX, vecs), writes=(yaT,))

            def TB(c):
                wrow = wrow2[c % 2]
                for dc in range(2):
                    E('dve', lambda dc=dc: nc.vector.scalar_tensor_tensor(out=Cf.t[:, dc, :], in0=Cf.t[:, dc, :], scalar=wrow.t[:, L - 1:L], in1=pcs[dc].t[:, 0:257], op0=ALU.mult, op1=ALU.add),
                      reads=(Cf, wrow, pcs[dc]), writes=(Cf,))
                E('act', lambda: nc.scalar.activation(out=Cb.t[:], in_=Cf.t[:], func=AF.Copy), reads=(Cf,), writes=(Cb,))

            P1(0); P2(0)
            for c in range(nch):
                if c + 1 < nch:
                    P1(c + 1)
                TA(c)
                TB(c)
                if c + 1 < nch:
                    P2(c + 1)
            spdma(O[g]["C"][l, hd].rearrange("(dc p) v -> p dc v", p=128), Cf.t[:, :, 0:256], reads=(Cf,))
            spdma(O[g]["n"][l, hd], Cf.t[:, :, 256], reads=(Cf,))

        def hgrn_group(l, hp, T, Lh, nch, g, first, sq):
            s = wnext(("w_in", l, 0, ((8192 + hp * 256, 256), (10240 + hp * 256, 256)), False))
            for hh in range(2):
                ps = nextpd(); fm_group(ps, s, hh, hT, T)
                E('act', lambda hh=hh, ps=ps: nc.scalar.activation(out=uqk.t[:, hh, 0:T], in_=ps.t[:, 0:T], func=AF.Silu, bias=V(l, V_BIN, 64 + hp * 2 + hh)), reads=(ps, vecs), writes=(uqk,))
            for hh in range(2):
                ps = nextpd(); fm_group(ps, s, 2 + hh, hT, T)
                E('act', lambda hh=hh, ps=ps: nc.scalar.activation(out=uqk.t[:, 2 + hh, 0:T], in_=ps.t[:, 0:T], func=AF.Sigmoid, bias=V(l, V_BIN, 80 + hp * 2 + hh)), reads=(ps, vecs), writes=(uqk,))
            s = wnext(("w_in", l, 0, ((12288 + hp * 256, 256), (14336 + hp * 256, 256)), True))
            for hh in range(2):
                ps = nextpd(); fm_group(ps, s, hh, hT, T)
                E('act', lambda hh=hh, ps=ps: nc.scalar.activation(out=sqb.t[:, 0:T], in_=ps.t[:, 0:T], func=AF.Identity, bias=V(l, V_BIN, 96 + hp * 2 + hh)), reads=(ps, vecs), writes=(sqb,))
                for c in range(nch):
                    E('pe', lambda c=c: nc.tensor.transpose(out=PX.t[0:Lh, (c % 8) * 128:(c % 8 + 1) * 128], in_=sqb.t[:, c * Lh:(c + 1) * Lh], identity=identb.t[:, :]), reads=(sqb, identb), writes=(PX,))
                    if c % 8 == 7 or c == nch - 1:
                        c0 = (c // 8) * 8
                        nn = c - c0 + 1
                        E('act', lambda c0=c0, nn=nn, hh=hh: nc.scalar.activation(out=vtm.t[0:Lh, c0:c0 + nn, hh * 128:(hh + 1) * 128],
                                                                                 in_=PX.t[0:Lh, 0:nn * 128].rearrange("p (c v) -> p c v", v=128), func=AF.Copy), reads=(PX,), writes=(vtm,))
            for hh in range(2):
                ps = nextpd(); fm_group(ps, s, 2 + hh, hT, T)
                E('act', lambda hh=hh, ps=ps: nc.scalar.activation(out=qkT.t[:, hh, 0:T], in_=ps.t[:, 0:T], func=AF.Sigmoid, bias=V(l, V_BIN, 112 + hp * 2 + hh)), reads=(ps, vecs), writes=(qkT,))
            qth = qkT.t[:, 2, :]; kth = qkT.t[:, 3, :]
            lfh = cacc; kkh = tmpf
            for hh in range(2):
                hd = hp * 2 + hh
                if first and g == "p":
                    E('dve', lambda: nc.vector.memset(Sf.t[:], 0.0), writes=(Sf,))
                else:
                    spdma(Sf.t[:], (Ss_in if first else O[g]["S"])[l, hd], writes=(Sf,))
                E('act', lambda: nc.scalar.activation(out=Sb.t[:], in_=Sf.t[:], func=AF.Copy), reads=(Sf,), writes=(Sb,))
                E('act', lambda hh=hh, hd=hd: nc.scalar.activation(out=lfh.t[:, 0:T], in_=uqk.t[:, 2 + hh, 0:T], func=AF.Ln, scale=omlt.t[:, l, hd:hd + 1], bias=lbt.t[:, l, hd:hd + 1]),
                  reads=(uqk, omlt, lbt), writes=(lfh,))
                E('dve', lambda hh=hh, hd=hd: nc.vector.tensor_scalar(out=kkh.t[:, 0:T], in0=uqk.t[:, 2 + hh, 0:T], scalar1=nomlt.t[:, l, hd:hd + 1], scalar2=omlt.t[:, l, hd:hd + 1], op0=ALU.mult, op1=ALU.add),
                  reads=(uqk, nomlt, omlt), writes=(kkh,))
                E('dve', lambda: nc.vector.tensor_tensor_scan(out=bh.t[:, 0:T], data0=rmask.t[:, 0:T], data1=lfh.t[:, 0:T], initial=0.0, op0=ALU.mult, op1=ALU.add), reads=(rmask, lfh), writes=(bh,))
                E('act', lambda: nc.scalar.activation(out=ebh.t[:, 0:T], in_=bh.t[:, 0:T], func=AF.Exp), reads=(bh,), writes=(ebh,))
                E('act', lambda: nc.scalar.activation(out=enbh.t[:, 0:T], in_=bh.t[:, 0:T], func=AF.Exp, scale=-1.0), reads=(bh,), writes=(enbh,))
                E('dve', lambda hh=hh: nc.vector.tensor_tensor(out=qth[:, 0:T], in0=uqk.t[:, hh, 0:T], in1=ebh.t[:, 0:T], op=ALU.mult), reads=(uqk, ebh), writes=(qkT,))
                E('dve', lambda: nc.vector.tensor_tensor(out=kth[:, 0:T], in0=kkh.t[:, 0:T], in1=enbh.t[:, 0:T], op=ALU.mult), reads=(kkh, enbh), writes=(qkT,))
                for c in range(nch):
                    cs_ = slice(c * Lh, (c + 1) * Lh)
                    E('pe', lambda cs_=cs_: nc.tensor.matmul(PS.t[0:Lh, cs_], lhsT=kth[:, cs_], rhs=qth[:, cs_], start=True, stop=True), reads=(qkT,), writes=(PS,))
                E('dve', lambda: nc.vector.tensor_tensor(out=Abf.t[0:Lh, 0:T], in0=PS.t[0:Lh, 0:T], in1=tri8[0:Lh, 0:T], op=ALU.mult), reads=(PS, cst), writes=(Abf,))
                for c in range(nch):
                    cs_ = slice(c * Lh, (c + 1) * Lh)
                    ce = (c + 1) * Lh - 1
                    E('dve', lambda cs_=cs_, ce=ce: nc.vector.tensor_scalar(out=khT.t[:, cs_], in0=kth[:, cs_], scalar1=ebh.t[:, ce:ce + 1], scalar2=None, op0=ALU.mult), reads=(qkT, ebh), writes=(khT,))
                    E('pe', lambda cs_=cs_, c=c: nc.tensor.transpose(out=PX.t[0:Lh, (c % 8) * 128:(c % 8 + 1) * 128], in_=khT.t[:, cs_], identity=identb.t[:, :]), reads=(khT, identb), writes=(PX,))
                    if c % 8 == 7 or c == nch - 1:
                        c0 = (c // 8) * 8
                        nn = c - c0 + 1
                        E('act', lambda c0=c0, nn=nn: nc.scalar.activation(out=khtm.t[0:Lh, c0 * 128:(c0 + nn) * 128], in_=PX.t[0:Lh, 0:nn * 128], func=AF.Copy), reads=(PX,), writes=(khtm,))
                ubanks = [P0, P1, PC, PT]
                for c in range(nch):
                    vsl = vtm.t[0:Lh, c, hh * 128:(hh + 1) * 128]
                    ub = ubanks[c // 4]
                    E('pe', lambda c=c, vsl=vsl, ub=ub: nc.tensor.matmul(ub.t[:, (c % 4) * 128:(c % 4 + 1) * 128], lhsT=khtm.t[0:Lh, c * 128:(c + 1) * 128], rhs=vsl, start=True, stop=True), reads=(khtm, vtm), writes=(ub,))
                for c in range(nch):
                    ce = (c + 1) * Lh - 1
                    ub = ubanks[c // 4]
                    E('dve', lambda c=c, ce=ce, ub=ub: nc.vector.scalar_tensor_tensor(out=Sf.t[:], in0=Sf.t[:], scalar=ebh.t[:, ce:ce + 1], in1=ub.t[:, (c % 4) * 128:(c % 4 + 1) * 128], op0=ALU.mult, op1=ALU.add), reads=(Sf, ebh, ub), writes=(Sf,))
                    if c < nch - 1:
                        E('act', lambda c=c: nc.scalar.activation(out=Sbc[c + 1].t, in_=Sf.t[:], func=AF.Copy), reads=(Sf,), writes=(Sbc[c + 1],))
                for c in range(nch):
                    cs_ = slice(c * Lh, (c + 1) * Lh)
                    vsl = vtm.t[0:Lh, c, hh * 128:(hh + 1) * 128]
                    sbc = Sb if c == 0 else Sbc[c]
                    sbap = Sb.t[:, :] if c == 0 else Sbc[c].t
                    E('pe', lambda cs_=cs_, vsl=vsl: nc.tensor.matmul(PN.t[:, cs_], lhsT=vsl, rhs=Abf.t[0:Lh, cs_], start=True, stop=False), reads=(vtm, Abf), writes=(PN,))
                    E('pe', lambda cs_=cs_, sbap=sbap: nc.tensor.matmul(PN.t[:, cs_], lhsT=sbap, rhs=qth[:, cs_], start=False, stop=True), reads=(sbc, qkT), writes=(PN,))
                spdma(O[g]["S"][l, hd], Sf.t[:], reads=(Sf,))
                E('act', lambda: nc.scalar.activation(out=sqb.t[:, 0:T], in_=PN.t[:, 0:T], func=AF.Square), reads=(PN,), writes=(sqb,))
                E('pe', lambda: nc.tensor.matmul(PA.t[:, 0:T], lhsT=onesb.t[:, :], rhs=sqb.t[:, 0:T], start=True, stop=True), reads=(onesb, sqb), writes=(PA,))
                E('act', lambda: nc.scalar.activation(out=rstd.t[:, 0:T], in_=PA.t[:, 0:T], func=AF.Sqrt, scale=1.0 / 128, bias=epsc), reads=(PA, smc), writes=(rstd,))
                E('dve', lambda: nc.vector.reciprocal(out=rstd.t[:, 0:T], in_=rstd.t[:, 0:T]), reads=(rstd,), writes=(rstd,))
                E('dve', lambda hd=hd: nc.vector.scalar_tensor_tensor(out=tmpf.t[:, 0:T], in0=PN.t[:, 0:T], scalar=V(l, V_HBN, hd), in1=rstd.t[:, 0:T], op0=ALU.mult, op1=ALU.mult), reads=(PN, vecs, rstd), writes=(tmpf,))
                E('dve', lambda hd=hd, hh=hh: nc.vector.tensor_tensor(out=ybT.t[:, hd, 0:T], in0=tmpf.t[:, 0:T], in1=qkT.t[:, hh, 0:T], op=ALU.mult), reads=(tmpf, qkT), writes=(ybT,))

        def layer(l, T, g, first, sq):
            Lm = min(128, T); Lh = min(32, T)
            modnorm(l, 0, sq, T, 0)
            gs = wnext(("w_in", l, 0, ((20480, 16),), True))
            mlstm_gates(l, T, Lm, T // Lm, gs)
            for hd in range(8):
                mlstm_head(l, hd, T, Lm, T // Lm, g, first, sq)
            for hp in range(8):
                hgrn_group(l, hp, T, Lh, T // Lh, g, first, sq)
            for jb in range(4):
                s = wnext(("w_in", l, 0, ((16384 + jb * 512, 512),), False))
                for jj in range(4):
                    ps = nextpd(); fm_group(ps, s, jj, hT, T)
                    E('act', lambda jj=jj, ps=ps: nc.scalar.activation(out=sg4.t[:, jj, 0:T], in_=ps.t[:, 0:T], func=AF.Sigmoid, bias=V(l, V_BIN, 128 + jb * 4 + jj)), reads=(ps, vecs), writes=(sg4,))
                s = wnext(("w_br_a", l, 0, ((jb * 512, 512),), False))
                for jj in range(4):
                    ps = nextpd(); fm_group(ps, s, jj, yaT, T)
                    E('dve', lambda jj=jj, ps=ps: nc.vector.tensor_tensor(out=uqk.t[:, jj, 0:T], in0=ps.t[:, 0:T], in1=sg4.t[:, jj, 0:T], op=ALU.mult), reads=(ps, sg4), writes=(uqk,))
                s = wnext(("w_in", l, 0, ((18432 + jb * 512, 512),), False))
                for jj in range(4):
                    ps = nextpd(); fm_group(ps, s, jj, hT, T)
                    E('act', lambda jj=jj, ps=ps: nc.scalar.activation(out=sg4.t[:, jj, 0:T], in_=ps.t[:, 0:T], func=AF.Sigmoid, bias=V(l, V_BIN, 144 + jb * 4 + jj)), reads=(ps, vecs), writes=(sg4,))
                s = wnext(("w_br_b", l, 0, ((jb * 512, 512),), False))
                for jj in range(4):
                    ps = nextpd(); fm_group(ps, s, jj, ybT, T)
                    E('dve', lambda jj=jj, ps=ps: nc.vector.tensor_tensor(out=tmpf.t[:, 0:T], in0=ps.t[:, 0:T], in1=sg4.t[:, jj, 0:T], op=ALU.mult), reads=(ps, sg4), writes=(tmpf,))
                    E('dve', lambda jj=jj: nc.vector.tensor_tensor(out=uT.t[:, jb * 4 + jj, 0:T], in0=tmpf.t[:, 0:T], in1=uqk.t[:, jj, 0:T], op=ALU.add), reads=(tmpf, uqk), writes=(uT,))
            for jb in range(4):
                s = wnext(("w_o", l, 0, ((jb * 512, 512),), False))
                for jj in range(4):
                    j = jb * 4 + jj
                    ps = nextpd(); fm_group(ps, s, jj, uT, T)
                    E('dve', lambda j=j, ps=ps: nc.vector.scalar_tensor_tensor(out=xT.t[:, j, 0:T], in0=ps.t[:, 0:T], scalar=mods.t[:, l, 32 + j, sq:sq + 1], in1=xT.t[:, j, 0:T], op0=ALU.mult, op1=ALU.add),
                      reads=(ps, mods, xT), writes=(xT,))
            modnorm(l, 1, sq, T, 48)
            for qd in range(4):
                for ub in range(4):
                    s = wnext(("w_up", l, 0, ((qd * 2048 + ub * 512, 512),), False))
                    for jj in range(4):
                        ps = nextpd(); fm_group(ps, s, jj, hT, T)
                        E('act', lambda ps=ps: nc.scalar.activation(out=sqb.t[:, 0:T], in_=ps.t[:, 0:T], func=AF.Relu), reads=(ps,), writes=(sqb,))
                        E('dve', lambda ub=ub, jj=jj: nc.vector.tensor_tensor(out=uT.t[:, ub * 4 + jj, 0:T], in0=sqb.t[:, 0:T], in1=sqb.t[:, 0:T], op=ALU.mult), reads=(sqb,), writes=(uT,))
                for ob in range(4):
                    s = wnext(("w_down", l, qd * 2048, ((ob * 512, 512),), False))
                    for jj in range(4):
                        j = ob * 4 + jj
                        ps = nextpd(); fm_group(ps, s, jj, uT, T)
                        E('dve', lambda j=j, ps=ps: nc.vector.scalar_tensor_tensor(out=xT.t[:, j, 0:T], in0=ps.t[:, 0:T], scalar=mods.t[:, l, 80 + j, sq:sq + 1], in1=xT.t[:, j, 0:T], op0=ALU.mult, op1=ALU.add),
                          reads=(ps, mods, xT), writes=(xT,))

        def final_norm(T):
            for kc in range(KC):
                E('act', lambda kc=kc: nc.scalar.activation(out=sqb.t[:, 0:T], in_=xT.t[:, kc, 0:T], func=AF.Square), reads=(xT,), writes=(sqb,))
                E('pe', lambda kc=kc: nc.tensor.matmul(PA.t[:, 0:T], lhsT=onesb.t[:, :], rhs=sqb.t[:, 0:T], start=(kc == 0), stop=(kc == KC - 1)), reads=(onesb, sqb), writes=(PA,))
            E('act', lambda: nc.scalar.activation(out=rstd.t[:, 0:T], in_=PA.t[:, 0:T], func=AF.Sqrt, scale=1.0 / D, bias=epsc), reads=(PA, smc), writes=(rstd,))
            E('dve', lambda: nc.vector.reciprocal(out=rstd.t[:, 0:T], in_=rstd.t[:, 0:T]), reads=(rstd,), writes=(rstd,))
            fg0 = NL * V_PER
            for kc in range(KC):
                E('dve', lambda kc=kc: nc.vector.scalar_tensor_tensor(out=xT.t[:, kc, 0:T], in0=xT.t[:, kc, 0:T], scalar=vecs.t[:, fg0 + kc:fg0 + kc + 1], in1=rstd.t[:, 0:T], op0=ALU.mult, op1=ALU.mult),
                  reads=(xT, vecs, rstd), writes=(xT,))

        groups = []
        if NTP > 0:
            groups.append(("p", NTP, TP, 0))
        if WITH_S:
            groups.append(("s", 1, 16, 1))
        for (g, ntiles, T, sq) in groups:
            if g == "p":
                E('dve', lambda: nc.vector.memset(hist.t[:], 0.0), writes=(hist,))
                E('dve', lambda: nc.vector.memset(mst.t[:], 0.0), writes=(mst,))
            else:
                spdma(hist.t[:], convs_in, writes=(hist,))
                spdma(mst.t[:], msamp, writes=(mst,))
            for ti in range(ntiles):
                src = (xp[:, ti * T:(ti + 1) * T] if g == "p" else xs).rearrange("(kc p) t -> p kc t", p=128)
                spdma(xT.t[:, :, 0:T], src, writes=(xT,))
                for l in range(NL):
                    layer(l, T, g, ti == 0, sq)
                final_norm(T)
                dst = (yp[:, ti * T:(ti + 1) * T] if g == "p" else ys).rearrange("(kc p) t -> p kc t", p=128)
                spdma(dst, xT.t[:, :, 0:T], reads=(xT,))
            spdma(O[g]["conv"], hist.t[:], reads=(hist,))
            spdma(O[g]["m"], mst.t[0:1, :], reads=(mst,))
        for i in range(SPN):
            k.wait('sp', sp_last[i])
        assert wstate['used'] == len(PL), (wstate, len(PL))
    return nc


def _fm(v):
    return np.ascontiguousarray(v.reshape(-1, 128).T)


def _consts():
    s = np.arange(128)
    tri = (s[:, None] <= s[None, :]).astype(np.float32)
    ident = np.eye(128, dtype=np.float32)
    maskb = np.where(s[:, None] <= s[None, :], 0.0, -1e4).astype(np.float32)
    t32 = np.zeros((128, 32), np.float32)
    t32[:32] = tri[:32, :32]
    return np.ascontiguousarray(np.concatenate([tri, ident, maskb, np.tile(t32, (1, 16))], axis=1))


def make_inputs(NL, NTP, TP, inp, core):
    bp = core % 4
    f = lambda a: np.ascontiguousarray(np.asarray(a, dtype=np.float32))
    SEQP = max(NTP, 1) * TP
    m = {}
    m["xp"] = f(inp["x_prompt"][bp, :SEQP].T)
    m["xs"] = f(inp["x_sample"][core].T)
    cv = np.stack([_fm(inp["c_prompt"][bp]), _fm(inp["c_sample"][core])], axis=-1)
    m["cvec"] = f(cv)
    cols = []
    for l in range(NL):
        cols += [_fm(inp["norm1_g"][l]), _fm(inp["norm2_g"][l]), _fm(inp["ada_b"][l]), _fm(inp["b_in"][l, :20480])]
        cols += [_fm(inp["conv_w"][l, j]) for j in range(4)]
        cols += [_fm(inp["conv_b"][l]), _fm(inp["ma_norm"][l]), _fm(inp["hb_norm"][l]), _fm(inp["hgrn_lb_raw"][l])]
    cols.append(_fm(inp["final_g"]))
    m["vecs"] = f(np.concatenate(cols, axis=1))
    m["consts"] = _consts()
    m["msamp"] = f(np.broadcast_to(inp["state_mlstm_m"][:NL, core].reshape(1, NL * 8), (128, NL * 8)))
    cc = inp["cache_conv"][:NL, core]
    m["convs_in"] = f(cc.reshape(NL, 3, 32, 128).transpose(3, 0, 2, 1))
    m["ns_in"] = f(inp["state_mlstm_n"][:NL, core].reshape(NL, 8, 2, 128).transpose(0, 1, 3, 2))
    m["Cs_in"] = f(inp["state_mlstm_C"][:NL, core])
    m["Ss_in"] = f(inp["state_hgrn"][:NL, core])
    for n in ("w_in", "ada_w", "w_br_a", "w_br_b", "w_o", "w_up", "w_down", "b_in"):
        m[n] = f(inp[n][:NL])
    return m


def assemble(NL, NTP, TP, results, WITH_S=True):
    SEQP = NTP * TP
    outs = {}
    yp = np.stack([results[b]["yp"].T for b in range(4)], axis=0)
    ys = np.stack([results[c]["ys"].T for c in range(8)], axis=0)

    def grp(g, cores):
        conv = np.stack([results[c]["conv" + g].transpose(1, 3, 2, 0).reshape(NL, 3, 4096) for c in cores], axis=1)
        C = np.stack([results[c]["C" + g] for c in cores], axis=1)
        n = np.stack([results[c]["n" + g].transpose(0, 1, 3, 2).reshape(NL, 8, 256) for c in cores], axis=1)
        mm = np.stack([results[c]["m" + g].reshape(NL, 8) for c in cores], axis=1)
        S = np.stack([results[c]["S" + g] for c in cores], axis=1)
        return [np.ascontiguousarray(a, dtype=np.float32) for a in (conv, C, n, mm, S)]

    return tuple([np.ascontiguousarray(yp, dtype=np.float32), np.ascontiguousarray(ys, dtype=np.float32)] + grp("p", range(4)) + grp("s", range(8)))


def kernel(**inputs):
    NL, NTP, TP = 4, 8, 512
    inp = {k_: np.asarray(v) for k_, v in inputs.items()}
    nc = build(NL, NTP, True, TP)
    in_maps = [make_inputs(NL, NTP, TP, inp, c) for c in range(8)]
    res = run_bass_kernel_spmd(nc, in_maps, core_ids=list(range(8)))
    return assemble(NL, NTP, TP, res.results)
```

```python
import math
from contextlib import ExitStack
import numpy as np
import concourse.bass as bass
import concourse.mybir as mybir
from concourse.bass_utils import run_bass_kernel_spmd

F32 = mybir.dt.float32
BF16 = mybir.dt.bfloat16
AF = mybir.ActivationFunctionType
ALU = mybir.AluOpType

D = 2048
KC = 16
N_IN = 20496
EPS = 1e-6
NSLOT = 2
LN16 = math.log(1.0 / 16.0)
V_N1G, V_N2G, V_ADAB, V_BIN, V_CW, V_CB, V_MAN, V_HBN, V_LBR = 0, 16, 32, 128, 288, 416, 448, 464, 480
V_PER = 496


class Buf:
    def __init__(self, t):
        self.t = t
        self.w = None
        self.r = {}

    def __getitem__(self, k):
        return self.t[k]


class K:
    def __init__(self, nc, es):
        self.nc = nc
        self.es = es
        self.engs = {'pe': nc.tensor, 'act': nc.scalar, 'dve': nc.vector, 'pool': nc.gpsimd, 'sp': nc.sync}
        self.sem = {}
        self.cnt = {}
        self.seen = {e: {} for e in self.engs}
        for e in ('pe', 'act', 'dve', 'pool'):
            self.newsem(e)

    def newsem(self, name):
        self.sem[name] = self.es.enter_context(self.nc.semaphore(name))
        self.cnt[name] = 0

    def sb(self, name, shape, dt):
        return Buf(self.es.enter_context(self.nc.sbuf_tensor("s_" + name, shape, dt)))

    def ps(self, name, shape, dt):
        return Buf(self.es.enter_context(self.nc.psum_tensor("p_" + name, shape, dt)))

    def wait(self, eng, tok):
        if tok is None:
            return
        k, v = tok
        if self.seen[eng].get(k, 0) >= v:
            return
        self.engs[eng].wait_ge(self.sem[k], v)
        self.seen[eng][k] = v

    def deps(self, eng, reads, writes):
        for b in reads:
            if b.w is not None and not (eng == 'pe' and b.w[0] == 'pe'):
                self.wait(eng, b.w)
        for b in writes:
            if b.w is not None and not (eng == 'pe' and b.w[0] == 'pe'):
                self.wait(eng, b.w)
            for k, t in b.r.items():
                if not (eng == 'pe' and t[0] == 'pe'):
                    self.wait(eng, t)

    def mark(self, tok, reads, writes):
        for b in reads:
            b.r[tok[0]] = tok
        for b in writes:
            b.w = tok
            b.r = {}

    def E(self, eng, fn, reads=(), writes=()):
        self.deps(eng, reads, writes)
        inst = fn()
        self.cnt[eng] += 1
        inst.then_inc(self.sem[eng], 1)
        tok = (eng, self.cnt[eng])
        self.mark(tok, reads, writes)
        return tok

    def DMA(self, eng, semname, fns, reads=(), writes=()):
        self.deps(eng, reads, writes)
        for fn in fns:
            inst = fn()
            self.cnt[semname] += 16
            inst.then_inc(self.sem[semname], 16)
        tok = (semname, self.cnt[semname])
        self.mark(tok, reads, writes)
        return tok


def build(NL, NTP, WITH_S, TP=512):
    SEQP = max(NTP, 1) * TP
    nc = bass.Bass("TRN2", target_bir_lowering=False)

    def din(name, shape):
        return nc.dram_tensor(name, shape, F32, kind="ExternalInput").ap()

    def dout(name, shape):
        return nc.dram_tensor(name, shape, F32, kind="ExternalOutput").ap()

    xp = din("xp", [D, SEQP]); xs = din("xs", [D, 16]); cvec = din("cvec", [128, KC, 2])
    vecs_d = din("vecs", [128, NL * V_PER + 16]); consts_d = din("consts", [128, 896])
    msamp = din("msamp", [128, NL * 8]); convs_in = din("convs_in", [128, NL, 32, 3])
    ns_in = din("ns_in", [NL, 8, 128, 2]); Cs_in = din("Cs_in", [NL, 8, 256, 256]); Ss_in = din("Ss_in", [NL, 16, 128, 128])
    Wd = {"w_in": din("w_in", [NL, D, N_IN]), "ada_w": din("ada_w", [NL, D, 6 * D]),
          "w_br_a": din("w_br_a", [NL, D, D]), "w_br_b": din("w_br_b", [NL, D, D]), "w_o": din("w_o", [NL, D, D]),
          "w_up": din("w_up", [NL, D, 4 * D]), "w_down": din("w_down", [NL, 4 * D, D])}
    b_in_d = din("b_in", [NL, N_IN])
    yp = dout("yp", [D, SEQP]); ys = dout("ys", [D, 16])
    O = {}
    for g in ("p", "s"):
        O[g] = dict(conv=dout("conv" + g, [128, NL, 32, 3]), C=dout("C" + g, [NL, 8, 256, 256]),
                    n=dout("n" + g, [NL, 8, 128, 2]), m=dout("m" + g, [1, NL * 8]), S=dout("S" + g, [NL, 16, 128, 128]))

    with ExitStack() as es:
        es.enter_context(nc.allow_non_contiguous_dma(reason="small strided state vectors"))
        k = K(nc, es)
        E, DMA = k.E, k.DMA
        xT = k.sb("xT", [128, KC, TP], F32)
        hT = k.sb("hT", [128, KC, TP], BF16)
        yaT = k.sb("yaT", [128, KC, TP], BF16)
        ybT = k.sb("ybT", [128, KC, TP], BF16)
        uT = k.sb("uT", [128, KC, TP], BF16)
        slots = [k.sb("ws%d" % i, [128, KC + 1, 512], BF16) for i in range(NSLOT)]
        for i in range(NSLOT):
            k.newsem("wsem%d" % i)
        vecs = k.sb("vecs", [128, NL * V_PER + 16], F32)
        cst = k.sb("cst", [128, 896], F32)
        tri = cst.t[:, 0:128]; ident = cst.t[:, 128:256]; maskb = cst.t[:, 256:384]; tri8 = cst.t[:, 384:896]
        identb = k.sb("identb", [128, 128], BF16); onesb = k.sb("onesb", [128, 128], BF16)
        onesf = k.sb("onesf", [128, 128], F32); negbig = k.sb("negbig", [128, 128], F32)
        rmask = k.sb("rmask", [128, 512], F32)
        mods = k.sb("mods", [128, NL, 96, 2], F32)
        A12 = k.sb("A12", [128, NL, 2, 16, 2], F32)
        lbt = k.sb("lbt", [128, NL, 16], F32); omlt = k.sb("omlt", [128, NL, 16], F32); nomlt = k.sb("nomlt", [128, NL, 16], F32)
        hist = k.sb("hist", [128, NL, 32, 3], F32)
        mst = k.sb("mst", [128, NL * 8], F32)
        rstd = k.sb("rstd", [128, TP], F32); tmpf = k.sb("tmpf", [128, TP], F32); sqb = k.sb("sqb", [128, TP], BF16)
        sg4 = k.sb("sg4", [128, 4, TP], BF16)
        uqk = k.sb("uqk", [128, 4, TP + 3], F32); cacc = k.sb("cacc", [128, TP], F32)
        qkT = k.sb("qkT", [128, 4, TP], BF16)
        vext = k.sb("vext", [128, 4, 257], BF16); sigo = k.sb("sigo", [128, 4, 256], BF16)
        igc = k.sb("igc", [128, 4, 8], F32); nlf = k.sb("nlf", [128, 4, 8], F32); nbc = k.sb("nbc", [128, 4, 8], F32)
        acadj = k.sb("acadj", [128, 4, 8], F32); gtmp = k.sb("gtmp", [128, 8], F32)
        igrep = k.sb("igrep", [128, 128], F32); nlfrep = k.sb("nlfrep", [128, 128], F32)
        grow = k.sb("grow", [128, 128], F32); wrow2 = [k.sb("wrow%d" % i, [128, 128], F32) for i in range(2)]
        dtmp = k.sb("dtmp", [128, 128], F32); DT = k.sb("DT", [128, 128], F32); STb2 = [k.sb("STb%d" % i, [128, 128], BF16) for i in range(2)]
        junk = dtmp; junk2 = k.sb("junk2", [128, 256], BF16)
        qsc2 = [k.sb("qsc%d" % i, [128, 2, 128], BF16) for i in range(2)]; ytm = k.sb("ytm", [128, 256], BF16); ktm = k.sb("ktm", [128, 256], BF16)
        sm2 = [k.sb("sm%d" % i, [128, 16], F32) for i in range(2)]
        Cf = k.sb("Cf", [128, 2, 257], F32); Cb = k.sb("Cb", [128, 2, 257], BF16)
        vtm = k.sb("vtm", [32, 16, 256], BF16)
        bh = rstd
        ebh = k.sb("ebh", [128, TP], F32); enbh = k.sb("enbh", [128, TP], F32)
        khT = k.sb("khT", [128, TP], BF16)
        Abf = k.sb("Abf", [32, TP], BF16); khtm = k.sb("khtm", [32, 16 * 128], BF16)
        Sf = k.sb("Sf", [128, 128], F32); Sb = k.sb("Sb", [128, 128], BF16)
        Sbc = [None] + [Buf(sg4.t[:, c // 4, (c % 4) * 128:(c % 4 + 1) * 128]) for c in range(1, 16)]
        P0 = k.ps("P0", [128, 512], F32); P1 = k.ps("P1", [128, 512], F32); PT = k.ps("PT", [128, 512], F32)
        PS = k.ps("PS", [128, 512], F32); PN = k.ps("PN", [128, 512], F32); PA = k.ps("PA", [128, 512], F32)
        PX = k.ps("PX", [128, 1024], BF16); PC = k.ps("PC", [128, 512], F32)
        PD = [P0, P1]
        pdi = [0]
        SPN = 8
        for i in range(SPN):
            k.newsem("sp%d" % i)
        spi = [0]
        sp_last = [None] * SPN

        def spdma(out_ap, in_ap, reads=(), writes=()):
            i = spi[0] % SPN
            spi[0] += 1
            k.wait('sp', sp_last[i])
            tok = DMA('sp', "sp%d" % i, [lambda: nc.sync.dma_start(out=out_ap, in_=in_ap)], reads, writes)
            sp_last[i] = tok
            return tok

        def plan():
            pl = []
            for l in range(NL):
                for jb in range(24):
                    pl.append(("ada_w", l, 0, ((jb * 512, 512),), False))
            tiles = [("p", i) for i in range(NTP)] + ([("s", 0)] if WITH_S else [])
            for _ in tiles:
                for l in range(NL):
                    pl.append(("w_in", l, 0, ((20480, 16),), True))
                    for hd in range(8):
                        pl.append(("w_in", l, 0, ((hd * 256, 256), (2048 + hd * 256, 256)), False))
                        pl.append(("w_in", l, 0, ((4096 + hd * 256, 256), (6144 + hd * 256, 256)), True))
                    for hp in range(8):
                        pl.append(("w_in", l, 0, ((8192 + hp * 256, 256), (10240 + hp * 256, 256)), False))
                        pl.append(("w_in", l, 0, ((12288 + hp * 256, 256), (14336 + hp * 256, 256)), True))
                    for jb in range(4):
                        pl.append(("w_in", l, 0, ((16384 + jb * 512, 512),), False))
                        pl.append(("w_br_a", l, 0, ((jb * 512, 512),), False))
                        pl.append(("w_in", l, 0, ((18432 + jb * 512, 512),), False))
                        pl.append(("w_br_b", l, 0, ((jb * 512, 512),), False))
                    for jb in range(4):
                        pl.append(("w_o", l, 0, ((jb * 512, 512),), False))
                    for qd in range(4):
                        for ub in range(4):
                            pl.append(("w_up", l, 0, ((qd * 2048 + ub * 512, 512),), False))
                        for ob in range(4):
                            pl.append(("w_down", l, qd * 2048, ((ob * 512, 512),), False))
            return pl

        PL = plan()
        wstate = dict(issued=0, used=0)
        N_ADA = NL * 24
        NTILES = NTP + (1 if WITH_S else 0)
        PER_PASS = (len(PL) - N_ADA) // max(NTILES, 1)
        USE_SCR = NTILES > 1
        if USE_SCR:
            SCRN = 100
            wscr_t = [nc.dram_tensor("wscr%d" % i, [min(SCRN, PER_PASS - i * SCRN), 128, (KC + 1) * 512], BF16, kind="Internal").ap()
                      for i in range((PER_PASS + SCRN - 1) // SCRN)]
            wscr = lambda pidx: wscr_t[pidx // SCRN][pidx % SCRN]
            scrb = [Buf(None) for _ in range(PER_PASS)]

        def w_issue(j):
            name, l, r0, segs, bias = PL[j]
            s = slots[j % NSLOT]
            fns = []
            if USE_SCR and j >= N_ADA:
                pno, pidx = divmod(j - N_ADA, PER_PASS)
                if pno >= 1:
                    src = wscr(pidx).rearrange("p (k n) -> p k n", n=512)
                    DMA('pool', "wsem%d" % (j % NSLOT), [lambda: nc.gpsimd.dma_start(out=s.t[:, :, :], in_=src)], reads=(scrb[pidx],), writes=(s,))
                    return
            c = 0
            for (c0, n) in segs:
                src = Wd[name][l, r0:r0 + D, c0:c0 + n].rearrange("(kc p) n -> p kc n", p=128)
                dst = s.t[:, 0:KC, c:c + n]
                fns.append(lambda src=src, dst=dst: nc.gpsimd.dma_start(out=dst, in_=src))
                if bias:
                    bsrc = b_in_d[l:l + 1, c0:c0 + n]
                    bdst = s.t[0:1, KC, c:c + n]
                    fns.append(lambda bsrc=bsrc, bdst=bdst: nc.gpsimd.dma_start(out=bdst, in_=bsrc))
                c += n
            DMA('pool', "wsem%d" % (j % NSLOT), fns, reads=(), writes=(s,))
            if USE_SCR and j >= N_ADA:
                pidx = (j - N_ADA) % PER_PASS
                spdma(wscr(pidx).rearrange("p (k n) -> p k n", n=512), s.t[:, :, :], reads=(s,), writes=(scrb[pidx],))

        def wnext(desc):
            j = wstate['used']
            assert PL[j] == desc, (j, PL[j], desc)
            while wstate['issued'] < len(PL) and wstate['issued'] <= j + NSLOT - 1:
                w_issue(wstate['issued'])
                wstate['issued'] += 1
            wstate['used'] += 1
            return slots[j % NSLOT]

        def nextpd():
            p = PD[pdi[0] % 2]
            pdi[0] += 1
            return p

        def fm_group(ps, slot, col, act, T, nk=KC):
            for kc in range(nk):
                E('pe', lambda kc=kc: nc.tensor.matmul(ps.t[:, 0:T], lhsT=slot.t[:, kc, col * 128:(col + 1) * 128],
                                                       rhs=act.t[:, kc, 0:T], start=(kc == 0), stop=(kc == nk - 1)),
                  reads=(slot, act), writes=(ps,))

        spdma(vecs.t[:], vecs_d, writes=(vecs,))
        spdma(cst.t[:], consts_d, writes=(cst,))
        E('dve', lambda: nc.vector.tensor_copy(out=identb.t[:], in_=ident), reads=(cst,), writes=(identb,))
        E('dve', lambda: nc.vector.memset(onesb.t[:], 1.0), writes=(onesb,))
        E('dve', lambda: nc.vector.memset(onesf.t[:], 1.0), writes=(onesf,))
        E('dve', lambda: nc.vector.memset(negbig.t[:], -1e30), writes=(negbig,))
        E('dve', lambda: nc.vector.memset(rmask.t[:], 1.0), writes=(rmask,))
        for c in range(16):
            E('dve', lambda c=c: nc.vector.memset(rmask.t[:, c * 32:c * 32 + 1], 0.0), writes=(rmask,))
        E('dve', lambda: nc.vector.memset(vext.t[:, :, 256:257], 1.0), writes=(vext,))

        def V(l, off, j=0, n=1):
            return vecs.t[:, l * V_PER + off + j: l * V_PER + off + j + n]

        lbe = Buf(tmpf.t[:, 0:NL * 16].rearrange("p (l f) -> p l f", f=16)); lbm = Buf(tmpf.t[:, 64:80]); lbs = Buf(tmpf.t[:, 80:96])
        E('dve', lambda: nc.vector.tensor_copy(out=lbm.t[:], in_=V(0, V_LBR, 0, 16)), reads=(vecs,), writes=(lbm,))
        for l in range(1, NL):
            E('dve', lambda l=l: nc.vector.tensor_tensor(out=lbm.t[:], in0=lbm.t[:], in1=V(l, V_LBR, 0, 16), op=ALU.max), reads=(vecs, lbm), writes=(lbm,))
        for l in range(NL):
            E('dve', lambda l=l: nc.vector.tensor_tensor(out=lbe.t[:, l, :], in0=V(l, V_LBR, 0, 16), in1=lbm.t[:], op=ALU.subtract), reads=(vecs, lbm), writes=(lbe,))
            E('act', lambda l=l: nc.scalar.activation(out=lbe.t[:, l, :], in_=lbe.t[:, l, :], func=AF.Exp), reads=(lbe,), writes=(lbe,))
        E('dve', lambda: nc.vector.tensor_copy(out=lbs.t[:], in_=lbe.t[:, 0, :]), reads=(lbe,), writes=(lbs,))
        for l in range(1, NL):
            E('dve', lambda l=l: nc.vector.tensor_tensor(out=lbs.t[:], in0=lbs.t[:], in1=lbe.t[:, l, :], op=ALU.add), reads=(lbe, lbs), writes=(lbs,))
        E('dve', lambda: nc.vector.reciprocal(out=lbs.t[:], in_=lbs.t[:]), reads=(lbs,), writes=(lbs,))
        for l in range(NL):
            E('dve', lambda l=l: nc.vector.tensor_tensor(out=lbe.t[:, l, :], in0=lbe.t[:, l, :], in1=lbs.t[:], op=ALU.mult), reads=(lbe, lbs), writes=(lbe,))
        E('dve', lambda: nc.vector.memset(lbt.t[:, 0, :], 0.0), writes=(lbt,))
        for l in range(1, NL):
            E('dve', lambda l=l: nc.vector.tensor_tensor(out=lbt.t[:, l, :], in0=lbt.t[:, l - 1, :], in1=lbe.t[:, l, :], op=ALU.add), reads=(lbe, lbt), writes=(lbt,))
        E('dve', lambda: nc.vector.tensor_scalar(out=omlt.t[:], in0=lbt.t[:], scalar1=-1.0, scalar2=1.0, op0=ALU.mult, op1=ALU.add), reads=(lbt,), writes=(omlt,))
        E('dve', lambda: nc.vector.tensor_scalar(out=nomlt.t[:], in0=omlt.t[:], scalar1=-1.0, scalar2=None, op0=ALU.mult), reads=(omlt,), writes=(nomlt,))

        csf = Buf(tmpf.t[:, 96:128].rearrange("p (k s) -> p k s", s=2)); csb = k.sb("csb", [128, KC, 2], BF16)
        spdma(csf.t[:], cvec, writes=(csf,))
        E('act', lambda: nc.scalar.activation(out=csb.t[:], in_=csf.t[:], func=AF.Silu), reads=(csf,), writes=(csb,))
        for l in range(NL):
            for jb in range(24):
                s = wnext(("ada_w", l, 0, ((jb * 512, 512),), False))
                for jj in range(4):
                    j = jb * 4 + jj
                    for kc in range(KC):
                        E('pe', lambda kc=kc, jj=jj, j=j, s=s: nc.tensor.matmul(PA.t[:, 2 * j:2 * j + 2], lhsT=s.t[:, kc, jj * 128:(jj + 1) * 128],
                                                                                rhs=csb.t[:, kc, :], start=(kc == 0), stop=(kc == KC - 1)),
                          reads=(s, csb), writes=(PA,))
            for sq in range(2):
                pav = PA.t[:, 0:192].rearrange("p (j s) -> p j s", s=2)[:, :, sq]
                E('dve', lambda l=l, sq=sq, pav=pav: nc.vector.tensor_tensor(out=mods.t[:, l, :, sq], in0=pav, in1=V(l, V_ADAB, 0, 96), op=ALU.add),
                  reads=(PA, vecs), writes=(mods,))
                for which, (goff, scoff) in enumerate(((V_N1G, 16), (V_N2G, 64))):
                    E('dve', lambda l=l, sq=sq, which=which, goff=goff, scoff=scoff: nc.vector.scalar_tensor_tensor(
                        out=A12.t[:, l, which, :, sq], in0=mods.t[:, l, scoff:scoff + 16, sq], scalar=1.0, in1=V(l, goff, 0, 16),
                        op0=ALU.add, op1=ALU.mult), reads=(mods, vecs), writes=(A12,))

        def modnorm(l, which, sq, T, shoff):
            for kc in range(KC):
                E('act', lambda kc=kc: nc.scalar.activation(out=sqb.t[:, 0:T], in_=xT.t[:, kc, 0:T], func=AF.Square), reads=(xT,), writes=(sqb,))
                E('pe', lambda kc=kc: nc.tensor.matmul(PA.t[:, 0:T], lhsT=onesb.t[:, :], rhs=sqb.t[:, 0:T], start=(kc == 0), stop=(kc == KC - 1)),
                  reads=(onesb, sqb), writes=(PA,))
            E('act', lambda: nc.scalar.activation(out=rstd.t[:, 0:T], in_=PA.t[:, 0:T], func=AF.Ln, scale=1.0 / D, bias=epsc), reads=(PA, smc), writes=(rstd,))
            E('act', lambda: nc.scalar.activation(out=rstd.t[:, 0:T], in_=rstd.t[:, 0:T], func=AF.Exp, scale=-0.5), reads=(rstd,), writes=(rstd,))
            for kc in range(KC):
                E('dve', lambda kc=kc: nc.vector.tensor_tensor(out=tmpf.t[:, 0:T], in0=xT.t[:, kc, 0:T], in1=rstd.t[:, 0:T], op=ALU.mult), reads=(xT, rstd), writes=(tmpf,))
                E('act', lambda kc=kc: nc.scalar.activation(out=hT.t[:, kc, 0:T], in_=tmpf.t[:, 0:T], func=AF.Identity,
                                                            scale=A12.t[:, l, which, kc, sq:sq + 1], bias=mods.t[:, l, shoff + kc, sq:sq + 1]),
                  reads=(tmpf, A12, mods), writes=(hT,))

        smc = k.sb("smc", [128, 4], F32)
        E('dve', lambda: nc.vector.memset(smc.t[:, 0:1], EPS), writes=(smc,))
        E('dve', lambda: nc.vector.memset(smc.t[:, 1:2], 1.0), writes=(smc,))
        E('dve', lambda: nc.vector.memset(smc.t[:, 2:3], 0.0), writes=(smc,))
        epsc = smc.t[:, 0:1]; onec = smc.t[:, 1:2]

        def mlstm_gates(l, T, L, nch, gs):
            for c in range(nch):
                for kc in range(KC):
                    E('pe', lambda kc=kc, c=c: nc.tensor.matmul(PT.t[0:L, 0:16], lhsT=hT.t[:, kc, c * L:(c + 1) * L], rhs=gs.t[:, kc, 0:16], start=(kc == 0), stop=False),
                      reads=(hT, gs), writes=(PT,))
                E('pe', lambda: nc.tensor.matmul(PT.t[0:L, 0:16], lhsT=onesb.t[0:1, 0:L], rhs=gs.t[0:1, KC, 0:16], start=False, stop=True), reads=(onesb, gs), writes=(PT,))
                E('dve', lambda c=c: nc.vector.tensor_copy(out=igc.t[0:L, c, :], in_=PT.t[0:L, 0:8]), reads=(PT,), writes=(igc,))
                E('act', lambda c=c: nc.scalar.activation(out=gtmp.t[0:L, :], in_=PT.t[0:L, 8:16], func=AF.Exp, scale=-1.0), reads=(PT,), writes=(gtmp,))
                E('act', lambda c=c: nc.scalar.activation(out=nlf.t[0:L, c, :], in_=gtmp.t[0:L, :], func=AF.Ln, bias=onec[0:L, :]), reads=(gtmp, smc), writes=(nlf,))
                E('pe', lambda c=c: nc.tensor.matmul(PA.t[0:L, 0:8], lhsT=tri[0:L, 0:L], rhs=nlf.t[0:L, c, :], start=True, stop=True), reads=(cst, nlf), writes=(PA,))
                E('dve', lambda c=c: nc.vector.tensor_copy(out=nbc.t[0:L, c, :], in_=PA.t[0:L, 0:8]), reads=(PA,), writes=(nbc,))
                E('dve', lambda c=c: nc.vector.tensor_tensor(out=acadj.t[0:L, c, :], in0=igc.t[0:L, c, :], in1=nbc.t[0:L, c, :], op=ALU.add), reads=(igc, nbc), writes=(acadj,))
                E('dve', lambda c=c: nc.vector.tensor_scalar(out=acadj.t[0:L, c, :], in0=acadj.t[0:L, c, :], scalar1=LN16, scalar2=None, op0=ALU.add), reads=(acadj,), writes=(acadj,))

        def mlstm_head(l, hd, T, L, nch, g, first, sq):
            if first and g == "p":
                E('dve', lambda: nc.vector.memset(Cf.t[:], 0.0), writes=(Cf,))
            else:
                srcC = (Cs_in if first else O[g]["C"])[l, hd].rearrange("(dc p) v -> p dc v", p=128)
                srcn = (ns_in if first else O[g]["n"])[l, hd]
                spdma(Cf.t[:, :, 0:256], srcC, writes=(Cf,))
                spdma(Cf.t[:, :, 256], srcn, writes=(Cf,))
            E('act', lambda: nc.scalar.activation(out=Cb.t[:], in_=Cf.t[:], func=AF.Copy), reads=(Cf,), writes=(Cb,))
            s = wnext(("w_in", l, 0, ((hd * 256, 256), (2048 + hd * 256, 256)), False))
            for b4 in range(4):
                fb = (hd * 2 + b4) if b4 < 2 else (16 + hd * 2 + (b4 - 2))
                ps = nextpd()
                fm_group(ps, s, b4, hT, T)
                E('act', lambda b4=b4, fb=fb, ps=ps: nc.scalar.activation(out=uqk.t[:, b4, 3:3 + T], in_=ps.t[:, 0:T], func=AF.Identity, bias=V(l, V_BIN, fb)),
                  reads=(ps, vecs), writes=(uqk,))
                E('dve', lambda b4=b4, fb=fb: nc.vector.tensor_copy(out=uqk.t[:, b4, 0:3], in_=hist.t[:, l, fb, :]), reads=(hist,), writes=(uqk,))
                E('dve', lambda b4=b4, fb=fb: nc.vector.tensor_scalar(out=cacc.t[:, 0:T], in0=uqk.t[:, b4, 0:T], scalar1=V(l, V_CW, fb), scalar2=V(l, V_CB, fb),
                                                                      op0=ALU.mult, op1=ALU.add), reads=(uqk, vecs), writes=(cacc,))
                for j in range(1, 4):
                    E('dve', lambda b4=b4, fb=fb, j=j: nc.vector.scalar_tensor_tensor(out=cacc.t[:, 0:T], in0=uqk.t[:, b4, j:j + T], scalar=V(l, V_CW, j * 32 + fb),
                                                                                      in1=cacc.t[:, 0:T], op0=ALU.mult, op1=ALU.add), reads=(uqk, vecs, cacc), writes=(cacc,))
                E('dve', lambda b4=b4, fb=fb: nc.vector.tensor_copy(out=hist.t[:, l, fb, :], in_=uqk.t[:, b4, T:T + 3]), reads=(uqk,), writes=(hist,))
                E('act', lambda b4=b4: nc.scalar.activation(out=qkT.t[:, b4, 0:T], in_=cacc.t[:, 0:T], func=AF.Silu), reads=(cacc,), writes=(qkT,))
            s = wnext(("w_in", l, 0, ((4096 + hd * 256, 256), (6144 + hd * 256, 256)), True))
            for c in range(nch):
                for kc in range(KC):
                    E('pe', lambda kc=kc, c=c: nc.tensor.matmul(PT.t[0:L, 0:512], lhsT=hT.t[:, kc, c * L:(c + 1) * L], rhs=s.t[:, kc, 0:512], start=(kc == 0), stop=False),
                      reads=(hT, s), writes=(PT,))
                E('pe', lambda: nc.tensor.matmul(PT.t[0:L, 0:512], lhsT=onesb.t[0:1, 0:L], rhs=s.t[0:1, KC, 0:512], start=False, stop=True), reads=(onesb, s), writes=(PT,))
                E('act', lambda c=c: nc.scalar.activation(out=vext.t[0:L, c, 0:256], in_=PT.t[0:L, 0:256], func=AF.Copy), reads=(PT,), writes=(vext,))
                E('act', lambda c=c: nc.scalar.activation(out=sigo.t[0:L, c, :], in_=PT.t[0:L, 256:512], func=AF.Sigmoid), reads=(PT,), writes=(sigo,))
            m0 = mst.t[:, l * 8 + hd:l * 8 + hd + 1]
            pcs = [PC, PT]

            def P1(c):
                par = c % 2
                STb, qsc, wrow, sm = STb2[par], qsc2[par], wrow2[par], sm2[par]
                cs_ = slice(c * L, (c + 1) * L)
                E('dve', lambda: nc.vector.tensor_scalar(out=igrep.t[0:L, :], in0=onesf.t[0:L, :], scalar1=igc.t[0:L, c, hd:hd + 1], scalar2=None, op0=ALU.mult),
                  reads=(onesf, igc), writes=(igrep,))
                E('dve', lambda: nc.vector.tensor_scalar(out=nlfrep.t[0:L, :], in0=onesf.t[0:L, :], scalar1=nlf.t[0:L, c, hd:hd + 1], scalar2=None, op0=ALU.mult),
                  reads=(onesf, nlf), writes=(nlfrep,))
                E('pe', lambda: nc.tensor.matmul(PA.t[:, 0:L], lhsT=igrep.t[0:L, :], rhs=ident[0:L, 0:L], start=True, stop=False), reads=(igrep, cst), writes=(PA,))
                E('pe', lambda: nc.tensor.matmul(PA.t[:, 0:L], lhsT=nlfrep.t[0:L, :], rhs=tri[0:L, 0:L], start=False, stop=True), reads=(nlfrep, cst), writes=(PA,))
                E('pe', lambda: nc.tensor.matmul(PA.t[:, 256:257], lhsT=nlfrep.t[0:L, :], rhs=onesf.t[0:L, 0:1], start=True, stop=True), reads=(nlfrep, onesf), writes=(PA,))
                E('dve', lambda: nc.vector.tensor_tensor_scan(out=grow.t[:, 0:L], data0=PA.t[:, 0:L], data1=negbig.t[:, 0:L], initial=m0, op0=ALU.max, op1=ALU.max),
                  reads=(PA, negbig, mst), writes=(grow,))
                E('dve', lambda: nc.vector.scalar_tensor_tensor(out=junk.t[0:L, 0:L], in0=grow.t[0:L, 0:L], scalar=1.0, in1=ident[0:L, 0:L], op0=ALU.mult, op1=ALU.mult,
                                                                accum_out=sm.t[0:L, 0:1]), reads=(grow, cst), writes=(junk, sm))
                E('dve', lambda: nc.vector.tensor_tensor(out=dtmp.t[0:L, 0:L], in0=maskb[0:L, 0:L], in1=grow.t[0:L, 0:L], op=ALU.subtract), reads=(cst, grow), writes=(dtmp,))
                E('act', lambda: nc.scalar.activation(out=DT.t[0:L, 0:L], in_=dtmp.t[0:L, 0:L], func=AF.Exp, bias=acadj.t[0:L, c, hd:hd + 1]), reads=(dtmp, acadj), writes=(DT,))
                for dc in range(2):
                    E('pe', lambda dc=dc: nc.tensor.matmul(PS.t[0:L, 0:L], lhsT=qkT.t[:, 2 + dc, cs_], rhs=qkT.t[:, dc, cs_], start=(dc == 0), stop=(dc == 1)), reads=(qkT,), writes=(PS,))
                E('dve', lambda: nc.vector.tensor_tensor(out=STb.t[0:L, 0:L], in0=PS.t[0:L, 0:L], in1=DT.t[0:L, 0:L], op=ALU.mult), reads=(PS, DT), writes=(STb,))
                E('act', lambda: nc.scalar.activation(out=wrow.t[:, 0:L], in_=grow.t[:, 0:L], func=AF.Exp, scale=-1.0, bias=m0), reads=(grow, mst), writes=(wrow,))
                for dc in range(2):
                    E('dve', lambda dc=dc: nc.vector.tensor_tensor(out=qsc.t[:, dc, 0:L], in0=qkT.t[:, dc, cs_], in1=wrow.t[:, 0:L], op=ALU.mult), reads=(qkT, wrow), writes=(qsc,))
                E('dve', lambda: nc.vector.tensor_tensor(out=sm.t[0:L, 1:2], in0=nbc.t[0:L, c, hd:hd + 1], in1=sm.t[0:L, 0:1], op=ALU.subtract), reads=(nbc, sm), writes=(sm,))
                E('act', lambda: nc.scalar.activation(out=sm.t[0:L, 2:3], in_=sm.t[0:L, 1:2], func=AF.Exp), reads=(sm,), writes=(sm,))
                E('dve', lambda: nc.vector.tensor_scalar(out=sm.t[:, 8:9], in0=grow.t[:, L - 1:L], scalar1=-1.0, scalar2=None, op0=ALU.mult), reads=(grow,), writes=(sm,))
                E('act', lambda: nc.scalar.activation(out=sm.t[0:L, 9:10], in_=acadj.t[0:L, c, hd:hd + 1], func=AF.Exp, bias=sm.t[0:L, 8:9]), reads=(acadj, sm), writes=(sm,))
                E('dve', lambda: nc.vector.tensor_tensor(out=mst.t[:, l * 8 + hd:l * 8 + hd + 1], in0=grow.t[:, L - 1:L], in1=PA.t[:, 256:257], op=ALU.subtract), reads=(grow, PA), writes=(mst,))

            def P2(c):
                par = c % 2
                sm = sm2[par]
                cs_ = slice(c * L, (c + 1) * L)
                for dc in range(2):
                    E('pe', lambda dc=dc: nc.tensor.transpose(out=PX.t[0:L, 256 + dc * 128:256 + (dc + 1) * 128], in_=qkT.t[:, 2 + dc, cs_], identity=identb.t[:, :]), reads=(qkT, identb), writes=(PX,))
                E('act', lambda: nc.scalar.activation(out=ktm.t[0:L, :], in_=PX.t[0:L, 256:512], func=AF.Identity, scale=sm.t[0:L, 9:10]), reads=(PX, sm), writes=(ktm,))
                for dc in range(2):
                    E('pe', lambda dc=dc: nc.tensor.matmul(pcs[dc].t[:, 0:257], lhsT=ktm.t[0:L, dc * 128:(dc + 1) * 128], rhs=vext.t[0:L, c, :], start=True, stop=True), reads=(ktm, vext), writes=(pcs[dc],))

            def TA(c):
                par = c % 2
                STb, qsc, wrow, sm = STb2[par], qsc2[par], wrow2[par], sm2[par]
                cs_ = slice(c * L, (c + 1) * L)
                E('pe', lambda: nc.tensor.matmul(PN.t[0:L, 0:257], lhsT=STb.t[0:L, 0:L], rhs=vext.t[0:L, c, :], start=True, stop=False), reads=(STb, vext), writes=(PN,))
                for dc in range(2):
                    E('pe', lambda dc=dc: nc.tensor.matmul(PN.t[0:L, 0:257], lhsT=qsc.t[:, dc, 0:L], rhs=Cb.t[:, dc, :], start=False, stop=(dc == 1)), reads=(qsc, Cb), writes=(PN,))
                E('act', lambda: nc.scalar.activation(out=sm.t[0:L, 3:4], in_=PN.t[0:L, 256:257], func=AF.Abs), reads=(PN,), writes=(sm,))
                E('dve', lambda: nc.vector.tensor_tensor(out=sm.t[0:L, 3:4], in0=sm.t[0:L, 3:4], in1=sm.t[0:L, 2:3], op=ALU.max), reads=(sm,), writes=(sm,))
                E('act', lambda: nc.scalar.activation(out=junk2.t[0:L, 0:256], in_=PN.t[0:L, 0:256], func=AF.Square, accum_out=sm.t[0:L, 4:5]), reads=(PN,), writes=(junk2, sm))
                E('dve', lambda: nc.vector.scalar_tensor_tensor(out=sm.t[0:L, 5:6], in0=sm.t[0:L, 3:4], scalar=EPS, in1=sm.t[0:L, 3:4], op0=ALU.mult, op1=ALU.mult), reads=(sm,), writes=(sm,))
                E('dve', lambda: nc.vector.scalar_tensor_tensor(out=sm.t[0:L, 6:7], in0=sm.t[0:L, 4:5], scalar=1.0 / 256, in1=sm.t[0:L, 5:6], op0=ALU.mult, op1=ALU.add), reads=(sm,), writes=(sm,))
                E('act', lambda: nc.scalar.activation(out=sm.t[0:L, 6:7], in_=sm.t[0:L, 6:7], func=AF.Ln), reads=(sm,), writes=(sm,))
                E('act', lambda: nc.scalar.activation(out=sm.t[0:L, 7:8], in_=sm.t[0:L, 6:7], func=AF.Exp, scale=-0.5), reads=(sm,), writes=(sm,))
                E('dve', lambda: nc.vector.scalar_tensor_tensor(out=ytm.t[0:L, :], in0=PN.t[0:L, 0:256], scalar=sm.t[0:L, 7:8], in1=sigo.t[0:L, c, :], op0=ALU.mult, op1=ALU.mult),
                  reads=(PN, sm, sigo), writes=(ytm,))
                for vc in range(2):
                    E('pe', lambda vc=vc: nc.tensor.transpose(out=PX.t[:, vc * 128:vc * 128 + L], in_=ytm.t[0:L, vc * 128:(vc + 1) * 128], identity=identb.t[0:L, 0:L]), reads=(ytm, identb), writes=(PX,))
                    E('act', lambda vc=vc: nc.scalar.activation(out=yaT.t[:, hd * 2 + vc, cs_], in_=PX.t[:, vc * 128:vc * 128 + L], func=AF.Identity, scale=V(l, V_MAN, hd * 2 + vc)),
                      reads=(PX, vecs), writes=(yaT,))

            def TB(c):
                wrow = wrow2[c % 2]
                for dc in range(2):
                    E('dve', lambda dc=dc: nc.vector.scalar_tensor_tensor(out=Cf.t[:, dc, :], in0=Cf.t[:, dc, :], scalar=wrow.t[:, L - 1:L], in1=pcs[dc].t[:, 0:257], op0=ALU.mult, op1=ALU.add),
                      reads=(Cf, wrow, pcs[dc]), writes=(Cf,))
                E('act', lambda: nc.scalar.activation(out=Cb.t[:], in_=Cf.t[:], func=AF.Copy), reads=(Cf,), writes=(Cb,))

            P1(0); P2(0)
            for c in range(nch):
                if c + 1 < nch:
                    P1(c + 1)
                TA(c)
                TB(c)
                if c + 1 < nch:
                    P2(c + 1)
            spdma(O[g]["C"][l, hd].rearrange("(dc p) v -> p dc v", p=128), Cf.t[:, :, 0:256], reads=(Cf,))
            spdma(O[g]["n"][l, hd], Cf.t[:, :, 256], reads=(Cf,))

        def hgrn_group(l, hp, T, Lh, nch, g, first, sq):
            s = wnext(("w_in", l, 0, ((8192 + hp * 256, 256), (10240 + hp * 256, 256)), False))
            for hh in range(2):
                ps = nextpd(); fm_group(ps, s, hh, hT, T)
                E('act', lambda hh=hh, ps=ps: nc.scalar.activation(out=uqk.t[:, hh, 0:T], in_=ps.t[:, 0:T], func=AF.Silu, bias=V(l, V_BIN, 64 + hp * 2 + hh)), reads=(ps, vecs), writes=(uqk,))
            for hh in range(2):
                ps = nextpd(); fm_group(ps, s, 2 + hh, hT, T)
                E('act', lambda hh=hh, ps=ps: nc.scalar.activation(out=uqk.t[:, 2 + hh, 0:T], in_=ps.t[:, 0:T], func=AF.Sigmoid, bias=V(l, V_BIN, 80 + hp * 2 + hh)), reads=(ps, vecs), writes=(uqk,))
            s = wnext(("w_in", l, 0, ((12288 + hp * 256, 256), (14336 + hp * 256, 256)), True))
            for hh in range(2):
                ps = nextpd(); fm_group(ps, s, hh, hT, T)
                E('act', lambda hh=hh, ps=ps: nc.scalar.activation(out=sqb.t[:, 0:T], in_=ps.t[:, 0:T], func=AF.Identity, bias=V(l, V_BIN, 96 + hp * 2 + hh)), reads=(ps, vecs), writes=(sqb,))
                for c in range(nch):
                    E('pe', lambda c=c: nc.tensor.transpose(out=PX.t[0:Lh, (c % 8) * 128:(c % 8 + 1) * 128], in_=sqb.t[:, c * Lh:(c + 1) * Lh], identity=identb.t[:, :]), reads=(sqb, identb), writes=(PX,))
                    if c % 8 == 7 or c == nch - 1:
                        c0 = (c // 8) * 8
                        nn = c - c0 + 1
                        E('act', lambda c0=c0, nn=nn, hh=hh: nc.scalar.activation(out=vtm.t[0:Lh, c0:c0 + nn, hh * 128:(hh + 1) * 128],
                                                                                 in_=PX.t[0:Lh, 0:nn * 128].rearrange("p (c v) -> p c v", v=128), func=AF.Copy), reads=(PX,), writes=(vtm,))
            for hh in range(2):
                ps = nextpd(); fm_group(ps, s, 2 + hh, hT, T)
                E('act', lambda hh=hh, ps=ps: nc.scalar.activation(out=qkT.t[:, hh, 0:T], in_=ps.t[:, 0:T], func=AF.Sigmoid, bias=V(l, V_BIN, 112 + hp * 2 + hh)), reads=(ps, vecs), writes=(qkT,))
            qth = qkT.t[:, 2, :]; kth = qkT.t[:, 3, :]
            lfh = cacc; kkh = tmpf
            for hh in range(2):
                hd = hp * 2 + hh
                if first and g == "p":
                    E('dve', lambda: nc.vector.memset(Sf.t[:], 0.0), writes=(Sf,))
                else:
                    spdma(Sf.t[:], (Ss_in if first else O[g]["S"])[l, hd], writes=(Sf,))
                E('act', lambda: nc.scalar.activation(out=Sb.t[:], in_=Sf.t[:], func=AF.Copy), reads=(Sf,), writes=(Sb,))
                E('act', lambda hh=hh, hd=hd: nc.scalar.activation(out=lfh.t[:, 0:T], in_=uqk.t[:, 2 + hh, 0:T], func=AF.Ln, scale=omlt.t[:, l, hd:hd + 1], bias=lbt.t[:, l, hd:hd + 1]),
                  reads=(uqk, omlt, lbt), writes=(lfh,))
                E('dve', lambda hh=hh, hd=hd: nc.vector.tensor_scalar(out=kkh.t[:, 0:T], in0=uqk.t[:, 2 + hh, 0:T], scalar1=nomlt.t[:, l, hd:hd + 1], scalar2=omlt.t[:, l, hd:hd + 1], op0=ALU.mult, op1=ALU.add),
                  reads=(uqk, nomlt, omlt), writes=(kkh,))
                E('dve', lambda: nc.vector.tensor_tensor_scan(out=bh.t[:, 0:T], data0=rmask.t[:, 0:T], data1=lfh.t[:, 0:T], initial=0.0, op0=ALU.mult, op1=ALU.add), reads=(rmask, lfh), writes=(bh,))
                E('act', lambda: nc.scalar.activation(out=ebh.t[:, 0:T], in_=bh.t[:, 0:T], func=AF.Exp), reads=(bh,), writes=(ebh,))
                E('act', lambda: nc.scalar.activation(out=enbh.t[:, 0:T], in_=bh.t[:, 0:T], func=AF.Exp, scale=-1.0), reads=(bh,), writes=(enbh,))
                E('dve', lambda hh=hh: nc.vector.tensor_tensor(out=qth[:, 0:T], in0=uqk.t[:, hh, 0:T], in1=ebh.t[:, 0:T], op=ALU.mult), reads=(uqk, ebh), writes=(qkT,))
                E('dve', lambda: nc.vector.tensor_tensor(out=kth[:, 0:T], in0=kkh.t[:, 0:T], in1=enbh.t[:, 0:T], op=ALU.mult), reads=(kkh, enbh), writes=(qkT,))
                for c in range(nch):
                    cs_ = slice(c * Lh, (c + 1) * Lh)
                    E('pe', lambda cs_=cs_: nc.tensor.matmul(PS.t[0:Lh, cs_], lhsT=kth[:, cs_], rhs=qth[:, cs_], start=True, stop=True), reads=(qkT,), writes=(PS,))
                E('dve', lambda: nc.vector.tensor_tensor(out=Abf.t[0:Lh, 0:T], in0=PS.t[0:Lh, 0:T], in1=tri8[0:Lh, 0:T], op=ALU.mult), reads=(PS, cst), writes=(Abf,))
                for c in range(nch):
                    cs_ = slice(c * Lh, (c + 1) * Lh)
                    ce = (c + 1) * Lh - 1
                    E('dve', lambda cs_=cs_, ce=ce: nc.vector.tensor_scalar(out=khT.t[:, cs_], in0=kth[:, cs_], scalar1=ebh.t[:, ce:ce + 1], scalar2=None, op0=ALU.mult), reads=(qkT, ebh), writes=(khT,))
                    E('pe', lambda cs_=cs_, c=c: nc.tensor.transpose(out=PX.t[0:Lh, (c % 8) * 128:(c % 8 + 1) * 128], in_=khT.t[:, cs_], identity=identb.t[:, :]), reads=(khT, identb), writes=(PX,))
                    if c % 8 == 7 or c == nch - 1:
                        c0 = (c // 8) * 8
                        nn = c - c0 + 1
                        E('act', lambda c0=c0, nn=nn: nc.scalar.activation(out=khtm.t[0:Lh, c0 * 128:(c0 + nn) * 128], in_=PX.t[0:Lh, 0:nn * 128], func=AF.Copy), reads=(PX,), writes=(khtm,))
                ubanks = [P0, P1, PC, PT]
                for c in range(nch):
                    vsl = vtm.t[0:Lh, c, hh * 128:(hh + 1) * 128]
                    ub = ubanks[c // 4]
                    E('pe', lambda c=c, vsl=vsl, ub=ub: nc.tensor.matmul(ub.t[:, (c % 4) * 128:(c % 4 + 1) * 128], lhsT=khtm.t[0:Lh, c * 128:(c + 1) * 128], rhs=vsl, start=True, stop=True), reads=(khtm, vtm), writes=(ub,))
                for c in range(nch):
                    ce = (c + 1) * Lh - 1
                    ub = ubanks[c // 4]
                    E('dve', lambda c=c, ce=ce, ub=ub: nc.vector.scalar_tensor_tensor(out=Sf.t[:], in0=Sf.t[:], scalar=ebh.t[:, ce:ce + 1], in1=ub.t[:, (c % 4) * 128:(c % 4 + 1) * 128], op0=ALU.mult, op1=ALU.add), reads=(Sf, ebh, ub), writes=(Sf,))
                    if c < nch - 1:
                        E('act', lambda c=c: nc.scalar.activation(out=Sbc[c + 1].t, in_=Sf.t[:], func=AF.Copy), reads=(Sf,), writes=(Sbc[c + 1],))
                for c in range(nch):
                    cs_ = slice(c * Lh, (c + 1) * Lh)
                    vsl = vtm.t[0:Lh, c, hh * 128:(hh + 1) * 128]
                    sbc = Sb if c == 0 else Sbc[c]
                    sbap = Sb.t[:, :] if c == 0 else Sbc[c].t
                    E('pe', lambda cs_=cs_, vsl=vsl: nc.tensor.matmul(PN.t[:, cs_], lhsT=vsl, rhs=Abf.t[0:Lh, cs_], start=True, stop=False), reads=(vtm, Abf), writes=(PN,))
                    E('pe', lambda cs_=cs_, sbap=sbap: nc.tensor.matmul(PN.t[:, cs_], lhsT=sbap, rhs=qth[:, cs_], start=False, stop=True), reads=(sbc, qkT), writes=(PN,))
                spdma(O[g]["S"][l, hd], Sf.t[:], reads=(Sf,))
                E('act', lambda: nc.scalar.activation(out=sqb.t[:, 0:T], in_=PN.t[:, 0:T], func=AF.Square), reads=(PN,), writes=(sqb,))
                E('pe', lambda: nc.tensor.matmul(PA.t[:, 0:T], lhsT=onesb.t[:, :], rhs=sqb.t[:, 0:T], start=True, stop=True), reads=(onesb, sqb), writes=(PA,))
                E('act', lambda: nc.scalar.activation(out=rstd.t[:, 0:T], in_=PA.t[:, 0:T], func=AF.Ln, scale=1.0 / 128, bias=epsc), reads=(PA, smc), writes=(rstd,))
                E('act', lambda: nc.scalar.activation(out=rstd.t[:, 0:T], in_=rstd.t[:, 0:T], func=AF.Exp, scale=-0.5), reads=(rstd,), writes=(rstd,))
                E('dve', lambda hd=hd: nc.vector.scalar_tensor_tensor(out=tmpf.t[:, 0:T], in0=PN.t[:, 0:T], scalar=V(l, V_HBN, hd), in1=rstd.t[:, 0:T], op0=ALU.mult, op1=ALU.mult), reads=(PN, vecs, rstd), writes=(tmpf,))
                E('dve', lambda hd=hd, hh=hh: nc.vector.tensor_tensor(out=ybT.t[:, hd, 0:T], in0=tmpf.t[:, 0:T], in1=qkT.t[:, hh, 0:T], op=ALU.mult), reads=(tmpf, qkT), writes=(ybT,))

        def layer(l, T, g, first, sq):
            Lm = min(128, T); Lh = min(32, T)
            modnorm(l, 0, sq, T, 0)
            gs = wnext(("w_in", l, 0, ((20480, 16),), True))
            mlstm_gates(l, T, Lm, T // Lm, gs)
            for hd in range(8):
                mlstm_head(l, hd, T, Lm, T // Lm, g, first, sq)
            for hp in range(8):
                hgrn_group(l, hp, T, Lh, T // Lh, g, first, sq)
            for jb in range(4):
                s = wnext(("w_in", l, 0, ((16384 + jb * 512, 512),), False))
                for jj in range(4):
                    ps = nextpd(); fm_group(ps, s, jj, hT, T)
                    E('act', lambda jj=jj, ps=ps: nc.scalar.activation(out=sg4.t[:, jj, 0:T], in_=ps.t[:, 0:T], func=AF.Sigmoid, bias=V(l, V_BIN, 128 + jb * 4 + jj)), reads=(ps, vecs), writes=(sg4,))
                s = wnext(("w_br_a", l, 0, ((jb * 512, 512),), False))
                for jj in range(4):
                    ps = nextpd(); fm_group(ps, s, jj, yaT, T)
                    E('dve', lambda jj=jj, ps=ps: nc.vector.tensor_tensor(out=uqk.t[:, jj, 0:T], in0=ps.t[:, 0:T], in1=sg4.t[:, jj, 0:T], op=ALU.mult), reads=(ps, sg4), writes=(uqk,))
                s = wnext(("w_in", l, 0, ((18432 + jb * 512, 512),), False))
                for jj in range(4):
                    ps = nextpd(); fm_group(ps, s, jj, hT, T)
                    E('act', lambda jj=jj, ps=ps: nc.scalar.activation(out=sg4.t[:, jj, 0:T], in_=ps.t[:, 0:T], func=AF.Sigmoid, bias=V(l, V_BIN, 144 + jb * 4 + jj)), reads=(ps, vecs), writes=(sg4,))
                s = wnext(("w_br_b", l, 0, ((jb * 512, 512),), False))
                for jj in range(4):
                    ps = nextpd(); fm_group(ps, s, jj, ybT, T)
                    E('dve', lambda jj=jj, ps=ps: nc.vector.tensor_tensor(out=tmpf.t[:, 0:T], in0=ps.t[:, 0:T], in1=sg4.t[:, jj, 0:T], op=ALU.mult), reads=(ps, sg4), writes=(tmpf,))
                    E('dve', lambda jj=jj: nc.vector.tensor_tensor(out=uT.t[:, jb * 4 + jj, 0:T], in0=tmpf.t[:, 0:T], in1=uqk.t[:, jj, 0:T], op=ALU.add), reads=(tmpf, uqk), writes=(uT,))
            for jb in range(4):
                s = wnext(("w_o", l, 0, ((jb * 512, 512),), False))
                for jj in range(4):
                    j = jb * 4 + jj
                    ps = nextpd(); fm_group(ps, s, jj, uT, T)
                    E('dve', lambda j=j, ps=ps: nc.vector.scalar_tensor_tensor(out=xT.t[:, j, 0:T], in0=ps.t[:, 0:T], scalar=mods.t[:, l, 32 + j, sq:sq + 1], in1=xT.t[:, j, 0:T], op0=ALU.mult, op1=ALU.add),
                      reads=(ps, mods, xT), writes=(xT,))
            modnorm(l, 1, sq, T, 48)
            for qd in range(4):
                for ub in range(4):
                    s = wnext(("w_up", l, 0, ((qd * 2048 + ub * 512, 512),), False))
                    for jj in range(4):
                        ps = nextpd(); fm_group(ps, s, jj, hT, T)
                        E('act', lambda ps=ps: nc.scalar.activation(out=sqb.t[:, 0:T], in_=ps.t[:, 0:T], func=AF.Relu), reads=(ps,), writes=(sqb,))
                        E('dve', lambda ub=ub, jj=jj: nc.vector.tensor_tensor(out=uT.t[:, ub * 4 + jj, 0:T], in0=sqb.t[:, 0:T], in1=sqb.t[:, 0:T], op=ALU.mult), reads=(sqb,), writes=(uT,))
                for ob in range(4):
                    s = wnext(("w_down", l, qd * 2048, ((ob * 512, 512),), False))
                    for jj in range(4):
                        j = ob * 4 + jj
                        ps = nextpd(); fm_group(ps, s, jj, uT, T)
                        E('dve', lambda j=j, ps=ps: nc.vector.scalar_tensor_tensor(out=xT.t[:, j, 0:T], in0=ps.t[:, 0:T], scalar=mods.t[:, l, 80 + j, sq:sq + 1], in1=xT.t[:, j, 0:T], op0=ALU.mult, op1=ALU.add),
                          reads=(ps, mods, xT), writes=(xT,))

        def final_norm(T):
            for kc in range(KC):
                E('act', lambda kc=kc: nc.scalar.activation(out=sqb.t[:, 0:T], in_=xT.t[:, kc, 0:T], func=AF.Square), reads=(xT,), writes=(sqb,))
                E('pe', lambda kc=kc: nc.tensor.matmul(PA.t[:, 0:T], lhsT=onesb.t[:, :], rhs=sqb.t[:, 0:T], start=(kc == 0), stop=(kc == KC - 1)), reads=(onesb, sqb), writes=(PA,))
            E('act', lambda: nc.scalar.activation(out=rstd.t[:, 0:T], in_=PA.t[:, 0:T], func=AF.Ln, scale=1.0 / D, bias=epsc), reads=(PA, smc), writes=(rstd,))
            E('act', lambda: nc.scalar.activation(out=rstd.t[:, 0:T], in_=rstd.t[:, 0:T], func=AF.Exp, scale=-0.5), reads=(rstd,), writes=(rstd,))
            fg0 = NL * V_PER
            for kc in range(KC):
                E('dve', lambda kc=kc: nc.vector.scalar_tensor_tensor(out=xT.t[:, kc, 0:T], in0=xT.t[:, kc, 0:T], scalar=vecs.t[:, fg0 + kc:fg0 + kc + 1], in1=rstd.t[:, 0:T], op0=ALU.mult, op1=ALU.mult),
                  reads=(xT, vecs, rstd), writes=(xT,))

        groups = []
        if NTP > 0:
            groups.append(("p", NTP, TP, 0))
        if WITH_S:
            groups.append(("s", 1, 16, 1))
        for (g, ntiles, T, sq) in groups:
            if g == "p":
                E('dve', lambda: nc.vector.memset(hist.t[:], 0.0), writes=(hist,))
                E('dve', lambda: nc.vector.memset(mst.t[:], 0.0), writes=(mst,))
            else:
                spdma(hist.t[:], convs_in, writes=(hist,))
                spdma(mst.t[:], msamp, writes=(mst,))
            for ti in range(ntiles):
                src = (xp[:, ti * T:(ti + 1) * T] if g == "p" else xs).rearrange("(kc p) t -> p kc t", p=128)
                spdma(xT.t[:, :, 0:T], src, writes=(xT,))
                for l in range(NL):
                    layer(l, T, g, ti == 0, sq)
                final_norm(T)
                dst = (yp[:, ti * T:(ti + 1) * T] if g == "p" else ys).rearrange("(kc p) t -> p kc t", p=128)
                spdma(dst, xT.t[:, :, 0:T], reads=(xT,))
            spdma(O[g]["conv"], hist.t[:], reads=(hist,))
            spdma(O[g]["m"], mst.t[0:1, :], reads=(mst,))
        for i in range(SPN):
            k.wait('sp', sp_last[i])
        assert wstate['used'] == len(PL), (wstate, len(PL))
    return nc


def _fm(v):
    return np.ascontiguousarray(v.reshape(-1, 128).T)


def _consts():
    s = np.arange(128)
    tri = (s[:, None] <= s[None, :]).astype(np.float32)
    ident = np.eye(128, dtype=np.float32)
    maskb = np.where(s[:, None] <= s[None, :], 0.0, -1e4).astype(np.float32)
    t32 = np.zeros((128, 32), np.float32)
    t32[:32] = tri[:32, :32]
    return np.ascontiguousarray(np.concatenate([tri, ident, maskb, np.tile(t32, (1, 16))], axis=1))


def make_inputs(NL, NTP, TP, inp, core):
    bp = core % 4
    f = lambda a: np.ascontiguousarray(np.asarray(a, dtype=np.float32))
    SEQP = max(NTP, 1) * TP
    m = {}
    m["xp"] = f(inp["x_prompt"][bp, :SEQP].T)
    m["xs"] = f(inp["x_sample"][core].T)
    cv = np.stack([_fm(inp["c_prompt"][bp]), _fm(inp["c_sample"][core])], axis=-1)
    m["cvec"] = f(cv)
    cols = []
    for l in range(NL):
        cols += [_fm(inp["norm1_g"][l]), _fm(inp["norm2_g"][l]), _fm(inp["ada_b"][l]), _fm(inp["b_in"][l, :20480])]
        cols += [_fm(inp["conv_w"][l, j]) for j in range(4)]
        cols += [_fm(inp["conv_b"][l]), _fm(inp["ma_norm"][l]), _fm(inp["hb_norm"][l]), _fm(inp["hgrn_lb_raw"][l])]
    cols.append(_fm(inp["final_g"]))
    m["vecs"] = f(np.concatenate(cols, axis=1))
    m["consts"] = _consts()
    m["msamp"] = f(np.broadcast_to(inp["state_mlstm_m"][:NL, core].reshape(1, NL * 8), (128, NL * 8)))
    cc = inp["cache_conv"][:NL, core]
    m["convs_in"] = f(cc.reshape(NL, 3, 32, 128).transpose(3, 0, 2, 1))
    m["ns_in"] = f(inp["state_mlstm_n"][:NL, core].reshape(NL, 8, 2, 128).transpose(0, 1, 3, 2))
    m["Cs_in"] = f(inp["state_mlstm_C"][:NL, core])
    m["Ss_in"] = f(inp["state_hgrn"][:NL, core])
    for n in ("w_in", "ada_w", "w_br_a", "w_br_b", "w_o", "w_up", "w_down", "b_in"):
        m[n] = f(inp[n][:NL])
    return m


def assemble(NL, NTP, TP, results, WITH_S=True):
    SEQP = NTP * TP
    outs = {}
    yp = np.stack([results[b]["yp"].T for b in range(4)], axis=0)
    ys = np.stack([results[c]["ys"].T for c in range(8)], axis=0)

    def grp(g, cores):
        conv = np.stack([results[c]["conv" + g].transpose(1, 3, 2, 0).reshape(NL, 3, 4096) for c in cores], axis=1)
        C = np.stack([results[c]["C" + g] for c in cores], axis=1)
        n = np.stack([results[c]["n" + g].transpose(0, 1, 3, 2).reshape(NL, 8, 256) for c in cores], axis=1)
        mm = np.stack([results[c]["m" + g].reshape(NL, 8) for c in cores], axis=1)
        S = np.stack([results[c]["S" + g] for c in cores], axis=1)
        return [np.ascontiguousarray(a, dtype=np.float32) for a in (conv, C, n, mm, S)]

    return tuple([np.ascontiguousarray(yp, dtype=np.float32), np.ascontiguousarray(ys, dtype=np.float32)] + grp("p", range(4)) + grp("s", range(8)))


def kernel(**inputs):
    NL, NTP, TP = 4, 8, 512
    inp = {k_: np.asarray(v) for k_, v in inputs.items()}
    nc = build(NL, NTP, True, TP)
    in_maps = [make_inputs(NL, NTP, TP, inp, c) for c in range(8)]
    res = run_bass_kernel_spmd(nc, in_maps, core_ids=list(range(8)))
    return assemble(NL, NTP, TP, res.results)
```

```python
import math
from contextlib import ExitStack
import numpy as np
import concourse.bass as bass
import concourse.mybir as mybir
from concourse.bass_utils import run_bass_kernel_spmd

F32 = mybir.dt.float32
BF16 = mybir.dt.bfloat16
AF = mybir.ActivationFunctionType
ALU = mybir.AluOpType

D = 2048
KC = 16
N_IN = 20496
EPS = 1e-6
NSLOT = 2
LN16 = math.log(1.0 / 16.0)
V_N1G, V_N2G, V_ADAB, V_BIN, V_CW, V_CB, V_MAN, V_HBN, V_LBR = 0, 16, 32, 128, 288, 416, 448, 464, 480
V_PER = 496


class Buf:
    def __init__(self, t):
        self.t = t
        self.w = None
        self.r = {}

    def __getitem__(self, k):
        return self.t[k]


class K:
    def __init__(self, nc, es):
        self.nc = nc
        self.es = es
        self.engs = {'pe': nc.tensor, 'act': nc.scalar, 'dve': nc.vector, 'pool': nc.gpsimd, 'sp': nc.sync}
        self.sem = {}
        self.cnt = {}
        self.seen = {e: {} for e in self.engs}
        for e in ('pe', 'act', 'dve', 'pool'):
            self.newsem(e)

    def newsem(self, name):
        self.sem[name] = self.es.enter_context(self.nc.semaphore(name))
        self.cnt[name] = 0

    def sb(self, name, shape, dt):
        return Buf(self.es.enter_context(self.nc.sbuf_tensor("s_" + name, shape, dt)))

    def ps(self, name, shape, dt):
        return Buf(self.es.enter_context(self.nc.psum_tensor("p_" + name, shape, dt)))

    def wait(self, eng, tok):
        if tok is None:
            return
        k, v = tok
        if self.seen[eng].get(k, 0) >= v:
            return
        self.engs[eng].wait_ge(self.sem[k], v)
        self.seen[eng][k] = v

    def deps(self, eng, reads, writes):
        for b in reads:
            if b.w is not None and not (eng == 'pe' and b.w[0] == 'pe'):
                self.wait(eng, b.w)
        for b in writes:
            if b.w is not None and not (eng == 'pe' and b.w[0] == 'pe'):
                self.wait(eng, b.w)
            for k, t in b.r.items():
                if not (eng == 'pe' and t[0] == 'pe'):
                    self.wait(eng, t)

    def mark(self, tok, reads, writes):
        for b in reads:
            b.r[tok[0]] = tok
        for b in writes:
            b.w = tok
            b.r = {}

    def E(self, eng, fn, reads=(), writes=()):
        self.deps(eng, reads, writes)
        inst = fn()
        self.cnt[eng] += 1
        inst.then_inc(self.sem[eng], 1)
        tok = (eng, self.cnt[eng])
        self.mark(tok, reads, writes)
        return tok

    def DMA(self, eng, semname, fns, reads=(), writes=()):
        self.deps(eng, reads, writes)
        for fn in fns:
            inst = fn()
            self.cnt[semname] += 16
            inst.then_inc(self.sem[semname], 16)
        tok = (semname, self.cnt[semname])
        self.mark(tok, reads, writes)
        return tok


def build(NL, NTP, WITH_S, TP=512):
    SEQP = max(NTP, 1) * TP
    nc = bass.Bass("TRN2", target_bir_lowering=False)

    def din(name, shape):
        return nc.dram_tensor(name, shape, F32, kind="ExternalInput").ap()

    def dout(name, shape):
        return nc.dram_tensor(name, shape, F32, kind="ExternalOutput").ap()

    xp = din("xp", [D, SEQP]); xs = din("xs", [D, 16]); cvec = din("cvec", [128, KC, 2])
    vecs_d = din("vecs", [128, NL * V_PER + 16]); consts_d = din("consts", [128, 896])
    msamp = din("msamp", [128, NL * 8]); convs_in = din("convs_in", [128, NL, 32, 3])
    ns_in = din("ns_in", [NL, 8, 128, 2]); Cs_in = din("Cs_in", [NL, 8, 256, 256]); Ss_in = din("Ss_in", [NL, 16, 128, 128])
    Wd = {"w_in": din("w_in", [NL, D, N_IN]), "ada_w": din("ada_w", [NL, D, 6 * D]),
          "w_br_a": din("w_br_a", [NL, D, D]), "w_br_b": din("w_br_b", [NL, D, D]), "w_o": din("w_o", [NL, D, D]),
          "w_up": din("w_up", [NL, D, 4 * D]), "w_down": din("w_down", [NL, 4 * D, D])}
    b_in_d = din("b_in", [NL, N_IN])
    yp = dout("yp", [D, SEQP]); ys = dout("ys", [D, 16])
    O = {}
    for g in ("p", "s"):
        O[g] = dict(conv=dout("conv" + g, [128, NL, 32, 3]), C=dout("C" + g, [NL, 8, 256, 256]),
                    n=dout("n" + g, [NL, 8, 128, 2]), m=dout("m" + g, [1, NL * 8]), S=dout("S" + g, [NL, 16, 128, 128]))

    with ExitStack() as es:
        es.enter_context(nc.allow_non_contiguous_dma(reason="small strided state vectors"))
        k = K(nc, es)
        E, DMA = k.E, k.DMA
        xT = k.sb("xT", [128, KC, TP], F32)
        hT = k.sb("hT", [128, KC, TP], BF16)
        yaT = k.sb("yaT", [128, KC, TP], BF16)
        ybT = k.sb("ybT", [128, KC, TP], BF16)
        uT = k.sb("uT", [128, KC, TP], BF16)
        slots = [k.sb("ws%d" % i, [128, KC + 1, 512], BF16) for i in range(NSLOT)]
        for i in range(NSLOT):
            k.newsem("wsem%d" % i)
        vecs = k.sb("vecs", [128, NL * V_PER + 16], F32)
        cst = k.sb("cst", [128, 896], F32)
        tri = cst.t[:, 0:128]; ident = cst.t[:, 128:256]; maskb = cst.t[:, 256:384]; tri8 = cst.t[:, 384:896]
        identb = k.sb("identb", [128, 128], BF16); onesb = k.sb("onesb", [128, 128], BF16)
        onesf = k.sb("onesf", [128, 128], F32); negbig = k.sb("negbig", [128, 128], F32)
        rmask = k.sb("rmask", [128, 512], F32)
        mods = k.sb("mods", [128, NL, 96, 2], F32)
        A12 = k.sb("A12", [128, NL, 2, 16, 2], F32)
        lbt = k.sb("lbt", [128, NL, 16], F32); omlt = k.sb("omlt", [128, NL, 16], F32); nomlt = k.sb("nomlt", [128, NL, 16], F32)
        hist = k.sb("hist", [128, NL, 32, 3], F32)
        mst = k.sb("mst", [128, NL * 8], F32)
        rstd = k.sb("rstd", [128, TP], F32); tmpf = k.sb("tmpf", [128, TP], F32); sqb = k.sb("sqb", [128, TP], BF16)
        sg4 = k.sb("sg4", [128, 4, TP], BF16)
        uqk = k.sb("uqk", [128, 4, TP + 3], F32); cacc = k.sb("cacc", [128, TP], F32)
        qkT = k.sb("qkT", [128, 4, TP], BF16)
        vext = k.sb("vext", [128, 4, 257], BF16); sigo = k.sb("sigo", [128, 4, 256], BF16)
        igc = k.sb("igc", [128, 4, 8], F32); nlf = k.sb("nlf", [128, 4, 8], F32); nbc = k.sb("nbc", [128, 4, 8], F32)
        acadj = k.sb("acadj", [128, 4, 8], F32); gtmp = k.sb("gtmp", [128, 8], F32)
        igrep = k.sb("igrep", [128, 128], F32); nlfrep = k.sb("nlfrep", [128, 128], F32)
        grow = k.sb("grow", [128, 128], F32); wrow2 = [k.sb("wrow%d" % i, [128, 128], F32) for i in range(2)]
        dtmp = k.sb("dtmp", [128, 128], F32); DT = k.sb("DT", [128, 128], F32); STb2 = [k.sb("STb%d" % i, [128, 128], BF16) for i in range(2)]
        junk = dtmp; junk2 = k.sb("junk2", [128, 256], BF16)
        qsc2 = [k.sb("qsc%d" % i, [128, 2, 128], BF16) for i in range(2)]; ytm = k.sb("ytm", [128, 256], BF16); ktm = k.sb("ktm", [128, 256], BF16)
        sm2 = [k.sb("sm%d" % i, [128, 16], F32) for i in range(2)]
        Cf = k.sb("Cf", [128, 2, 257], F32); Cb = k.sb("Cb", [128, 2, 257], BF16)
        vtm = k.sb("vtm", [32, 16, 256], BF16)
        bh = rstd
        ebh = k.sb("ebh", [128, TP], F32); enbh = k.sb("enbh", [128, TP], F32)
        khT = k.sb("khT", [128, TP], BF16)
        Abf = k.sb("Abf", [32, TP], BF16); khtm = k.sb("khtm", [32, 16 * 128], BF16)
        Sf = k.sb("Sf", [128, 128], F32); Sb = k.sb("Sb", [128, 128], BF16)
        Sbc = [None] + [Buf(sg4.t[:, c // 4, (c % 4) * 128:(c % 4 + 1) * 128]) for c in range(1, 16)]
        P0 = k.ps("P0", [128, 512], F32); P1 = k.ps("P1", [128, 512], F32); PT = k.ps("PT", [128, 512], F32)
        PS = k.ps("PS", [128, 512], F32); PN = k.ps("PN", [128, 512], F32); PA = k.ps("PA", [128, 512], F32)
        PX = k.ps("PX", [128, 1024], BF16); PC = k.ps("PC", [128, 512], F32)
        PD = [P0, P1]
        pdi = [0]
        SPN = 8
        for i in range(SPN):
            k.newsem("sp%d" % i)
        spi = [0]
        sp_last = [None] * SPN

        def spdma(out_ap, in_ap, reads=(), writes=()):
            i = spi[0] % SPN
            spi[0] += 1
            k.wait('sp', sp_last[i])
            tok = DMA('sp', "sp%d" % i, [lambda: nc.sync.dma_start(out=out_ap, in_=in_ap)], reads, writes)
            sp_last[i] = tok
            return tok

        def plan():
            pl = []
            for l in range(NL):
                for jb in range(24):
                    pl.append(("ada_w", l, 0, ((jb * 512, 512),), False))
            tiles = [("p", i) for i in range(NTP)] + ([("s", 0)] if WITH_S else [])
            for _ in tiles:
                for l in range(NL):
                    pl.append(("w_in", l, 0, ((20480, 16),), True))
                    for hd in range(8):
                        pl.append(("w_in", l, 0, ((hd * 256, 256), (2048 + hd * 256, 256)), False))
                        pl.append(("w_in", l, 0, ((4096 + hd * 256, 256), (6144 + hd * 256, 256)), True))
                    for hp in range(8):
                        pl.append(("w_in", l, 0, ((8192 + hp * 256, 256), (10240 + hp * 256, 256)), False))
                        pl.append(("w_in", l, 0, ((12288 + hp * 256, 256), (14336 + hp * 256, 256)), True))
                    for jb in range(4):
                        pl.append(("w_in", l, 0, ((16384 + jb * 512, 512),), False))
                        pl.append(("w_br_a", l, 0, ((jb * 512, 512),), False))
                        pl.append(("w_in", l, 0, ((18432 + jb * 512, 512),), False))
                        pl.append(("w_br_b", l, 0, ((jb * 512, 512),), False))
                    for jb in range(4):
                        pl.append(("w_o", l, 0, ((jb * 512, 512),), False))
                    for qd in range(4):
                        for ub in range(4):
                            pl.append(("w_up", l, 0, ((qd * 2048 + ub * 512, 512),), False))
                        for ob in range(4):
                            pl.append(("w_down", l, qd * 2048, ((ob * 512, 512),), False))
            return pl

        PL = plan()
        wstate = dict(issued=0, used=0)
        N_ADA = NL * 24
        NTILES = NTP + (1 if WITH_S else 0)
        PER_PASS = (len(PL) - N_ADA) // max(NTILES, 1)
        USE_SCR = NTILES > 1
        if USE_SCR:
            SCRN = 100
            wscr_t = [nc.dram_tensor("wscr%d" % i, [min(SCRN, PER_PASS - i * SCRN), 128, (KC + 1) * 512], BF16, kind="Internal").ap()
                      for i in range((PER_PASS + SCRN - 1) // SCRN)]
            wscr = lambda pidx: wscr_t[pidx // SCRN][pidx % SCRN]
            scrb = [Buf(None) for _ in range(PER_PASS)]

        def w_issue(j):
            name, l, r0, segs, bias = PL[j]
            s = slots[j % NSLOT]
            fns = []
            if USE_SCR and j >= N_ADA:
                pno, pidx = divmod(j - N_ADA, PER_PASS)
                if pno >= 1:
                    src = wscr(pidx).rearrange("p (k n) -> p k n", n=512)
                    DMA('pool', "wsem%d" % (j % NSLOT), [lambda: nc.gpsimd.dma_start(out=s.t[:, :, :], in_=src)], reads=(scrb[pidx],), writes=(s,))
                    return
            c = 0
            for (c0, n) in segs:
                src = Wd[name][l, r0:r0 + D, c0:c0 + n].rearrange("(kc p) n -> p kc n", p=128)
                dst = s.t[:, 0:KC, c:c + n]
                fns.append(lambda src=src, dst=dst: nc.gpsimd.dma_start(out=dst, in_=src))
                if bias:
                    bsrc = b_in_d[l:l + 1, c0:c0 + n]
                    bdst = s.t[0:1, KC, c:c + n]
                    fns.append(lambda bsrc=bsrc, bdst=bdst: nc.gpsimd.dma_start(out=bdst, in_=bsrc))
                c += n
            DMA('pool', "wsem%d" % (j % NSLOT), fns, reads=(), writes=(s,))
            if USE_SCR and j >= N_ADA:
                pidx = (j - N_ADA) % PER_PASS
                spdma(wscr(pidx).rearrange("p (k n) -> p k n", n=512), s.t[:, :, :], reads=(s,), writes=(scrb[pidx],))

        def wnext(desc):
            j = wstate['used']
            assert PL[j] == desc, (j, PL[j], desc)
            while wstate['issued'] < len(PL) and wstate['issued'] <= j + NSLOT - 1:
                w_issue(wstate['issued'])
                wstate['issued'] += 1
            wstate['used'] += 1
            return slots[j % NSLOT]

        def nextpd():
            p = PD[pdi[0] % 2]
            pdi[0] += 1
            return p

        def fm_group(ps, slot, col, act, T, nk=KC):
            for kc in range(nk):
                E('pe', lambda kc=kc: nc.tensor.matmul(ps.t[:, 0:T], lhsT=slot.t[:, kc, col * 128:(col + 1) * 128],
                                                       rhs=act.t[:, kc, 0:T], start=(kc == 0), stop=(kc == nk - 1)),
                  reads=(slot, act), writes=(ps,))

        spdma(vecs.t[:], vecs_d, writes=(vecs,))
        spdma(cst.t[:], consts_d, writes=(cst,))
        E('dve', lambda: nc.vector.tensor_copy(out=identb.t[:], in_=ident), reads=(cst,), writes=(identb,))
        E('dve', lambda: nc.vector.memset(onesb.t[:], 1.0), writes=(onesb,))
        E('dve', lambda: nc.vector.memset(onesf.t[:], 1.0), writes=(onesf,))
        E('dve', lambda: nc.vector.memset(negbig.t[:], -1e30), writes=(negbig,))
        E('dve', lambda: nc.vector.memset(rmask.t[:], 1.0), writes=(rmask,))
        for c in range(16):
            E('dve', lambda c=c: nc.vector.memset(rmask.t[:, c * 32:c * 32 + 1], 0.0), writes=(rmask,))
        E('dve', lambda: nc.vector.memset(vext.t[:, :, 256:257], 1.0), writes=(vext,))

        def V(l, off, j=0, n=1):
            return vecs.t[:, l * V_PER + off + j: l * V_PER + off + j + n]

        lbe = Buf(tmpf.t[:, 0:NL * 16].rearrange("p (l f) -> p l f", f=16)); lbm = Buf(tmpf.t[:, 64:80]); lbs = Buf(tmpf.t[:, 80:96])
        E('dve', lambda: nc.vector.tensor_copy(out=lbm.t[:], in_=V(0, V_LBR, 0, 16)), reads=(vecs,), writes=(lbm,))
        for l in range(1, NL):
            E('dve', lambda l=l: nc.vector.tensor_tensor(out=lbm.t[:], in0=lbm.t[:], in1=V(l, V_LBR, 0, 16), op=ALU.max), reads=(vecs, lbm), writes=(lbm,))
        for l in range(NL):
            E('dve', lambda l=l: nc.vector.tensor_tensor(out=lbe.t[:, l, :], in0=V(l, V_LBR, 0, 16), in1=lbm.t[:], op=ALU.subtract), reads=(vecs, lbm), writes=(lbe,))
            E('act', lambda l=l: nc.scalar.activation(out=lbe.t[:, l, :], in_=lbe.t[:, l, :], func=AF.Exp), reads=(lbe,), writes=(lbe,))
        E('dve', lambda: nc.vector.tensor_copy(out=lbs.t[:], in_=lbe.t[:, 0, :]), reads=(lbe,), writes=(lbs,))
        for l in range(1, NL):
            E('dve', lambda l=l: nc.vector.tensor_tensor(out=lbs.t[:], in0=lbs.t[:], in1=lbe.t[:, l, :], op=ALU.add), reads=(lbe, lbs), writes=(lbs,))
        E('dve', lambda: nc.vector.reciprocal(out=lbs.t[:], in_=lbs.t[:]), reads=(lbs,), writes=(lbs,))
        for l in range(NL):
            E('dve', lambda l=l: nc.vector.tensor_tensor(out=lbe.t[:, l, :], in0=lbe.t[:, l, :], in1=lbs.t[:], op=ALU.mult), reads=(lbe, lbs), writes=(lbe,))
        E('dve', lambda: nc.vector.memset(lbt.t[:, 0, :], 0.0), writes=(lbt,))
        for l in range(1, NL):
            E('dve', lambda l=l: nc.vector.tensor_tensor(out=lbt.t[:, l, :], in0=lbt.t[:, l - 1, :], in1=lbe.t[:, l, :], op=ALU.add), reads=(lbe, lbt), writes=(lbt,))
        E('dve', lambda: nc.vector.tensor_scalar(out=omlt.t[:], in0=lbt.t[:], scalar1=-1.0, scalar2=1.0, op0=ALU.mult, op1=ALU.add), reads=(lbt,), writes=(omlt,))
        E('dve', lambda: nc.vector.tensor_scalar(out=nomlt.t[:], in0=omlt.t[:], scalar1=-1.0, scalar2=None, op0=ALU.mult), reads=(omlt,), writes=(nomlt,))

        csf = Buf(tmpf.t[:, 96:128].rearrange("p (k s) -> p k s", s=2)); csb = k.sb("csb", [128, KC, 2], BF16)
        spdma(csf.t[:], cvec, writes=(csf,))
        E('act', lambda: nc.scalar.activation(out=csb.t[:], in_=csf.t[:], func=AF.Silu), reads=(csf,), writes=(csb,))
        for l in range(NL):
            for jb in range(24):
                s = wnext(("ada_w", l, 0, ((jb * 512, 512),), False))
                for jj in range(4):
                    j = jb * 4 + jj
                    for kc in range(KC):
                        E('pe', lambda kc=kc, jj=jj, j=j, s=s: nc.tensor.matmul(PA.t[:, 2 * j:2 * j + 2], lhsT=s.t[:, kc, jj * 128:(jj + 1) * 128],
                                                                                rhs=csb.t[:, kc, :], start=(kc == 0), stop=(kc == KC - 1)),
                          reads=(s, csb), writes=(PA,))
            for sq in range(2):
                pav = PA.t[:, 0:192].rearrange("p (j s) -> p j s", s=2)[:, :, sq]
                E('dve', lambda l=l, sq=sq, pav=pav: nc.vector.tensor_tensor(out=mods.t[:, l, :, sq], in0=pav, in1=V(l, V_ADAB, 0, 96), op=ALU.add),
                  reads=(PA, vecs), writes=(mods,))
                for which, (goff, scoff) in enumerate(((V_N1G, 16), (V_N2G, 64))):
                    E('dve', lambda l=l, sq=sq, which=which, goff=goff, scoff=scoff: nc.vector.scalar_tensor_tensor(
                        out=A12.t[:, l, which, :, sq], in0=mods.t[:, l, scoff:scoff + 16, sq], scalar=1.0, in1=V(l, goff, 0, 16),
                        op0=ALU.add, op1=ALU.mult), reads=(mods, vecs), writes=(A12,))

        def modnorm(l, which, sq, T, shoff):
            for kc in range(KC):
                E('act', lambda kc=kc: nc.scalar.activation(out=sqb.t[:, 0:T], in_=xT.t[:, kc, 0:T], func=AF.Square), reads=(xT,), writes=(sqb,))
                E('pe', lambda kc=kc: nc.tensor.matmul(PA.t[:, 0:T], lhsT=onesb.t[:, :], rhs=sqb.t[:, 0:T], start=(kc == 0), stop=(kc == KC - 1)),
                  reads=(onesb, sqb), writes=(PA,))
            E('act', lambda: nc.scalar.activation(out=rstd.t[:, 0:T], in_=PA.t[:, 0:T], func=AF.Ln, scale=1.0 / D, bias=epsc), reads=(PA, smc), writes=(rstd,))
            E('act', lambda: nc.scalar.activation(out=rstd.t[:, 0:T], in_=rstd.t[:, 0:T], func=AF.Exp, scale=-0.5), reads=(rstd,), writes=(rstd,))
            for kc in range(KC):
                E('dve', lambda kc=kc: nc.vector.tensor_tensor(out=tmpf.t[:, 0:T], in0=xT.t[:, kc, 0:T], in1=rstd.t[:, 0:T], op=ALU.mult), reads=(xT, rstd), writes=(tmpf,))
                E('act', lambda kc=kc: nc.scalar.activation(out=hT.t[:, kc, 0:T], in_=tmpf.t[:, 0:T], func=AF.Identity,
                                                            scale=A12.t[:, l, which, kc, sq:sq + 1], bias=mods.t[:, l, shoff + kc, sq:sq + 1]),
                  reads=(tmpf, A12, mods), writes=(hT,))

        smc = k.sb("smc", [128, 4], F32)
        E('dve', lambda: nc.vector.memset(smc.t[:, 0:1], EPS), writes=(smc,))
        E('dve', lambda: nc.vector.memset(smc.t[:, 1:2], 1.0), writes=(smc,))
        E('dve', lambda: nc.vector.memset(smc.t[:, 2:3], 0.0), writes=(smc,))
        epsc = smc.t[:, 0:1]; onec = smc.t[:, 1:2]

        def mlstm_gates(l, T, L, nch, gs):
            for c in range(nch):
                for kc in range(KC):
                    E('pe', lambda kc=kc, c=c: nc.tensor.matmul(PT.t[0:L, 0:16], lhsT=hT.t[:, kc, c * L:(c + 1) * L], rhs=gs.t[:, kc, 0:16], start=(kc == 0), stop=False),
                      reads=(hT, gs), writes=(PT,))
                E('pe', lambda: nc.tensor.matmul(PT.t[0:L, 0:16], lhsT=onesb.t[0:1, 0:L], rhs=gs.t[0:1, KC, 0:16], start=False, stop=True), reads=(onesb, gs), writes=(PT,))
                E('dve', lambda c=c: nc.vector.tensor_copy(out=igc.t[0:L, c, :], in_=PT.t[0:L, 0:8]), reads=(PT,), writes=(igc,))
                E('act', lambda c=c: nc.scalar.activation(out=gtmp.t[0:L, :], in_=PT.t[0:L, 8:16], func=AF.Exp, scale=-1.0), reads=(PT,), writes=(gtmp,))
                E('act', lambda c=c: nc.scalar.activation(out=nlf.t[0:L, c, :], in_=gtmp.t[0:L, :], func=AF.Ln, bias=onec[0:L, :]), reads=(gtmp, smc), writes=(nlf,))
                E('pe', lambda c=c: nc.tensor.matmul(PA.t[0:L, 0:8], lhsT=tri[0:L, 0:L], rhs=nlf.t[0:L, c, :], start=True, stop=True), reads=(cst, nlf), writes=(PA,))
                E('dve', lambda c=c: nc.vector.tensor_copy(out=nbc.t[0:L, c, :], in_=PA.t[0:L, 0:8]), reads=(PA,), writes=(nbc,))
                E('dve', lambda c=c: nc.vector.tensor_tensor(out=acadj.t[0:L, c, :], in0=igc.t[0:L, c, :], in1=nbc.t[0:L, c, :], op=ALU.add), reads=(igc, nbc), writes=(acadj,))
                E('dve', lambda c=c: nc.vector.tensor_scalar(out=acadj.t[0:L, c, :], in0=acadj.t[0:L, c, :], scalar1=LN16, scalar2=None, op0=ALU.add), reads=(acadj,), writes=(acadj,))

        def mlstm_head(l, hd, T, L, nch, g, first, sq):
            if first and g == "p":
                E('dve', lambda: nc.vector.memset(Cf.t[:], 0.0), writes=(Cf,))
            else:
                srcC = (Cs_in if first else O[g]["C"])[l, hd].rearrange("(dc p) v -> p dc v", p=128)
                srcn = (ns_in if first else O[g]["n"])[l, hd]
                spdma(Cf.t[:, :, 0:256], srcC, writes=(Cf,))
                spdma(Cf.t[:, :, 256], srcn, writes=(Cf,))
            E('act', lambda: nc.scalar.activation(out=Cb.t[:], in_=Cf.t[:], func=AF.Copy), reads=(Cf,), writes=(Cb,))
            s = wnext(("w_in", l, 0, ((hd * 256, 256), (2048 + hd * 256, 256)), False))
            for b4 in range(4):
                fb = (hd * 2 + b4) if b4 < 2 else (16 + hd * 2 + (b4 - 2))
                ps = nextpd()
                fm_group(ps, s, b4, hT, T)
                E('act', lambda b4=b4, fb=fb, ps=ps: nc.scalar.activation(out=uqk.t[:, b4, 3:3 + T], in_=ps.t[:, 0:T], func=AF.Identity, bias=V(l, V_BIN, fb)),
                  reads=(ps, vecs), writes=(uqk,))
                E('dve', lambda b4=b4, fb=fb: nc.vector.tensor_copy(out=uqk.t[:, b4, 0:3], in_=hist.t[:, l, fb, :]), reads=(hist,), writes=(uqk,))
                E('dve', lambda b4=b4, fb=fb: nc.vector.tensor_scalar(out=cacc.t[:, 0:T], in0=uqk.t[:, b4, 0:T], scalar1=V(l, V_CW, fb), scalar2=V(l, V_CB, fb),
                                                                      op0=ALU.mult, op1=ALU.add), reads=(uqk, vecs), writes=(cacc,))
                for j in range(1, 4):
                    E('dve', lambda b4=b4, fb=fb, j=j: nc.vector.scalar_tensor_tensor(out=cacc.t[:, 0:T], in0=uqk.t[:, b4, j:j + T], scalar=V(l, V_CW, j * 32 + fb),
                                                                                      in1=cacc.t[:, 0:T], op0=ALU.mult, op1=ALU.add), reads=(uqk, vecs, cacc), writes=(cacc,))
                E('dve', lambda b4=b4, fb=fb: nc.vector.tensor_copy(out=hist.t[:, l, fb, :], in_=uqk.t[:, b4, T:T + 3]), reads=(uqk,), writes=(hist,))
                E('act', lambda b4=b4: nc.scalar.activation(out=qkT.t[:, b4, 0:T], in_=cacc.t[:, 0:T], func=AF.Silu), reads=(cacc,), writes=(qkT,))
            s = wnext(("w_in", l, 0, ((4096 + hd * 256, 256), (6144 + hd * 256, 256)), True))
            for c in range(nch):
                for kc in range(KC):
                    E('pe', lambda kc=kc, c=c: nc.tensor.matmul(PT.t[0:L, 0:512], lhsT=hT.t[:, kc, c * L:(c + 1) * L], rhs=s.t[:, kc, 0:512], start=(kc == 0), stop=False),
                      reads=(hT, s), writes=(PT,))
                E('pe', lambda: nc.tensor.matmul(PT.t[0:L, 0:512], lhsT=onesb.t[0:1, 0:L], rhs=s.t[0:1, KC, 0:512], start=False, stop=True), reads=(onesb, s), writes=(PT,))
                E('act', lambda c=c: nc.scalar.activation(out=vext.t[0:L, c, 0:256], in_=PT.t[0:L, 0:256], func=AF.Copy), reads=(PT,), writes=(vext,))
                E('act', lambda c=c: nc.scalar.activation(out=sigo.t[0:L, c, :], in_=PT.t[0:L, 256:512], func=AF.Sigmoid), reads=(PT,), writes=(sigo,))
            m0 = mst.t[:, l * 8 + hd:l * 8 + hd + 1]
            pcs = [PC, PT]

            def P1(c):
                par = c % 2
                STb, qsc, wrow, sm = STb2[par], qsc2[par], wrow2[par], sm2[par]
                cs_ = slice(c * L, (c + 1) * L)
                E('dve', lambda: nc.vector.tensor_scalar(out=igrep.t[0:L, :], in0=onesf.t[0:L, :], scalar1=igc.t[0:L, c, hd:hd + 1], scalar2=None, op0=ALU.mult),
                  reads=(onesf, igc), writes=(igrep,))
                E('dve', lambda: nc.vector.tensor_scalar(out=nlfrep.t[0:L, :], in0=onesf.t[0:L, :], scalar1=nlf.t[0:L, c, hd:hd + 1], scalar2=None, op0=ALU.mult),
                  reads=(onesf, nlf), writes=(nlfrep,))
                E('pe', lambda: nc.tensor.matmul(PA.t[:, 0:L], lhsT=igrep.t[0:L, :], rhs=ident[0:L, 0:L], start=True, stop=False), reads=(igrep, cst), writes=(PA,))
                E('pe', lambda: nc.tensor.matmul(PA.t[:, 0:L], lhsT=nlfrep.t[0:L, :], rhs=tri[0:L, 0:L], start=False, stop=True), reads=(nlfrep, cst), writes=(PA,))
                E('pe', lambda: nc.tensor.matmul(PA.t[:, 256:257], lhsT=nlfrep.t[0:L, :], rhs=onesf.t[0:L, 0:1], start=True, stop=True), reads=(nlfrep, onesf), writes=(PA,))
                E('dve', lambda: nc.vector.tensor_tensor_scan(out=grow.t[:, 0:L], data0=PA.t[:, 0:L], data1=negbig.t[:, 0:L], initial=m0, op0=ALU.max, op1=ALU.max),
                  reads=(PA, negbig, mst), writes=(grow,))
                E('dve', lambda: nc.vector.scalar_tensor_tensor(out=junk.t[0:L, 0:L], in0=grow.t[0:L, 0:L], scalar=1.0, in1=ident[0:L, 0:L], op0=ALU.mult, op1=ALU.mult,
                                                                accum_out=sm.t[0:L, 0:1]), reads=(grow, cst), writes=(junk, sm))
                E('dve', lambda: nc.vector.tensor_tensor(out=dtmp.t[0:L, 0:L], in0=maskb[0:L, 0:L], in1=grow.t[0:L, 0:L], op=ALU.subtract), reads=(cst, grow), writes=(dtmp,))
                E('act', lambda: nc.scalar.activation(out=DT.t[0:L, 0:L], in_=dtmp.t[0:L, 0:L], func=AF.Exp, bias=acadj.t[0:L, c, hd:hd + 1]), reads=(dtmp, acadj), writes=(DT,))
                for dc in range(2):
                    E('pe', lambda dc=dc: nc.tensor.matmul(PS.t[0:L, 0:L], lhsT=qkT.t[:, 2 + dc, cs_], rhs=qkT.t[:, dc, cs_], start=(dc == 0), stop=(dc == 1)), reads=(qkT,), writes=(PS,))
                E('dve', lambda: nc.vector.tensor_tensor(out=STb.t[0:L, 0:L], in0=PS.t[0:L, 0:L], in1=DT.t[0:L, 0:L], op=ALU.mult), reads=(PS, DT), writes=(STb,))
                E('act', lambda: nc.scalar.activation(out=wrow.t[:, 0:L], in_=grow.t[:, 0:L], func=AF.Exp, scale=-1.0, bias=m0), reads=(grow, mst), writes=(wrow,))
                for dc in range(2):
                    E('dve', lambda dc=dc: nc.vector.tensor_tensor(out=qsc.t[:, dc, 0:L], in0=qkT.t[:, dc, cs_], in1=wrow.t[:, 0:L], op=ALU.mult), reads=(qkT, wrow), writes=(qsc,))
                E('dve', lambda: nc.vector.tensor_tensor(out=sm.t[0:L, 1:2], in0=nbc.t[0:L, c, hd:hd + 1], in1=sm.t[0:L, 0:1], op=ALU.subtract), reads=(nbc, sm), writes=(sm,))
                E('act', lambda: nc.scalar.activation(out=sm.t[0:L, 2:3], in_=sm.t[0:L, 1:2], func=AF.Exp), reads=(sm,), writes=(sm,))
                E('dve', lambda: nc.vector.tensor_scalar(out=sm.t[:, 8:9], in0=grow.t[:, L - 1:L], scalar1=-1.0, scalar2=None, op0=ALU.mult), reads=(grow,), writes=(sm,))
                E('act', lambda: nc.scalar.activation(out=sm.t[0:L, 9:10], in_=acadj.t[0:L, c, hd:hd + 1], func=AF.Exp, bias=sm.t[0:L, 8:9]), reads=(acadj, sm), writes=(sm,))
                E('dve', lambda: nc.vector.tensor_tensor(out=mst.t[:, l * 8 + hd:l * 8 + hd + 1], in0=grow.t[:, L - 1:L], in1=PA.t[:, 256:257], op=ALU.subtract), reads=(grow, PA), writes=(mst,))

            def P2(c):
                par = c % 2
                sm = sm2[par]
                cs_ = slice(c * L, (c + 1) * L)
                for dc in range(2):
                    E('pe', lambda dc=dc: nc.tensor.transpose(out=PX.t[0:L, 256 + dc * 128:256 + (dc + 1) * 128], in_=qkT.t[:, 2 + dc, cs_], identity=identb.t[:, :]), reads=(qkT, identb), writes=(PX,))
                E('act', lambda: nc.scalar.activation(out=ktm.t[0:L, :], in_=PX.t[0:L, 256:512], func=AF.Identity, scale=sm.t[0:L, 9:10]), reads=(PX, sm), writes=(ktm,))
                for dc in range(2):
                    E('pe', lambda dc=dc: nc.tensor.matmul(pcs[dc].t[:, 0:257], lhsT=ktm.t[0:L, dc * 128:(dc + 1) * 128], rhs=vext.t[0:L, c, :], start=True, stop=True), reads=(ktm, vext), writes=(pcs[dc],))

            def TA(c):
                par = c % 2
                STb, qsc, wrow, sm = STb2[par], qsc2[par], wrow2[par], sm2[par]
                cs_ = slice(c * L, (c + 1) * L)
                E('pe', lambda: nc.tensor.matmul(PN.t[0:L, 0:257], lhsT=STb.t[0:L, 0:L], rhs=vext.t[0:L, c, :], start=True, stop=False), reads=(STb, vext), writes=(PN,))
                for dc in range(2):
                    E('pe', lambda dc=dc: nc.tensor.matmul(PN.t[0:L, 0:257], lhsT=qsc.t[:, dc, 0:L], rhs=Cb.t[:, dc, :], start=False, stop=(dc == 1)), reads=(qsc, Cb), writes=(PN,))
                E('act', lambda: nc.scalar.activation(out=sm.t[0:L, 3:4], in_=PN.t[0:L, 256:257], func=AF.Abs), reads=(PN,), writes=(sm,))
                E('dve', lambda: nc.vector.tensor_tensor(out=sm.t[0:L, 3:4], in0=sm.t[0:L, 3:4], in1=sm.t[0:L, 2:3], op=ALU.max), reads=(sm,), writes=(sm,))
                E('act', lambda: nc.scalar.activation(out=junk2.t[0:L, 0:256], in_=PN.t[0:L, 0:256], func=AF.Square, accum_out=sm.t[0:L, 4:5]), reads=(PN,), writes=(junk2, sm))
                E('dve', lambda: nc.vector.scalar_tensor_tensor(out=sm.t[0:L, 5:6], in0=sm.t[0:L, 3:4], scalar=EPS, in1=sm.t[0:L, 3:4], op0=ALU.mult, op1=ALU.mult), reads=(sm,), writes=(sm,))
                E('dve', lambda: nc.vector.scalar_tensor_tensor(out=sm.t[0:L, 6:7], in0=sm.t[0:L, 4:5], scalar=1.0 / 256, in1=sm.t[0:L, 5:6], op0=ALU.mult, op1=ALU.add), reads=(sm,), writes=(sm,))
                E('act', lambda: nc.scalar.activation(out=sm.t[0:L, 6:7], in_=sm.t[0:L, 6:7], func=AF.Ln), reads=(sm,), writes=(sm,))
                E('act', lambda: nc.scalar.activation(out=sm.t[0:L, 7:8], in_=sm.t[0:L, 6:7], func=AF.Exp, scale=-0.5), reads=(sm,), writes=(sm,))
                E('dve', lambda: nc.vector.scalar_tensor_tensor(out=ytm.t[0:L, :], in0=PN.t[0:L, 0:256], scalar=sm.t[0:L, 7:8], in1=sigo.t[0:L, c, :], op0=ALU.mult, op1=ALU.mult),
                  reads=(PN, sm, sigo), writes=(ytm,))
                for vc in range(2):
                    E('pe', lambda vc=vc: nc.tensor.transpose(out=PX.t[:, vc * 128:vc * 128 + L], in_=ytm.t[0:L, vc * 128:(vc + 1) * 128], identity=identb.t[0:L, 0:L]), reads=(ytm, identb), writes=(PX,))
                    E('act', lambda vc=vc: nc.scalar.activation(out=yaT.t[:, hd * 2 + vc, cs_], in_=PX.t[:, vc * 128:vc * 128 + L], func=AF.Identity, scale=V(l, V_MAN, hd * 2 + vc)),
                      reads=(PX, vecs), writes=(yaT,))

            def TB(c):
                wrow = wrow2[c % 2]
                for dc in range(2):
                    E('dve', lambda dc=dc: nc.vector.scalar_tensor_tensor(out=Cf.t[:, dc, :], in0=Cf.t[:, dc, :], scalar=wrow.t[:, L - 1:L], in1=pcs[dc].t[:, 0:257], op0=ALU.mult, op1=ALU.add),
                      reads=(Cf, wrow, pcs[dc]), writes=(Cf,))
                E('act', lambda: nc.scalar.activation(out=Cb.t[:], in_=Cf.t[:], func=AF.Copy), reads=(Cf,), writes=(Cb,))

            P1(0); P2(0)
            for c in range(nch):
                if c + 1 < nch:
                    P1(c + 1)
                TA(c)
                TB(c)
                if c + 1 < nch:
                    P2(c + 1)
            spdma(O[g]["C"][l, hd].rearrange("(dc p) v -> p dc v", p=128), Cf.t[:, :, 0:256], reads=(Cf,))
            spdma(O[g]["n"][l, hd], Cf.t[:, :, 256], reads=(Cf,))

        def hgrn_group(l, hp, T, Lh, nch, g, first, sq):
            s = wnext(("w_in", l, 0, ((8192 + hp * 256, 256), (10240 + hp * 256, 256)), False))
            for hh in range(2):
                ps = nextpd(); fm_group(ps, s, hh, hT, T)
                E('act', lambda hh=hh, ps=ps: nc.scalar.activation(out=uqk.t[:, hh, 0:T], in_=ps.t[:, 0:T], func=AF.Silu, bias=V(l, V_BIN, 64 + hp * 2 + hh)), reads=(ps, vecs), writes=(uqk,))
            for hh in range(2):
                ps = nextpd(); fm_group(ps, s, 2 + hh, hT, T)
                E('act', lambda hh=hh, ps=ps: nc.scalar.activation(out=uqk.t[:, 2 + hh, 0:T], in_=ps.t[:, 0:T], func=AF.Sigmoid, bias=V(l, V_BIN, 80 + hp * 2 + hh)), reads=(ps, vecs), writes=(uqk,))
            s = wnext(("w_in", l, 0, ((12288 + hp * 256, 256), (14336 + hp * 256, 256)), True))
            for hh in range(2):
                ps = nextpd(); fm_group(ps, s, hh, hT, T)
                E('act', lambda hh=hh, ps=ps: nc.scalar.activation(out=sqb.t[:, 0:T], in_=ps.t[:, 0:T], func=AF.Identity, bias=V(l, V_BIN, 96 + hp * 2 + hh)), reads=(ps, vecs), writes=(sqb,))
                for c in range(nch):
                    E('pe', lambda c=c: nc.tensor.transpose(out=PX.t[0:Lh, (c % 8) * 128:(c % 8 + 1) * 128], in_=sqb.t[:, c * Lh:(c + 1) * Lh], identity=identb.t[:, :]), reads=(sqb, identb), writes=(PX,))
                    if c % 8 == 7 or c == nch - 1:
                        c0 = (c // 8) * 8
                        nn = c - c0 + 1
                        E('act', lambda c0=c0, nn=nn, hh=hh: nc.scalar.activation(out=vtm.t[0:Lh, c0:c0 + nn, hh * 128:(hh + 1) * 128],
                                                                                 in_=PX.t[0:Lh, 0:nn * 128].rearrange("p (c v) -> p c v", v=128), func=AF.Copy), reads=(PX,), writes=(vtm,))
            for hh in range(2):
                ps = nextpd(); fm_group(ps, s, 2 + hh, hT, T)
                E('act', lambda hh=hh, ps=ps: nc.scalar.activation(out=qkT.t[:, hh, 0:T], in_=ps.t[:, 0:T], func=AF.Sigmoid, bias=V(l, V_BIN, 112 + hp * 2 + hh)), reads=(ps, vecs), writes=(qkT,))
            qth = qkT.t[:, 2, :]; kth = qkT.t[:, 3, :]
            lfh = cacc; kkh = tmpf
            for hh in range(2):
                hd = hp * 2 + hh
                if first and g == "p":
                    E('dve', lambda: nc.vector.memset(Sf.t[:], 0.0), writes=(Sf,))
                else:
                    spdma(Sf.t[:], (Ss_in if first else O[g]["S"])[l, hd], writes=(Sf,))
                E('act', lambda: nc.scalar.activation(out=Sb.t[:], in_=Sf.t[:], func=AF.Copy), reads=(Sf,), writes=(Sb,))
                E('act', lambda hh=hh, hd=hd: nc.scalar.activation(out=lfh.t[:, 0:T], in_=uqk.t[:, 2 + hh, 0:T], func=AF.Ln, scale=omlt.t[:, l, hd:hd + 1], bias=lbt.t[:, l, hd:hd + 1]),
                  reads=(uqk, omlt, lbt), writes=(lfh,))
                E('dve', lambda hh=hh, hd=hd: nc.vector.tensor_scalar(out=kkh.t[:, 0:T], in0=uqk.t[:, 2 + hh, 0:T], scalar1=nomlt.t[:, l, hd:hd + 1], scalar2=omlt.t[:, l, hd:hd + 1], op0=ALU.mult, op1=ALU.add),
                  reads=(uqk, nomlt, omlt), writes=(kkh,))
                E('dve', lambda: nc.vector.tensor_tensor_scan(out=bh.t[:, 0:T], data0=rmask.t[:, 0:T], data1=lfh.t[:, 0:T], initial=0.0, op0=ALU.mult, op1=ALU.add), reads=(rmask, lfh), writes=(bh,))
                E('act', lambda: nc.scalar.activation(out=ebh.t[:, 0:T], in_=bh.t[:, 0:T], func=AF.Exp), reads=(bh,), writes=(ebh,))
                E('act', lambda: nc.scalar.activation(out=enbh.t[:, 0:T], in_=bh.t[:, 0:T], func=AF.Exp, scale=-1.0), reads=(bh,), writes=(enbh,))
                E('dve', lambda hh=hh: nc.vector.tensor_tensor(out=qth[:, 0:T], in0=uqk.t[:, hh, 0:T], in1=ebh.t[:, 0:T], op=ALU.mult), reads=(uqk, ebh), writes=(qkT,))
                E('dve', lambda: nc.vector.tensor_tensor(out=kth[:, 0:T], in0=kkh.t[:, 0:T], in1=enbh.t[:, 0:T], op=ALU.mult), reads=(kkh, enbh), writes=(qkT,))
                for c in range(nch):
                    cs_ = slice(c * Lh, (c + 1) * Lh)
                    E('pe', lambda cs_=cs_: nc.tensor.matmul(PS.t[0:Lh, cs_], lhsT=kth[:, cs_], rhs=qth[:, cs_], start=True, stop=True), reads=(qkT,), writes=(PS,))
                E('dve', lambda: nc.vector.tensor_tensor(out=Abf.t[0:Lh, 0:T], in0=PS.t[0:Lh, 0:T], in1=tri8[0:Lh, 0:T], op=ALU.mult), reads=(PS, cst), writes=(Abf,))
                E('dve', lambda: nc.vector.tensor_tensor(out=khT.t[:, 0:T].rearrange("p (c l) -> p c l", l=Lh), in0=kth[:, 0:T].rearrange("p (c l) -> p c l", l=Lh),
                                                         in1=ebh.t[:, 0:T].rearrange("p (c l) -> p c l", l=Lh)[:, :, Lh - 1:Lh].broadcast_to([128, nch, Lh]), op=ALU.mult),
                  reads=(qkT, ebh), writes=(khT,))
                for c in range(nch):
                    cs_ = slice(c * Lh, (c + 1) * Lh)
                    ce = (c + 1) * Lh - 1
                    E('pe', lambda cs_=cs_, c=c: nc.tensor.transpose(out=PX.t[0:Lh, (c % 8) * 128:(c % 8 + 1) * 128], in_=khT.t[:, cs_], identity=identb.t[:, :]), reads=(khT, identb), writes=(PX,))
                    if c % 8 == 7 or c == nch - 1:
                        c0 = (c // 8) * 8
                        nn = c - c0 + 1
                        E('act', lambda c0=c0, nn=nn: nc.scalar.activation(out=khtm.t[0:Lh, c0 * 128:(c0 + nn) * 128], in_=PX.t[0:Lh, 0:nn * 128], func=AF.Copy), reads=(PX,), writes=(khtm,))
                ubanks = [P0, P1, PC, PT]
                for c in range(nch):
                    vsl = vtm.t[0:Lh, c, hh * 128:(hh + 1) * 128]
                    ub = ubanks[c // 4]
                    E('pe', lambda c=c, vsl=vsl, ub=ub: nc.tensor.matmul(ub.t[:, (c % 4) * 128:(c % 4 + 1) * 128], lhsT=khtm.t[0:Lh, c * 128:(c + 1) * 128], rhs=vsl, start=True, stop=True), reads=(khtm, vtm), writes=(ub,))
                for c in range(nch):
                    ce = (c + 1) * Lh - 1
                    ub = ubanks[c // 4]
                    E('dve', lambda c=c, ce=ce, ub=ub: nc.vector.scalar_tensor_tensor(out=Sf.t[:], in0=Sf.t[:], scalar=ebh.t[:, ce:ce + 1], in1=ub.t[:, (c % 4) * 128:(c % 4 + 1) * 128], op0=ALU.mult, op1=ALU.add), reads=(Sf, ebh, ub), writes=(Sf,))
                    if c < nch - 1:
                        E('act', lambda c=c: nc.scalar.activation(out=Sbc[c + 1].t, in_=Sf.t[:], func=AF.Copy), reads=(Sf,), writes=(Sbc[c + 1],))
                for c in range(nch):
                    cs_ = slice(c * Lh, (c + 1) * Lh)
                    vsl = vtm.t[0:Lh, c, hh * 128:(hh + 1) * 128]
                    sbc = Sb if c == 0 else Sbc[c]
                    sbap = Sb.t[:, :] if c == 0 else Sbc[c].t
                    E('pe', lambda cs_=cs_, vsl=vsl: nc.tensor.matmul(PN.t[:, cs_], lhsT=vsl, rhs=Abf.t[0:Lh, cs_], start=True, stop=False), reads=(vtm, Abf), writes=(PN,))
                    E('pe', lambda cs_=cs_, sbap=sbap: nc.tensor.matmul(PN.t[:, cs_], lhsT=sbap, rhs=qth[:, cs_], start=False, stop=True), reads=(sbc, qkT), writes=(PN,))
                spdma(O[g]["S"][l, hd], Sf.t[:], reads=(Sf,))
                E('act', lambda: nc.scalar.activation(out=sqb.t[:, 0:T], in_=PN.t[:, 0:T], func=AF.Square), reads=(PN,), writes=(sqb,))
                E('pe', lambda: nc.tensor.matmul(PA.t[:, 0:T], lhsT=onesb.t[:, :], rhs=sqb.t[:, 0:T], start=True, stop=True), reads=(onesb, sqb), writes=(PA,))
                E('act', lambda: nc.scalar.activation(out=rstd.t[:, 0:T], in_=PA.t[:, 0:T], func=AF.Ln, scale=1.0 / 128, bias=epsc), reads=(PA, smc), writes=(rstd,))
                E('act', lambda: nc.scalar.activation(out=rstd.t[:, 0:T], in_=rstd.t[:, 0:T], func=AF.Exp, scale=-0.5), reads=(rstd,), writes=(rstd,))
                E('dve', lambda hd=hd: nc.vector.scalar_tensor_tensor(out=tmpf.t[:, 0:T], in0=PN.t[:, 0:T], scalar=V(l, V_HBN, hd), in1=rstd.t[:, 0:T], op0=ALU.mult, op1=ALU.mult), reads=(PN, vecs, rstd), writes=(tmpf,))
                E('dve', lambda hd=hd, hh=hh: nc.vector.tensor_tensor(out=ybT.t[:, hd, 0:T], in0=tmpf.t[:, 0:T], in1=qkT.t[:, hh, 0:T], op=ALU.mult), reads=(tmpf, qkT), writes=(ybT,))

        def layer(l, T, g, first, sq):
            Lm = min(128, T); Lh = min(32, T)
            modnorm(l, 0, sq, T, 0)
            gs = wnext(("w_in", l, 0, ((20480, 16),), True))
            mlstm_gates(l, T, Lm, T // Lm, gs)
            for hd in range(8):
                mlstm_head(l, hd, T, Lm, T // Lm, g, first, sq)
            for hp in range(8):
                hgrn_group(l, hp, T, Lh, T // Lh, g, first, sq)
            for jb in range(4):
                s = wnext(("w_in", l, 0, ((16384 + jb * 512, 512),), False))
                for jj in range(4):
                    ps = nextpd(); fm_group(ps, s, jj, hT, T)
                    E('act', lambda jj=jj, ps=ps: nc.scalar.activation(out=sg4.t[:, jj, 0:T], in_=ps.t[:, 0:T], func=AF.Sigmoid, bias=V(l, V_BIN, 128 + jb * 4 + jj)), reads=(ps, vecs), writes=(sg4,))
                s = wnext(("w_br_a", l, 0, ((jb * 512, 512),), False))
                for jj in range(4):
                    ps = nextpd(); fm_group(ps, s, jj, yaT, T)
                    E('dve', lambda jj=jj, ps=ps: nc.vector.tensor_tensor(out=uqk.t[:, jj, 0:T], in0=ps.t[:, 0:T], in1=sg4.t[:, jj, 0:T], op=ALU.mult), reads=(ps, sg4), writes=(uqk,))
                s = wnext(("w_in", l, 0, ((18432 + jb * 512, 512),), False))
                for jj in range(4):
                    ps = nextpd(); fm_group(ps, s, jj, hT, T)
                    E('act', lambda jj=jj, ps=ps: nc.scalar.activation(out=sg4.t[:, jj, 0:T], in_=ps.t[:, 0:T], func=AF.Sigmoid, bias=V(l, V_BIN, 144 + jb * 4 + jj)), reads=(ps, vecs), writes=(sg4,))
                s = wnext(("w_br_b", l, 0, ((jb * 512, 512),), False))
                for jj in range(4):
                    ps = nextpd(); fm_group(ps, s, jj, ybT, T)
                    E('dve', lambda jj=jj, ps=ps: nc.vector.tensor_tensor(out=tmpf.t[:, 0:T], in0=ps.t[:, 0:T], in1=sg4.t[:, jj, 0:T], op=ALU.mult), reads=(ps, sg4), writes=(tmpf,))
                    E('dve', lambda jj=jj: nc.vector.tensor_tensor(out=uT.t[:, jb * 4 + jj, 0:T], in0=tmpf.t[:, 0:T], in1=uqk.t[:, jj, 0:T], op=ALU.add), reads=(tmpf, uqk), writes=(uT,))
            for jb in range(4):
                s = wnext(("w_o", l, 0, ((jb * 512, 512),), False))
                for jj in range(4):
                    j = jb * 4 + jj
                    ps = nextpd(); fm_group(ps, s, jj, uT, T)
                    E('dve', lambda j=j, ps=ps: nc.vector.scalar_tensor_tensor(out=xT.t[:, j, 0:T], in0=ps.t[:, 0:T], scalar=mods.t[:, l, 32 + j, sq:sq + 1], in1=xT.t[:, j, 0:T], op0=ALU.mult, op1=ALU.add),
                      reads=(ps, mods, xT), writes=(xT,))
            modnorm(l, 1, sq, T, 48)
            for qd in range(4):
                for ub in range(4):
                    s = wnext(("w_up", l, 0, ((qd * 2048 + ub * 512, 512),), False))
                    for jj in range(4):
                        ps = nextpd(); fm_group(ps, s, jj, hT, T)
                        E('act', lambda ps=ps: nc.scalar.activation(out=sqb.t[:, 0:T], in_=ps.t[:, 0:T], func=AF.Relu), reads=(ps,), writes=(sqb,))
                        E('dve', lambda ub=ub, jj=jj: nc.vector.tensor_tensor(out=uT.t[:, ub * 4 + jj, 0:T], in0=sqb.t[:, 0:T], in1=sqb.t[:, 0:T], op=ALU.mult), reads=(sqb,), writes=(uT,))
                for ob in range(4):
                    s = wnext(("w_down", l, qd * 2048, ((ob * 512, 512),), False))
                    for jj in range(4):
                        j = ob * 4 + jj
                        ps = nextpd(); fm_group(ps, s, jj, uT, T)
                        E('dve', lambda j=j, ps=ps: nc.vector.scalar_tensor_tensor(out=xT.t[:, j, 0:T], in0=ps.t[:, 0:T], scalar=mods.t[:, l, 80 + j, sq:sq + 1], in1=xT.t[:, j, 0:T], op0=ALU.mult, op1=ALU.add),
                          reads=(ps, mods, xT), writes=(xT,))

        def final_norm(T):
            for kc in range(KC):
                E('act', lambda kc=kc: nc.scalar.activation(out=sqb.t[:, 0:T], in_=xT.t[:, kc, 0:T], func=AF.Square), reads=(xT,), writes=(sqb,))
                E('pe', lambda kc=kc: nc.tensor.matmul(PA.t[:, 0:T], lhsT=onesb.t[:, :], rhs=sqb.t[:, 0:T], start=(kc == 0), stop=(kc == KC - 1)), reads=(onesb, sqb), writes=(PA,))
            E('act', lambda: nc.scalar.activation(out=rstd.t[:, 0:T], in_=PA.t[:, 0:T], func=AF.Ln, scale=1.0 / D, bias=epsc), reads=(PA, smc), writes=(rstd,))
            E('act', lambda: nc.scalar.activation(out=rstd.t[:, 0:T], in_=rstd.t[:, 0:T], func=AF.Exp, scale=-0.5), reads=(rstd,), writes=(rstd,))
            fg0 = NL * V_PER
            for kc in range(KC):
                E('dve', lambda kc=kc: nc.vector.scalar_tensor_tensor(out=xT.t[:, kc, 0:T], in0=xT.t[:, kc, 0:T], scalar=vecs.t[:, fg0 + kc:fg0 + kc + 1], in1=rstd.t[:, 0:T], op0=ALU.mult, op1=ALU.mult),
                  reads=(xT, vecs, rstd), writes=(xT,))

        groups = []
        if NTP > 0:
            groups.append(("p", NTP, TP, 0))
        if WITH_S:
            groups.append(("s", 1, 16, 1))
        for (g, ntiles, T, sq) in groups:
            if g == "p":
                E('dve', lambda: nc.vector.memset(hist.t[:], 0.0), writes=(hist,))
                E('dve', lambda: nc.vector.memset(mst.t[:], 0.0), writes=(mst,))
            else:
                spdma(hist.t[:], convs_in, writes=(hist,))
                spdma(mst.t[:], msamp, writes=(mst,))
            for ti in range(ntiles):
                src = (xp[:, ti * T:(ti + 1) * T] if g == "p" else xs).rearrange("(kc p) t -> p kc t", p=128)
                spdma(xT.t[:, :, 0:T], src, writes=(xT,))
                for l in range(NL):
                    layer(l, T, g, ti == 0, sq)
                final_norm(T)
                dst = (yp[:, ti * T:(ti + 1) * T] if g == "p" else ys).rearrange("(kc p) t -> p kc t", p=128)
                spdma(dst, xT.t[:, :, 0:T], reads=(xT,))
            spdma(O[g]["conv"], hist.t[:], reads=(hist,))
            spdma(O[g]["m"], mst.t[0:1, :], reads=(mst,))
        for i in range(SPN):
            k.wait('sp', sp_last[i])
        assert wstate['used'] == len(PL), (wstate, len(PL))
    return nc


def _fm(v):
    return np.ascontiguousarray(v.reshape(-1, 128).T)


def _consts():
    s = np.arange(128)
    tri = (s[:, None] <= s[None, :]).astype(np.float32)
    ident = np.eye(128, dtype=np.float32)
    maskb = np.where(s[:, None] <= s[None, :], 0.0, -1e4).astype(np.float32)
    t32 = np.zeros((128, 32), np.float32)
    t32[:32] = tri[:32, :32]
    return np.ascontiguousarray(np.concatenate([tri, ident, maskb, np.tile(t32, (1, 16))], axis=1))


def make_inputs(NL, NTP, TP, inp, core):
    bp = core % 4
    f = lambda a: np.ascontiguousarray(np.asarray(a, dtype=np.float32))
    SEQP = max(NTP, 1) * TP
    m = {}
    m["xp"] = f(inp["x_prompt"][bp, :SEQP].T)
    m["xs"] = f(inp["x_sample"][core].T)
    cv = np.stack([_fm(inp["c_prompt"][bp]), _fm(inp["c_sample"][core])], axis=-1)
    m["cvec"] = f(cv)
    cols = []
    for l in range(NL):
        cols += [_fm(inp["norm1_g"][l]), _fm(inp["norm2_g"][l]), _fm(inp["ada_b"][l]), _fm(inp["b_in"][l, :20480])]
        cols += [_fm(inp["conv_w"][l, j]) for j in range(4)]
        cols += [_fm(inp["conv_b"][l]), _fm(inp["ma_norm"][l]), _fm(inp["hb_norm"][l]), _fm(inp["hgrn_lb_raw"][l])]
    cols.append(_fm(inp["final_g"]))
    m["vecs"] = f(np.concatenate(cols, axis=1))
    m["consts"] = _consts()
    m["msamp"] = f(np.broadcast_to(inp["state_mlstm_m"][:NL, core].reshape(1, NL * 8), (128, NL * 8)))
    cc = inp["cache_conv"][:NL, core]
    m["convs_in"] = f(cc.reshape(NL, 3, 32, 128).transpose(3, 0, 2, 1))
    m["ns_in"] = f(inp["state_mlstm_n"][:NL, core].reshape(NL, 8, 2, 128).transpose(0, 1, 3, 2))
    m["Cs_in"] = f(inp["state_mlstm_C"][:NL, core])
    m["Ss_in"] = f(inp["state_hgrn"][:NL, core])
    for n in ("w_in", "ada_w", "w_br_a", "w_br_b", "w_o", "w_up", "w_down", "b_in"):
        m[n] = f(inp[n][:NL])
    return m


def assemble(NL, NTP, TP, results, WITH_S=True):
    SEQP = NTP * TP
    outs = {}
    yp = np.stack([results[b]["yp"].T for b in range(4)], axis=0)
    ys = np.stack([results[c]["ys"].T for c in range(8)], axis=0)

    def grp(g, cores):
        conv = np.stack([results[c]["conv" + g].transpose(1, 3, 2, 0).reshape(NL, 3, 4096) for c in cores], axis=1)
        C = np.stack([results[c]["C" + g] for c in cores], axis=1)
        n = np.stack([results[c]["n" + g].transpose(0, 1, 3, 2).reshape(NL, 8, 256) for c in cores], axis=1)
        mm = np.stack([results[c]["m" + g].reshape(NL, 8) for c in cores], axis=1)
        S = np.stack([results[c]["S" + g] for c in cores], axis=1)
        return [np.ascontiguousarray(a, dtype=np.float32) for a in (conv, C, n, mm, S)]

    return tuple([np.ascontiguousarray(yp, dtype=np.float32), np.ascontiguousarray(ys, dtype=np.float32)] + grp("p", range(4)) + grp("s", range(8)))


def kernel(**inputs):
    NL, NTP, TP = 4, 8, 512
    inp = {k_: np.asarray(v) for k_, v in inputs.items()}
    nc = build(NL, NTP, True, TP)
    in_maps = [make_inputs(NL, NTP, TP, inp, c) for c in range(8)]
    res = run_bass_kernel_spmd(nc, in_maps, core_ids=list(range(8)))
    return assemble(NL, NTP, TP, res.results)
```
